# Optimizing a Trainium2 kernel written in Bass

```python
import jax, jax.numpy as jnp
from jax import lax
import numpy as np

D_MODEL = 1024
BATCH = 32
SEQ = 256
DEPTH = 1
DEC_BATCH = 8
DEC_SEQ = 4096
PAST_LEN = 256

GRID_W = 64
H_A = 4
DK_A = 128
DV_A = 128
CONV_W = 5
CHUNK = 64
H_B = 8
KV_B = 2
GROUP_B = H_B // KV_B
HD_B = 64
WINDOW = 128
BLOCK_B = 128
ROPE_BASE = 10000.0
ROPE_AXIS_PAIRS = HD_B // 4
D_FF = 2816
HALF_STEP = 0.5
N_MOD = 9
EPS = 1e-6
QKV_A = 2 * H_A * DK_A + H_A * DV_A
Z_A = H_A * DV_A
DEC_A = 2 * H_A
BETA_A = 2 * H_A
Q_B = H_B * HD_B
K_B = KV_B * HD_B
V_B = KV_B * HD_B
GATES = 2 * D_MODEL
SPLIT_SIZES = (QKV_A, Z_A, DEC_A, BETA_A, Q_B, K_B, V_B, GATES)
IN_WIDTH = QKV_A + Z_A + DEC_A + BETA_A + Q_B + K_B + V_B + GATES

kernel_name = 'hybrid_deltanet_swa_prefix_dit_step'


def rms_norm(x, w):
    xf = x.astype(jnp.float32)
    y = xf * lax.rsqrt(jnp.mean(xf * xf, axis=-1, keepdims=True) + EPS)
    return (y * w.astype(jnp.float32)).astype(x.dtype)


def l2norm(x):
    return x * lax.rsqrt(jnp.sum(x * x, axis=-1, keepdims=True) + EPS)


def modulate(x, shift, scale):
    return x * (1 + scale) + shift


def swiglu(h, w13, w2):
    a, b = jnp.split(h @ w13, 2, axis=-1)
    return (jax.nn.silu(a) * b) @ w2


def ffn_sublayer(x, norm_w, shift, scale, gate, w13, w2):
    return x + HALF_STEP * gate * swiglu(modulate(rms_norm(x, norm_w), shift, scale), w13, w2)


def split_mixer_inputs(p):
    points = [int(v) for v in np.cumsum(SPLIT_SIZES)[:-1]]
    return jnp.split(p, points, axis=-1)


def short_conv(x, w):
    C = x.shape[-1]
    y = lax.conv_general_dilated(x, w[:, None, :].astype(x.dtype), window_strides=(1,),
                                 padding=((CONV_W // 2, CONV_W // 2),),
                                 dimension_numbers=('NWC', 'WIO', 'NWC'), feature_group_count=C)
    return jax.nn.silu(y)


def chunk_gated_delta(q, k, v, g, beta, s0):
    B_, T, H, _ = q.shape
    DV = v.shape[-1]
    n = T // CHUNK

    def blocks(t):
        t = t.reshape((B_, n, CHUNK, H) + t.shape[3:])
        return jnp.moveaxis(t, (1, 3), (0, 2))

    qc, kc, vc, bc = blocks(q), blocks(k), blocks(v), blocks(beta)
    gc = jnp.cumsum(blocks(g), axis=-1)
    idx = jnp.arange(CHUNK)
    causal = idx[:, None] >= idx[None, :]
    strict = idx[:, None] > idx[None, :]
    decay = jnp.exp(jnp.where(causal, gc[..., :, None] - gc[..., None, :], -jnp.inf))
    kb = kc * bc[..., None]
    vb = vc * bc[..., None]
    lower = jnp.where(strict, jnp.einsum('nbhid,nbhjd->nbhij', kb, kc) * decay, 0.0)
    eye = jnp.eye(CHUNK, dtype=jnp.float32)
    tmat = lax.linalg.triangular_solve(eye + lower, jnp.broadcast_to(eye, lower.shape),
                                       left_side=True, lower=True, unit_diagonal=True)
    u = tmat @ vb
    w = tmat @ (kb * jnp.exp(gc)[..., None])
    a_intra = jnp.einsum('nbhid,nbhjd->nbhij', qc, kc) * decay

    def step(S, xs):
        q_i, k_i, u_i, w_i, a_i, g_i = xs
        v_new = u_i - w_i @ S
        o_i = (q_i * jnp.exp(g_i)[..., None]) @ S + a_i @ v_new
        g_last = g_i[..., -1:]
        S = S * jnp.exp(g_last)[..., None] + jnp.einsum(
            'bhck,bhcv->bhkv', k_i * jnp.exp(g_last - g_i)[..., None], v_new)
        return S, o_i

    s_final, o = lax.scan(step, s0.astype(jnp.float32), (qc, kc, u, w, a_intra, gc))
    o = jnp.moveaxis(o, (0, 2), (1, 3)).reshape(B_, T, H, DV)
    return o, s_final


def mixer_a(qkv_raw, z_raw, dec_raw, beta_raw, conv_w, a_log, dt_bias, onorm, s0_f, s0_b):
    B_, T = qkv_raw.shape[:2]
    qkv = short_conv(qkv_raw, conv_w).astype(jnp.float32)
    q, k, v = jnp.split(qkv, [H_A * DK_A, 2 * H_A * DK_A], axis=-1)
    q = l2norm(q.reshape(B_, T, H_A, DK_A)) * DK_A ** -0.5
    k = l2norm(k.reshape(B_, T, H_A, DK_A))
    v = v.reshape(B_, T, H_A, DV_A)
    g = -jnp.exp(a_log.astype(jnp.float32)) * jax.nn.softplus(
        dec_raw.astype(jnp.float32).reshape(B_, T, 2, H_A) + dt_bias.astype(jnp.float32))
    beta = jax.nn.sigmoid(beta_raw.astype(jnp.float32).reshape(B_, T, 2, H_A))
    o_f, s_f = chunk_gated_delta(q, k, v, g[:, :, 0], beta[:, :, 0], s0_f)
    rev = lambda t: jnp.flip(t, axis=1)
    o_b, s_b = chunk_gated_delta(rev(q), rev(k), rev(v), rev(g[:, :, 1]), rev(beta[:, :, 1]), s0_b)
    o = o_f + rev(o_b)
    y = rms_norm(o, onorm) * jax.nn.silu(z_raw.astype(jnp.float32).reshape(B_, T, H_A, DV_A))
    return y.reshape(B_, T, H_A * DV_A).astype(qkv_raw.dtype), s_f, s_b


def axial_rope_tables(rows):
    row = jnp.repeat(jnp.arange(rows, dtype=jnp.float32), GRID_W)
    col = jnp.tile(jnp.arange(GRID_W, dtype=jnp.float32), rows)
    inv = jnp.power(ROPE_BASE, -jnp.arange(ROPE_AXIS_PAIRS, dtype=jnp.float32) / ROPE_AXIS_PAIRS)
    ang = jnp.concatenate([row[:, None] * inv, col[:, None] * inv], axis=-1)
    return jnp.cos(ang), jnp.sin(ang)


def apply_rope(x, cos, sin):
    xf = x.astype(jnp.float32)
    x1, x2 = jnp.split(xf, 2, axis=-1)
    c = cos[None, :, None, :]
    s = sin[None, :, None, :]
    return jnp.concatenate([x1 * c - x2 * s, x2 * c + x1 * s], axis=-1).astype(x.dtype)


def context_attention(q, k, v, sink):
    B_, S = q.shape[:2]
    qg = q.reshape(B_, S, KV_B, GROUP_B, HD_B)
    s = jnp.einsum('bqgrd,bkgd->bgrqk', qg, k).astype(jnp.float32) * HD_B ** -0.5
    s_snk = jnp.broadcast_to(sink.astype(jnp.float32).reshape(KV_B, GROUP_B, 1, 1), s.shape[:-1] + (1,))
    p = jax.nn.softmax(jnp.concatenate([s, s_snk], axis=-1), axis=-1)[..., :S].astype(v.dtype)
    o = jnp.einsum('bgrqk,bkgd->bqgrd', p, v)
    return o.reshape(B_, S, H_B * HD_B)


def window_attention(q, k, v, k_ctx, v_ctx, sink):
    B_, T = q.shape[:2]
    n = T // BLOCK_B
    scale = HD_B ** -0.5
    qb = jnp.moveaxis(q.reshape(B_, n, BLOCK_B, KV_B, GROUP_B, HD_B), 1, 0)

    def windows(t):
        tp = jnp.pad(t, ((0, 0), (BLOCK_B, BLOCK_B), (0, 0), (0, 0))).reshape(B_, n + 2, BLOCK_B, KV_B, HD_B)
        w = jnp.concatenate([tp[:, :-2], tp[:, 1:-1], tp[:, 2:]], axis=2)
        return jnp.moveaxis(w, 1, 0)

    kw, vw = windows(k), windows(v)
    blk = jnp.arange(n)[:, None]
    q_pos = blk * BLOCK_B + jnp.arange(BLOCK_B)[None, :]
    k_pos = (blk - 1) * BLOCK_B + jnp.arange(3 * BLOCK_B)[None, :]
    valid = ((jnp.abs(q_pos[:, :, None] - k_pos[:, None, :]) <= WINDOW)
             & (k_pos[:, None, :] >= 0) & (k_pos[:, None, :] < T))
    sink_logit = sink.astype(jnp.float32).reshape(KV_B, GROUP_B, 1, 1)
    neg = jnp.finfo(jnp.float32).min
    n_loc = 3 * BLOCK_B
    n_ctx = k_ctx.shape[1]

    def one_block(args):
        q_i, k_i, v_i, m_i = args
        s_loc = jnp.einsum('bqgrd,bkgd->bgrqk', q_i, k_i).astype(jnp.float32) * scale
        s_loc = jnp.where(m_i[None, None, None], s_loc, neg)
        s_ctx = jnp.einsum('bqgrd,bkgd->bgrqk', q_i, k_ctx).astype(jnp.float32) * scale
        s_snk = jnp.broadcast_to(sink_logit, s_loc.shape[:-1] + (1,))
        p = jax.nn.softmax(jnp.concatenate([s_loc, s_ctx, s_snk], axis=-1), axis=-1).astype(v_i.dtype)
        return (jnp.einsum('bgrqk,bkgd->bqgrd', p[..., :n_loc], v_i)
                + jnp.einsum('bgrqk,bkgd->bqgrd', p[..., n_loc:n_loc + n_ctx], v_ctx))

    o = lax.map(one_block, (qb, kw, vw, valid))
    return jnp.moveaxis(o, 0, 1).reshape(B_, T, H_B * HD_B)


def merge_branches(y_a, y_b, gate_raw, w_oa, w_ob, w_out):
    g_a, g_b = jnp.split(jax.nn.sigmoid(gate_raw), 2, axis=-1)
    return (g_a * (y_a @ w_oa) + g_b * (y_b @ w_ob)) @ w_out


def setup_inputs(seed: int = 0) -> dict:
    key = jax.random.key(seed)
    ks = jax.random.split(key, 32)
    nrm = lambda k, shape, s: jax.random.normal(k, shape, jnp.float32) * s
    dt = jnp.exp(jax.random.uniform(ks[17], (DEPTH, 2, H_A), jnp.float32, jnp.log(1e-3), jnp.log(1e-1)))
    return {
        'x_prompt': nrm(ks[0], (BATCH, SEQ, D_MODEL), 1.0),
        'x_sample': nrm(ks[1], (DEC_BATCH, DEC_SEQ, D_MODEL), 1.0),
        'state_delta_fwd': nrm(ks[2], (DEC_BATCH, DEPTH, H_A, DK_A, DV_A), 0.1),
        'state_delta_bwd': nrm(ks[3], (DEC_BATCH, DEPTH, H_A, DK_A, DV_A), 0.1),
        'cache_k': nrm(ks[4], (DEC_BATCH, DEPTH, PAST_LEN, KV_B, HD_B), 1.0),
        'cache_v': nrm(ks[5], (DEC_BATCH, DEPTH, PAST_LEN, KV_B, HD_B), 1.0),
        'c': nrm(ks[6], (DEC_BATCH, D_MODEL), 1.0),
        'c_ctx': nrm(ks[7], (D_MODEL,), 1.0),
        'ada_w': nrm(ks[8], (DEPTH, D_MODEL, N_MOD * D_MODEL), 0.5 * D_MODEL ** -0.5),
        'ada_b': nrm(ks[9], (DEPTH, N_MOD * D_MODEL), 0.01),
        'norm_ffn1': 1.0 + nrm(ks[10], (DEPTH, D_MODEL), 0.01),
        'ffn1_w13': nrm(ks[11], (DEPTH, D_MODEL, 2 * D_FF), D_MODEL ** -0.5),
        'ffn1_w2': nrm(ks[12], (DEPTH, D_FF, D_MODEL), D_FF ** -0.5),
        'norm_mix': 1.0 + nrm(ks[13], (DEPTH, D_MODEL), 0.01),
        'w_in': nrm(ks[14], (DEPTH, D_MODEL, IN_WIDTH), D_MODEL ** -0.5),
        'conv_w': nrm(ks[15], (DEPTH, CONV_W, QKV_A), CONV_W ** -0.5),
        'a_log': jnp.log(jax.random.uniform(ks[16], (DEPTH, 2, H_A), jnp.float32, 1.0, 16.0)),
        'dt_bias': dt + jnp.log(-jnp.expm1(-dt)),
        'onorm_a': 1.0 + nrm(ks[18], (DEPTH, DV_A), 0.01),
        'w_oa': nrm(ks[19], (DEPTH, H_A * DV_A, D_MODEL), (H_A * DV_A) ** -0.5),
        'w_ob': nrm(ks[20], (DEPTH, H_B * HD_B, D_MODEL), (H_B * HD_B) ** -0.5),
        'w_out': nrm(ks[21], (DEPTH, D_MODEL, D_MODEL), D_MODEL ** -0.5),
        'sink': nrm(ks[22], (DEPTH, H_B), 0.5),
        'norm_ffn2': 1.0 + nrm(ks[23], (DEPTH, D_MODEL), 0.01),
        'ffn2_w13': nrm(ks[24], (DEPTH, D_MODEL, 2 * D_FF), D_MODEL ** -0.5),
        'ffn2_w2': nrm(ks[25], (DEPTH, D_FF, D_MODEL), D_FF ** -0.5),
        'norm_final': 1.0 + nrm(ks[26], (D_MODEL,), 0.01),
    }


def reference(x_prompt, x_sample, state_delta_fwd, state_delta_bwd, cache_k, cache_v, c, c_ctx,
              ada_w, ada_b, norm_ffn1, ffn1_w13, ffn1_w2, norm_mix, w_in, conv_w, a_log, dt_bias,
              onorm_a, w_oa, w_ob, w_out, sink, norm_ffn2, ffn2_w13, ffn2_w2, norm_final):
    B_p, S_p = x_prompt.shape[:2]
    B_s, T_s = x_sample.shape[:2]
    rows = T_s // GRID_W
    cos, sin = axial_rope_tables(rows)
    xp, xs = x_prompt, x_sample
    st_f, st_b, ck, cv = [], [], [], []
    for l in range(DEPTH):
        mod_p = (jax.nn.silu(c_ctx) @ ada_w[l] + ada_b[l])[None, None, :]
        mod_s = (jax.nn.silu(c) @ ada_w[l] + ada_b[l])[:, None, :]
        sh1p, sc1p, g1p, sh2p, sc2p, g2p, sh3p, sc3p, g3p = jnp.split(mod_p, N_MOD, axis=-1)
        sh1s, sc1s, g1s, sh2s, sc2s, g2s, sh3s, sc3s, g3s = jnp.split(mod_s, N_MOD, axis=-1)

        xp = ffn_sublayer(xp, norm_ffn1[l], sh1p, sc1p, g1p, ffn1_w13[l], ffn1_w2[l])
        hp = modulate(rms_norm(xp, norm_mix[l]), sh2p, sc2p)
        qkv_a, z_a, dec_a, beta_a, q_b, k_b, v_b, gate_raw = split_mixer_inputs(hp @ w_in[l])
        zero_state = jnp.zeros((B_p, H_A, DK_A, DV_A), jnp.float32)
        y_a, s_f, s_b = mixer_a(qkv_a, z_a, dec_a, beta_a, conv_w[l], a_log[l], dt_bias[l], onorm_a[l],
                                zero_state, zero_state)
        k_p = k_b.reshape(B_p, S_p, KV_B, HD_B)
        v_p = v_b.reshape(B_p, S_p, KV_B, HD_B)
        y_b = context_attention(q_b.reshape(B_p, S_p, H_B, HD_B), k_p, v_p, sink[l])
        xp = xp + g2p * merge_branches(y_a, y_b, gate_raw, w_oa[l], w_ob[l], w_out[l])
        xp = ffn_sublayer(xp, norm_ffn2[l], sh3p, sc3p, g3p, ffn2_w13[l], ffn2_w2[l])
        st_f.append(s_f)
        st_b.append(s_b)
        ck.append(k_p)
        cv.append(v_p)

        xs = ffn_sublayer(xs, norm_ffn1[l], sh1s, sc1s, g1s, ffn1_w13[l], ffn1_w2[l])
        hs = modulate(rms_norm(xs, norm_mix[l]), sh2s, sc2s)
        qkv_a, z_a, dec_a, beta_a, q_b, k_b, v_b, gate_raw = split_mixer_inputs(hs @ w_in[l])
        y_a, _, _ = mixer_a(qkv_a, z_a, dec_a, beta_a, conv_w[l], a_log[l], dt_bias[l], onorm_a[l],
                            state_delta_fwd[:, l], state_delta_bwd[:, l])
        q_s = apply_rope(q_b.reshape(B_s, T_s, H_B, HD_B), cos, sin)
        k_s = apply_rope(k_b.reshape(B_s, T_s, KV_B, HD_B), cos, sin)
        y_b = window_attention(q_s, k_s, v_b.reshape(B_s, T_s, KV_B, HD_B), cache_k[:, l], cache_v[:, l], sink[l])
        xs = xs + g2s * merge_branches(y_a, y_b, gate_raw, w_oa[l], w_ob[l], w_out[l])
        xs = ffn_sublayer(xs, norm_ffn2[l], sh3s, sc3s, g3s, ffn2_w13[l], ffn2_w2[l])

    y_prompt = rms_norm(xp, norm_final)
    y_sample = rms_norm(xs, norm_final)
    new_state_delta_fwd = jnp.stack(st_f, axis=1)
    new_state_delta_bwd = jnp.stack(st_b, axis=1)
    new_cache_k = jnp.stack(ck, axis=1)
    new_cache_v = jnp.stack(cv, axis=1)
    return (y_prompt, y_sample, new_state_delta_fwd, new_state_delta_bwd, new_cache_k, new_cache_v)
```

```python
import numpy as np
from contextlib import ExitStack
import concourse.bass as bass
import concourse.mybir as mybir
from concourse.bass_utils import run_bass_kernel_spmd

F32 = mybir.dt.float32
BF16 = mybir.dt.bfloat16
AF = mybir.ActivationFunctionType
ALU = mybir.AluOpType

D = 1024
DFF = 2816
NFC = 22
EPS = 1e-6
NEG = -30000.0
DEBUG = False
STAGES = 99
NT_A = 10
SUB = 99
SUB2 = 99
NOASSERT = False
SKIPDB = False
PAD = 0
DBG_SEQS = None
DBG_NCH = 999
CSUB = 99


class Res:
    __slots__ = ("name", "lw", "rd")

    def __init__(self, name=""):
        self.name = name
        self.lw = []
        self.rd = []


class Buf(Res):
    __slots__ = ("t", "psum")

    def __init__(self, t, name="", psum=False):
        Res.__init__(self, name)
        self.t = t
        self.psum = psum

    def __getitem__(self, k):
        return self.t[k]


class Op:
    __slots__ = ("eng", "fn", "deps", "sig", "dma", "semv", "idx")


class Sched:
    ENG = ("pe", "act", "dve", "pool", "sp")
    DQ = ("sp", "pool")
    NDSEM = 8

    def __init__(self, nc, stack):
        self.nc = nc
        self.stack = stack
        self.ops = {e: [] for e in self.ENG}
        self.esem = {e: stack.enter_context(nc.semaphore("s_" + e)) for e in self.ENG}
        self.dsem = {e: [stack.enter_context(nc.semaphore("d_%s%d" % (e, i))) for i in range(self.NDSEM)]
                     for e in self.DQ}
        self.ndma = {e: 0 for e in self.DQ}
        self.dmatok = {e: [] for e in self.DQ}
        self.nbuf = 0
        self.pending = {e: [] for e in self.ENG}
        self.ring = []
        self.ringi = 0

    def sb(self, shape, dt, name=None, stack=None):
        self.nbuf += 1
        name = "%s_%d" % (name or "sb", self.nbuf)
        return Buf((stack or self.stack).enter_context(self.nc.sbuf_tensor(name, list(shape), dt)), name)

    def ps(self, shape, dt, name=None):
        self.nbuf += 1
        name = "%s_%d" % (name or "ps", self.nbuf)
        return Buf(self.stack.enter_context(self.nc.psum_tensor(name, list(shape), dt)), name, psum=True)

    def bank(self):
        b = self.ring[self.ringi % len(self.ring)]
        self.ringi += 1
        assert NOASSERT or (not b.lw) or b.rd, "ring bank reused before consumption: " + b.name
        return b

    def _rec(self, o, reads, writes, addw=False):
        deps = list(self.pending[o.eng])
        self.pending[o.eng] = []
        for r in reads:
            deps.extend(r.lw)
            if getattr(r, "psum", False):
                deps.extend([x for x in r.rd if x.eng != o.eng])
        for w in writes:
            deps.extend(w.lw)
            deps.extend(w.rd)
        if o.eng == "pe" and not o.dma:
            deps = [d for d in deps if d.dma or d.eng != "pe"]
        o.deps = deps
        o.idx = len(self.ops[o.eng])
        self.ops[o.eng].append(o)
        for r in reads:
            r.rd.append(o)
        for w in writes:
            if addw:
                w.lw = w.lw + [o]
            else:
                w.lw = [o]
                w.rd = []

    def op(self, eng, fn, reads=(), writes=()):
        o = Op()
        o.eng = eng; o.fn = fn; o.sig = False; o.dma = False; o.semv = None
        self._rec(o, reads, writes)
        return o

    def dma(self, q, out, in_, reads=(), writes=(), addw=False, **kw):
        o = Op()
        o.eng = q; o.dma = True; o.sig = True
        o.fn = lambda e: e.dma_start(out=out, in_=in_, **kw)
        self._rec(o, reads, writes, addw)
        n = self.ndma[q]
        self.ndma[q] += 1
        o.semv = (self.dsem[q][n % self.NDSEM], 16 * (n // self.NDSEM + 1))
        if n >= self.NDSEM:
            o.deps.append(self.dmatok[q][n - self.NDSEM])
        self.dmatok[q].append(o)
        return o

    def barrier(self):
        toks = []
        for e in self.ENG:
            comp = [o for o in self.ops[e] if not o.dma]
            if comp:
                toks.append(comp[-1])
        for q in self.DQ:
            toks.extend(self.dmatok[q][-self.NDSEM:])
        for e in self.ENG:
            self.pending[e] = self.pending[e] + toks

    def emit(self, block, final_waits=()):
        for e in self.ENG:
            for o in self.ops[e]:
                for d in o.deps:
                    if not d.dma:
                        d.sig = True
        for e in self.ENG:
            c = 0
            for o in self.ops[e]:
                if not o.dma and o.sig:
                    c += 1
                    o.semv = (self.esem[e], c)
        engobj = {"pe": "tensor", "act": "scalar", "dve": "vector", "pool": "gpsimd", "sp": "sync"}

        def make(e):
            def body(E):
                waited = {}

                def wait(tok):
                    s, v = tok.semv
                    k = id(s)
                    if waited.get(k, 0) >= v:
                        return
                    waited[k] = v
                    E.wait_ge(s, v)
                for o in self.ops[e]:
                    for d in o.deps:
                        if d is not o:
                            wait(d)
                    ins = o.fn(E)
                    if o.sig:
                        s, v = o.semv
                        ins.then_inc(s, 16 if o.dma else 1)
                if e == "sp":
                    for o in final_waits:
                        wait(o)
            return body
        for e in self.ENG:
            getattr(block, engobj[e])(make(e))


def bc(ap, shape):
    return ap.broadcast_to(list(shape))


def build_program():
    nc = bass.Bass("TRN2", target_bir_lowering=False)
    SK = "ExternalOutput" if DEBUG else "Internal"

    def din(name, shape, dt=F32):
        return nc.dram_tensor(name, list(shape), dt, kind="ExternalInput").ap()

    def dout(name, shape, dt=F32):
        return nc.dram_tensor(name, list(shape), dt, kind="ExternalOutput").ap()

    def dscr(name, shape, dt=F32, dbg=False):
        return nc.dram_tensor(name, list(shape), dt, kind=(SK if dbg else "Internal")).ap()

    xin = {"p": din("xp", [1024, D]), "s": din("xs", [4096, D])}
    sf0 = din("sf0", [4, 128, 128]); sb0 = din("sb0", [4, 128, 128])
    ck = din("ck", [256, 128]); cv = din("cv", [256, 128])
    cvec = din("cvec", [2, D])
    ada_w = din("ada_w", [D, 9 * D]); ada_b = din("ada_b", [1, 9 * D])
    normw = [din("norm_ffn1", [1, D]), din("norm_mix", [1, D]), din("norm_ffn2", [1, D])]
    w13 = [din("ffn1_w13", [D, 2 * DFF]), din("ffn2_w13", [D, 2 * DFF])]
    w2 = [din("ffn1_w2", [DFF, D]), din("ffn2_w2", [DFF, D])]
    w_in = din("w_in", [D, 4880])
    conv_w = din("conv_w", [5, 1536])
    a_log = din("a_log", [1, 8]); dt_bias = din("dt_bias", [1, 8])
    onorm = din("onorm_a", [1, 128])
    w_oa = din("w_oa", [512, D]); w_ob = din("w_ob", [512, D]); w_out = din("w_out", [D, D])
    sink = din("sink", [1, 8])
    norm_final = din("norm_final", [1, D])
    cst = din("cst", [128, 13, 128])
    csT = din("csT", [2, 128, 4096])
    cs = din("cs", [4096, 64])

    yout = {"p": dout("yp", [1024, D]), "s": dout("ys", [4096, D])}
    nsf = dout("nsf", [4, 4, 128, 128]); nsb = dout("nsb", [4, 4, 128, 128])
    nck = dout("nck", [1024, 128]); ncv = dout("ncv", [1024, 128])

    w13s = [dscr("w13s%d" % f, [NFC, 128, 2, 8, 128], BF16) for f in range(2)]
    w2s = [dscr("w2s%d" % f, [128, NFC, D], BF16) for f in range(2)]
    wina = dscr("wina", [33, 128, 8, 128], BF16)
    winb = dscr("winb", [128, 8, 1296], BF16)
    modrow = dscr("modrow", [2, 3, D])
    G = {"p": dict(nseq=4, T=256, v=0), "s": dict(nseq=1, T=4096, v=1)}
    for g, gd in G.items():
        N = gd["nseq"] * gd["T"]
        gd["N"] = N
        gd["x1"] = dscr("x1_" + g, [N, D], dbg=True)
        gd["rawT"] = dscr("rawT_" + g, [12, 128, gd["nseq"], gd["T"]], dbg=True)
        gd["zs"] = dscr("zs_" + g, [N, 512], dbg=True)
        gd["db"] = dscr("db_" + g, [N, 128], dbg=True)
        gd["QT"] = dscr("QT_" + g, [4, 128, N], BF16)
        gd["KT"] = dscr("KT_" + g, [2, 128, N], BF16)
        gd["Vt"] = dscr("Vt_" + g, [N, 128], BF16)
        gd["gT"] = dscr("gT_" + g, [16, 128, N], BF16)
        gd["of"] = dscr("of_" + g, [N, 512], dbg=True)
        gd["QA"] = dscr("QA_" + g, [N // 128, 128, 16, 128], BF16)

    with ExitStack() as st:
        S = Sched(nc, st)
        block = st.enter_context(nc.Block())
        finals = []
        tb = S.ps([128, 8, 128], BF16, "tb")
        accN = S.ps([128, 512], F32, "accN")
        accD = S.ps([128, 512], F32, "accD")
        S.ring = [S.ps([128, 512], F32, "rb") for _ in range(5)]

        def v4(b):
            return b[:].rearrange("p (h i) -> p h i", h=4)

        cstf = S.sb([128, 13, 128], F32, "cstf")
        S.dma("sp", cstf[:], cst, writes=[cstf])
        identb = S.sb([128, 128], BF16, "identb")
        onesb = S.sb([128, 128], BF16, "onesb")
        mprev = S.sb([128, 128], BF16, "mprev")
        mnext = S.sb([128, 128], BF16, "mnext")
        S.op("dve", lambda e: e.tensor_copy(out=identb[:], in_=cstf[:, 0, :]), [cstf], [identb])
        S.op("dve", lambda e: e.tensor_copy(out=onesb[:], in_=cstf[:, 7, :]), [cstf], [onesb])
        S.op("dve", lambda e: e.tensor_copy(out=mprev[:], in_=cstf[:, 8, :]), [cstf], [mprev])
        S.op("dve", lambda e: e.tensor_copy(out=mnext[:], in_=cstf[:, 9, :]), [cstf], [mnext])
        IDf = lambda: cstf[:, 0, :]
        Ud = lambda d: cstf[:, 1 + d, :]
        NEGL = lambda d: cstf[:, 3 + d, :]
        NEGA = lambda d: cstf[:, 5 + d, :]
        ONESf = lambda: cstf[:, 7, :]
        modA = S.sb([128, 3, 2, 8], F32, "modA")
        modB = S.sb([128, 3, 2, 8], F32, "modB")
        convw = S.sb([128, 5, 12], F32, "convw")
        onbc = S.sb([128, 128], F32, "onbc")
        sexp = S.sb([128, 8], F32, "sexp")
        negA = S.sb([128, 8], F32, "negA")
        dtb = S.sb([128, 8], F32, "dtb")
        ss = S.sb([128, 8], F32, "ss")
        rs = S.sb([128, 8], F32, "rs")

        Rw13 = [[Res() for _ in range(NFC)] for _ in range(2)]
        Rw2 = [Res(), Res()]
        Rwina = [Res() for _ in range(33)]
        Rwinb = Res()

        def conv_w13(f):
            src = w13[f].rearrange("(kc p) n -> p kc n", p=128)
            for j in range(NFC):
                for ab in range(2):
                    c0 = ab * DFF + j * 128
                    S.dma("pool", w13s[f][j, :, ab], src[:, :, c0:c0 + 128], writes=[Rw13[f][j]], addw=True)

        def conv_w2(f):
            src = w2[f].rearrange("(fc p) d -> p fc d", p=128)
            for h in range(2):
                S.dma("pool", w2s[f][:, h * 11:(h + 1) * 11, :], src[:, h * 11:(h + 1) * 11, :], writes=[Rw2[f]], addw=True)

        def conv_win():
            src = w_in.rearrange("(kc p) n -> p kc n", p=128)
            for j in range(33):
                c0 = j * 128 if j < 12 else (2832 + (j - 12) * 128 if j < 28 else 2064 + (j - 28) * 128)
                S.dma("pool", wina[j], src[:, :, c0:c0 + 128], writes=[Rwina[j]])
            S.dma("pool", winb, src[:, :, 1536:2832], writes=[Rwinb])

        with ExitStack() as pst:
            def load_T(dbuf, dst, src_rows, R):
                tmp_ = S.sb([R, 128], F32, "ldT", pst)
                S.dma("sp", tmp_[:], src_rows, writes=[tmp_])
                pb__ = S.bank()
                S.op("pe", lambda e: e.matmul(pb__[:, 0:R], lhsT=tmp_[:], rhs=cstf[0:R, 0, 0:R], start=True, stop=True), [tmp_, cstf], [pb__])
                S.op("dve", lambda e: e.tensor_copy(out=dst, in_=pb__[:, 0:R]), [pb__], [dbuf])
            cT = S.sb([128, 2, 8], F32, "cT", pst)
            load_T(cT, cT[:].rearrange("p v k -> p (v k)"), cvec.rearrange("v (kc p) -> (v kc) p", p=128), 16)
            scT = S.sb([128, 8, 2], BF16, "scT", pst)
            S.op("act", lambda e: e.activation(out=scT[:].rearrange("p k v -> p v k"), in_=cT[:], func=AF.Silu), [cT], [scT])
            adabT = S.sb([128, 72], F32, "adabT", pst)
            load_T(adabT, adabT[:], ada_b.rearrange("o (c p) -> (o c) p", p=128), 72)
            nwT = S.sb([128, 3, 8], F32, "nwT", pst)
            for n in range(3):
                load_T(nwT, nwT[:, n, :], normw[n].rearrange("o (c p) -> (o c) p", p=128), 8)
            load_T(convw, convw[:].rearrange("p w c -> p (w c)"), conv_w.rearrange("w (c p) -> (w c) p", p=128), 60)
            S.dma("sp", onbc[:], onorm.partition_broadcast(128), writes=[onbc])
            S.dma("sp", sexp[:], sink.partition_broadcast(128), writes=[sexp])
            S.dma("sp", negA[:], a_log.partition_broadcast(128), writes=[negA])
            S.dma("sp", dtb[:], dt_bias.partition_broadcast(128), writes=[dtb])
            S.op("act", lambda e: e.activation(out=sexp[:], in_=sexp[:], func=AF.Exp), [sexp], [sexp])
            S.op("act", lambda e: e.activation(out=negA[:], in_=negA[:], func=AF.Exp), [negA], [negA])
            S.op("dve", lambda e: e.tensor_scalar(out=negA[:], in0=negA[:], scalar1=-1.0, scalar2=None, op0=ALU.mult), [negA], [negA])
            adap = [S.sb([128, 8, 128], BF16, "adap", pst) for _ in range(3)]
            pm = S.bank()
            pmv = pm[:, 0:144].rearrange("p (c v) -> p c v", v=2)
            asrc = ada_w.rearrange("(kc p) n -> p kc n", p=128)
            conv_w13(0)
            for j in range(72):
                a = adap[j % 3]
                S.dma("pool", a[:], asrc[:, :, j * 128:(j + 1) * 128], writes=[a])
                for kc in range(8):
                    S.op("pe", lambda e, a=a, kc=kc, j=j: e.matmul(pmv[:, j, :], lhsT=a[:, kc, :], rhs=scT[:, kc, :],
                                                                  start=(kc == 0), stop=(kc == 7)), [a, scT], [pm])
                if j == 24:
                    conv_w2(0)
            modT = S.sb([128, 72, 2], F32, "modT", pst)
            S.op("dve", lambda e: e.tensor_tensor(out=modT[:], in0=pmv, in1=bc(adabT[:].unsqueeze(2), [128, 72, 2]), op=ALU.add),
                 [pm, adabT], [modT])
            gs = S.sb([128, 2, 3, 8], F32, "gs", pst)
            for n in range(3):
                for v in range(2):
                    c_sh, c_sc, c_g = (3 * n) * 8, (3 * n + 1) * 8, (3 * n + 2) * 8
                    S.op("dve", lambda e, n=n, v=v, c=c_sc: e.scalar_tensor_tensor(
                        out=modA[:, n, v, :], in0=modT[:, c:c + 8, v], scalar=1.0, in1=nwT[:, n, :], op0=ALU.add, op1=ALU.mult),
                        [modT, nwT], [modA])
                    S.op("dve", lambda e, n=n, v=v, c=c_sh: e.tensor_copy(out=modB[:, n, v, :], in_=modT[:, c:c + 8, v]), [modT], [modB])
                    S.op("dve", lambda e, n=n, v=v, c=c_g: e.tensor_scalar(
                        out=gs[:, v, n, :], in0=modT[:, c:c + 8, v], scalar1=(1.0 if n == 1 else 0.5), scalar2=None, op0=ALU.mult),
                        [modT], [gs])
            Rmod = Res()
            pgs = S.bank()
            gsT = S.sb([48, 128], F32, "gsT", pst)
            S.op("pe", lambda e: e.matmul(pgs[0:48, 0:128], lhsT=gs[:].rearrange("p v n c -> p (v n c)"), rhs=cstf[:, 0, :], start=True, stop=True), [gs, cstf], [pgs])
            S.op("dve", lambda e: e.tensor_copy(out=gsT[:], in_=pgs[0:48, 0:128]), [pgs], [gsT])
            S.dma("sp", modrow.rearrange("v n (c p) -> (v n c) p", p=128), gsT[:], reads=[gsT], writes=[Rmod])
            conv_win()
            conv_w13(1)
            conv_w2(1)
            S.barrier()

        def rstd_from_ss(col, scale):
            S.op("dve", lambda e: e.tensor_scalar(out=rs[:, col], in0=ss[:, col], scalar1=scale, scalar2=EPS, op0=ALU.mult, op1=ALU.add),
                 [ss], [rs])
            S.op("act", lambda e: e.activation(out=rs[:, col], in_=rs[:, col], func=AF.Ln), [rs], [rs])
            S.op("act", lambda e: e.activation(out=rs[:, col], in_=rs[:, col], func=AF.Exp, scale=-0.5), [rs], [rs])

        def norm_to_hT(xt, n, v, xn, hT, nsub):
            for sub in range(nsub):
                x = xt[sub]
                S.op("pool", lambda e: e.memset(ss[:, 0:1], 0.0), [], [ss])
                S.op("act", lambda e, x=x: e.activation(out=xn[:], in_=x[:], func=AF.Square, accum_out=ss[:, 0:1]), [x], [xn, ss])
                rstd_from_ss(slice(0, 1), 1.0 / D)
                S.op("dve", lambda e, x=x: e.tensor_scalar(out=xn[:], in0=x[:], scalar1=rs[:, 0:1], scalar2=None, op0=ALU.mult), [x, rs], [xn])
                for kc in range(8):
                    S.op("pe", lambda e, kc=kc: e.transpose(out=tb[:, kc, :], in_=xn[:, kc * 128:(kc + 1) * 128], identity=identb[:]),
                         [xn, identb], [tb])
                hs = hT[:, :, sub * 128:(sub + 1) * 128]
                S.op("dve", lambda e, hs=hs: e.tensor_tensor(out=hs, in0=tb[:], in1=bc(modA[:, n, v, :].unsqueeze(2), [128, 8, 128]), op=ALU.mult),
                     [tb, modA], [hT])
                S.op("dve", lambda e, hs=hs: e.tensor_tensor(out=hs, in0=hs, in1=bc(modB[:, n, v, :].unsqueeze(2), [128, 8, 128]), op=ALU.add),
                     [hT, modB], [hT])

        def ffn_stage(f, n, gi, tiles, tail, extra_alloc):
            with ExitStack() as fs:
                xt = [S.sb([128, D], F32, "xt", fs) for _ in range(4)]
                xn = S.sb([128, D], BF16, "xn", fs)
                hT = S.sb([128, 8, 512], BF16, "hT", fs)
                w13p = [S.sb([128, 2, 8, 128], BF16, "w13p", fs) for _ in range(3)]
                sa = [S.sb([128, 512], F32, "sa", fs) for _ in range(2)]
                gTt = S.sb([128, NFC, 512], BF16, "gTt", fs)
                w2q = [S.sb([128, NFC, 256], BF16, "w2q", fs) for _ in range(2)]
                tmp = [S.sb([128, 256], F32, "tmp", fs) for _ in range(2)]
                gbc = [S.sb([128, D], F32, "gbc", fs) for _ in range(2)]
                ex = extra_alloc(fs)
                cnt = dict(w13=0, w2=0, sa=0, tmp=0)
                for ti, tl in enumerate(tiles):
                    v = tl["v"]
                    for sub in range(4):
                        S.dma("sp", xt[sub][:], tl["src"][sub * 128:(sub + 1) * 128, :], writes=[xt[sub]])
                    gb_ = gbc[ti % 2]
                    S.dma("sp", gb_[:], modrow[v, n].partition_broadcast(128), reads=[Rmod], writes=[gb_])
                    norm_to_hT(xt, n, v, xn, hT, 4)
                    for j in range(NFC if SUB >= 1 else 0):
                        wp = w13p[cnt["w13"] % 3]; cnt["w13"] += 1
                        S.dma("sp", wp[:], w13s[f][j], reads=[Rw13[f][j]], writes=[wp])
                        pa = S.bank(); pb = S.bank()
                        for ab, pp in ((0, pa), (1, pb)):
                            for kc in range(8):
                                S.op("pe", lambda e, wp=wp, ab=ab, pp=pp, kc=kc: e.matmul(pp[:], lhsT=wp[:, ab, kc, :], rhs=hT[:, kc, :],
                                                                                     start=(kc == 0), stop=(kc == 7)), [wp, hT], [pp])
                        s_ = sa[cnt["sa"] % 2]; cnt["sa"] += 1
                        S.op("act", lambda e, s_=s_, pa=pa: e.activation(out=s_[:], in_=pa[:], func=AF.Silu), [pa], [s_])
                        S.op("dve", lambda e, s_=s_, pb=pb, j=j: e.tensor_tensor(out=gTt[:, j, :], in0=pb[:], in1=s_[:], op=ALU.mult),
                             [pb, s_], [gTt])
                    for q in range(4 if SUB >= 2 else 0):
                        wq = w2q[cnt["w2"] % 2]; cnt["w2"] += 1
                        S.dma("sp", wq[:], w2s[f][:, :, q * 256:(q + 1) * 256], reads=[Rw2[f]], writes=[wq])
                        for sub in range(4):
                            pd = S.bank()
                            for fc in range(NFC):
                                S.op("pe", lambda e, pd=pd, fc=fc, sub=sub, wq=wq: e.matmul(
                                    pd[:, 0:256], lhsT=gTt[:, fc, sub * 128:(sub + 1) * 128], rhs=wq[:, fc, :],
                                    start=(fc == 0), stop=(fc == NFC - 1)), [gTt, wq], [pd])
                            t_ = tmp[cnt["tmp"] % 2]; cnt["tmp"] += 1
                            S.op("dve", lambda e, t_=t_, pd=pd, q=q, gb_=gb_: e.tensor_tensor(
                                out=t_[:], in0=pd[:, 0:256], in1=gb_[:, q * 256:(q + 1) * 256], op=ALU.mult), [pd, gb_], [t_])
                            xs_ = xt[sub]
                            S.op("pool", lambda e, t_=t_, xs_=xs_, q=q: e.tensor_tensor(
                                out=xs_[:, q * 256:(q + 1) * 256], in0=xs_[:, q * 256:(q + 1) * 256], in1=t_[:], op=ALU.add), [xs_, t_], [xs_])
                    if SUB >= 3:
                        tail(tl, xt, xn, hT, ex)
                S.barrier()

        def allocA(fs):
            ex = {}
            ex["winp"] = [S.sb([128, 8, 128], BF16, "winp", fs) for _ in range(3)]
            ex["winb"] = S.sb([128, 8, 1296], BF16, "winb", fs)
            ex["rawo"] = [S.sb([128, 512], F32, "rawo", fs) for _ in range(2)]
            ex["sgo"] = [S.sb([128, 512], BF16, "sgo", fs) for _ in range(2)]
            ex["zo"] = [S.sb([128, 512], F32, "zo", fs) for _ in range(2)]
            ex["dbo"] = [S.sb([128, 128], F32, "dbo", fs) for _ in range(2)]
            ex["qk"] = S.sb([128, 10, 64], F32, "qk", fs)
            ex["kvo"] = S.sb([128, 256], F32, "kvo", fs)
            ex["vo"] = [S.sb([128, 128], BF16, "vo", fs) for _ in range(2)]
            ex["qr"] = S.sb([128, 12, 64], BF16, "qr", fs)
            ex["ta"] = S.sb([128, 10, 32], F32, "ta", fs)
            ex["tb2"] = S.sb([128, 10, 32], F32, "tb2", fs)
            ex["cst"] = S.sb([128, 64], F32, "cst", fs)
            ex["qkT"] = [S.sb([128, 6, 128], BF16, "qkT", fs) for _ in range(2)]
            ex["xf"] = S.sb([128, 512], F32, "xf", fs)
            ex["xr"] = S.sb([128, 512], F32, "xr", fs)
            ex["rt"] = S.sb([128, 512], F32, "rt", fs)
            ex["cosT"] = S.sb([128, 512], F32, "cosT", fs)
            ex["sinT"] = S.sb([128, 512], F32, "sinT", fs)
            ex["cnt"] = 0
            return ex

        def tailA(tl, xt, xn, hT, ex):
            g = tl["g"]; gd = G[g]; n0 = tl["n0"]; v = tl["v"]
            for sub in range(4):
                S.dma("pool", gd["x1"][n0 + sub * 128:n0 + (sub + 1) * 128, :], xt[sub][:], reads=[xt[sub]])
            norm_to_hT(xt, 1, v, xn, hT, 4)
            if SUB < 4:
                return
            wb = ex["winb"]
            S.dma("sp", wb[:], winb, reads=[Rwinb], writes=[wb])
            if g == "s":
                S.dma("sp", ex["cosT"][:], csT[0, :, n0:n0 + 512], writes=[ex["cosT"]])
                S.dma("sp", ex["sinT"][:], csT[1, :, n0:n0 + 512], writes=[ex["sinT"]])
            for j in range(33):
                wp = ex["winp"][j % 3]
                S.dma("sp", wp[:], wina[j], reads=[Rwina[j]], writes=[wp])
                pp = S.bank()
                for kc in range(8):
                    S.op("pe", lambda e, wp=wp, pp=pp, kc=kc: e.matmul(pp[:], lhsT=wp[:, kc, :], rhs=hT[:, kc, :], start=(kc == 0), stop=(kc == 7)),
                         [wp, hT], [pp])
                if j < 12:
                    ro = ex["rawo"][j % 2]
                    S.op("act", lambda e, ro=ro, pp=pp: e.activation(out=ro[:], in_=pp[:], func=AF.Copy), [pp], [ro])
                    if g == "p":
                        s0 = n0 // 256
                        S.dma("pool", gd["rawT"][j, :, s0:s0 + 2, :], ro[:].rearrange("p (s t) -> p s t", s=2), reads=[ro])
                    else:
                        S.dma("pool", gd["rawT"][j, :, 0, n0:n0 + 512], ro[:], reads=[ro])
                elif j < 28:
                    so = ex["sgo"][j % 2]
                    S.op("act", lambda e, so=so, pp=pp: e.activation(out=so[:], in_=pp[:], func=AF.Sigmoid), [pp], [so])
                    S.dma("pool", gd["gT"][j - 12, :, n0:n0 + 512], so[:], reads=[so])
                else:
                    xf = ex["xf"]; xr = ex["xr"]; rt = ex["rt"]
                    if g == "s":
                        S.op("act", lambda e, pp=pp: e.activation(out=xf[:], in_=pp[:], func=AF.Copy), [pp], [xf])
                        prot = S.bank()
                        S.op("pe", lambda e, prot=prot: e.matmul(prot[:], lhsT=cstf[:, 10, :], rhs=xf[:], start=True, stop=True), [cstf, xf], [prot])
                        S.op("dve", lambda e, prot=prot: e.tensor_tensor(out=rt[:], in0=prot[:], in1=ex["sinT"][:], op=ALU.mult), [prot, ex["sinT"]], [rt])
                        S.op("pool", lambda e: e.tensor_tensor(out=xr[:], in0=xf[:], in1=ex["cosT"][:], op=ALU.mult), [xf, ex["cosT"]], [xr])
                        src = xr
                        if j < 32:
                            so = ex["sgo"][j % 2]
                            S.op("dve", lambda e, so=so: e.tensor_tensor(out=so[:], in0=xr[:], in1=rt[:], op=ALU.add), [xr, rt], [so])
                        else:
                            S.op("dve", lambda e: e.tensor_tensor(out=xr[:], in0=xr[:], in1=rt[:], op=ALU.add), [xr, rt], [xr])
                    else:
                        if j < 32:
                            so = ex["sgo"][j % 2]
                            S.op("act", lambda e, so=so, pp=pp: e.activation(out=so[:], in_=pp[:], func=AF.Copy), [pp], [so])
                        else:
                            S.op("act", lambda e, pp=pp: e.activation(out=xr[:], in_=pp[:], func=AF.Copy), [pp], [xr])
                    if j < 32:
                        S.dma("pool", gd["QT"][j - 28, :, n0:n0 + 512], so[:], reads=[so])
                    else:
                        for gk in range(2):
                            psel = S.bank()
                            S.op("pe", lambda e, psel=psel, gk=gk: e.matmul(psel[:], lhsT=cstf[:, 11 + gk, :], rhs=xr[:], start=True, stop=True), [cstf, xr], [psel])
                            so = ex["sgo"][gk]
                            S.op("act", lambda e, so=so, psel=psel: e.activation(out=so[:], in_=psel[:], func=AF.Copy), [psel], [so])
                            S.dma("pool", gd["KT"][gk, :, n0:n0 + 512], so[:], reads=[so])
            for sub in range(4 if SUB >= 5 else 0):
                r0 = n0 + sub * 128
                hs = lambda kc, sub=sub: hT[:, kc, sub * 128:(sub + 1) * 128]
                c = ex["cnt"]; ex["cnt"] += 1
                pz = S.bank()
                for kc in range(8):
                    S.op("pe", lambda e, pz=pz, kc=kc, hs=hs: e.matmul(pz[:], lhsT=hs(kc), rhs=wb[:, kc, 0:512], start=(kc == 0), stop=(kc == 7)), [hT, wb], [pz])
                zo = ex["zo"][c % 2]
                S.op("act", lambda e, zo=zo, pz=pz: e.activation(out=zo[:], in_=pz[:], func=AF.Silu), [pz], [zo])
                S.dma("pool", gd["zs"][r0:r0 + 128, :], zo[:], reads=[zo])
                if SUB2 < 2:
                    continue
                pk = S.bank()
                for kc in range(8):
                    S.op("pe", lambda e, pk=pk, kc=kc, hs=hs: e.matmul(pk[:, 0:256], lhsT=hs(kc), rhs=wb[:, kc, 1040:1296], start=(kc == 0), stop=(kc == 7)), [hT, wb], [pk])
                if not SKIPDB:
                    pdb = S.bank()
                    for kc in range(8):
                        S.op("pe", lambda e, pdb=pdb, kc=kc, hs=hs: e.matmul(pdb[:, 0:128], lhsT=hs(kc), rhs=wb[:, kc, 512:640], start=(kc == 0), stop=(kc == 7)), [hT, wb], [pdb])
                    dbo = ex["dbo"][c % 2]
                    S.op("dve", lambda e, dbo=dbo, pdb=pdb: e.tensor_copy(out=dbo[:], in_=pdb[:, 0:128]), [pdb], [dbo])
                    S.dma("pool", gd["db"][r0:r0 + 128, :], dbo[:], reads=[dbo])
                if SUB2 < 3:
                    continue
                vo = ex["vo"][c % 2]
                S.op("dve", lambda e, vo=vo, pk=pk: e.tensor_copy(out=vo[:], in_=pk[:, 128:256]), [pk], [vo])
                S.dma("pool", gd["Vt"][r0:r0 + 128, :], vo[:], reads=[vo])
                if SUB2 < 4:
                    continue
                if g == "p":
                    kvo = ex["kvo"]
                    S.op("act", lambda e, pk=pk: e.activation(out=kvo[:], in_=pk[:, 0:256], func=AF.Copy), [pk], [kvo])
                    finals.append(S.dma("pool", nck[r0:r0 + 128, :], kvo[:, 0:128], reads=[kvo]))
                    finals.append(S.dma("pool", ncv[r0:r0 + 128, :], kvo[:, 128:256], reads=[kvo]))

        tilesA = []
        for i in range(2):
            tilesA.append(dict(g="p", n0=i * 512, v=0, src=xin["p"][i * 512:(i + 1) * 512, :]))
        for i in range(8):
            tilesA.append(dict(g="s", n0=i * 512, v=1, src=xin["s"][i * 512:(i + 1) * 512, :]))
        if STAGES >= 1:
            ffn_stage(0, 0, 0, tilesA[:NT_A], tailA, allocA)
        for _ in range(PAD):
            S.op("pe", lambda e: e.matmul(accN[:, 0:128], lhsT=identb[:], rhs=identb[:], start=True, stop=True), [identb], [accN])

        def mixer_alloc(ms):
            m = {}
            m["raw"] = S.sb([128, 12, 132], F32, "raw", ms)
            m["cT"] = S.sb([128, 12, 128], F32, "cTt", ms)
            m["sq"] = S.sb([128, 8, 128], BF16, "sq", ms)
            m["rinv"] = S.sb([128, 8, 128], F32, "rinv", ms)
            m["qk"] = S.sb([128, 16, 128], BF16, "qkA", ms)
            m["vT"] = S.sb([128, 4, 128], BF16, "vT", ms)
            m["db"] = S.sb([128, 128], F32, "dbt", ms)
            m["g"] = S.sb([128, 4], F32, "g", ms)
            m["beta"] = S.sb([128, 4], F32, "beta", ms)
            m["sm"] = S.sb([128, 8, 4], F32, "sm", ms)
            m["Gb"] = S.sb([128, 4, 128], F32, "Gb", ms)
            m["X1"] = S.sb([128, 4, 128], F32, "X1", ms)
            m["X2"] = S.sb([128, 4, 128], F32, "X2", ms)
            m["EG"] = S.sb([128, 4, 128], F32, "EG", ms)
            m["tf"] = S.sb([128, 4, 128], F32, "tf", ms)
            m["L"] = [S.sb([128, 4, 128], F32, "L", ms) for _ in range(2)]
            m["LT"] = [S.sb([128, 4, 128], F32, "LT", ms) for _ in range(2)]
            m["TT"] = [S.sb([128, 4, 128], F32, "TT", ms) for _ in range(2)]
            m["TTb"] = S.sb([128, 4, 128], BF16, "TTb", ms)
            m["aT"] = S.sb([128, 4, 128], BF16, "aT", ms)
            m["qgT"] = S.sb([128, 4, 128], BF16, "qgT", ms)
            m["vb"] = S.sb([128, 4, 128], BF16, "vb", ms)
            m["kbg"] = S.sb([128, 4, 128], BF16, "kbg", ms)
            m["kd"] = S.sb([128, 4, 128], BF16, "kd", ms)
            m["nwT"] = S.sb([128, 4, 128], BF16, "nwTm", ms)
            m["vnew"] = S.sb([128, 4, 128], BF16, "vnew", ms)
            m["S"] = S.sb([128, 4, 128], F32, "S", ms)
            m["Sb"] = S.sb([128, 4, 128], BF16, "Sb", ms)
            return m

        def conv_prep(m, gd, s, c):
            raw = m["raw"]; cT_ = m["cT"]; qk = m["qk"]
            lo = max(c * 128 - 2, 0); hi = min(c * 128 + 130, gd["T"])
            o0 = lo - (c * 128 - 2)
            if o0 > 0:
                S.op("pool", lambda e: e.memset(raw[:, :, 0:2], 0.0), [], [raw])
            if hi < c * 128 + 130:
                S.op("pool", lambda e: e.memset(raw[:, :, 130:132], 0.0), [], [raw])
            S.dma("sp", raw[:, :, o0:o0 + hi - lo], gd["rawT"].rearrange("j p s t -> p j s t")[:, :, s, lo:hi], writes=[raw], addw=(o0 > 0 or hi < c * 128 + 130))
            for jc in range(12):
                S.op("act", lambda e, jc=jc: e.activation(out=cT_[:, jc, :], in_=raw[:, jc, 0:128], func=AF.Identity, scale=convw[:, 0, jc:jc + 1]),
                     [raw, convw], [cT_])
                for tap in range(1, 5):
                    S.op("dve", lambda e, jc=jc, tap=tap: e.scalar_tensor_tensor(
                        out=cT_[:, jc, :], in0=raw[:, jc, tap:tap + 128], scalar=convw[:, tap, jc:jc + 1], in1=cT_[:, jc, :],
                        op0=ALU.mult, op1=ALU.add), [raw, convw, cT_], [cT_])
            S.op("act", lambda e: e.activation(out=cT_[:], in_=cT_[:], func=AF.Silu), [cT_], [cT_])
            yield
            S.op("act", lambda e: e.activation(out=m["sq"][:], in_=cT_[:, 0:8, :], func=AF.Square), [cT_], [m["sq"]])
            yield
            for hf in range(2):
                pb_ = S.bank()
                for x in range(4):
                    S.op("pe", lambda e, pb_=pb_, x=x, hf=hf: e.matmul(v4(pb_)[:, x, :], lhsT=onesb[:], rhs=m["sq"][:, hf * 4 + x, :], start=True, stop=True),
                         [onesb, m["sq"]], [pb_])
                ri = m["rinv"][:, hf * 4:(hf + 1) * 4, :]
                S.op("dve", lambda e, pb_=pb_, ri=ri: e.tensor_scalar(out=ri, in0=v4(pb_), scalar1=EPS, scalar2=None, op0=ALU.add), [pb_], [m["rinv"]])
            S.op("act", lambda e: e.activation(out=m["rinv"][:], in_=m["rinv"][:], func=AF.Ln), [m["rinv"]], [m["rinv"]])
            yield
            S.op("act", lambda e: e.activation(out=m["rinv"][:], in_=m["rinv"][:], func=AF.Exp, scale=-0.5), [m["rinv"]], [m["rinv"]])
            yield
            S.op("dve", lambda e: e.scalar_tensor_tensor(out=qk[:, 0:4, :], in0=cT_[:, 0:4, :], scalar=128.0 ** -0.5, in1=m["rinv"][:, 0:4, :],
                                                         op0=ALU.mult, op1=ALU.mult), [cT_, m["rinv"]], [qk])
            S.op("dve", lambda e: e.tensor_tensor(out=qk[:, 4:8, :], in0=cT_[:, 4:8, :], in1=m["rinv"][:, 4:8, :], op=ALU.mult), [cT_, m["rinv"]], [qk])
            S.op("act", lambda e: e.activation(out=m["vT"][:], in_=cT_[:, 8:12, :], func=AF.Copy), [cT_], [m["vT"]])
            yield
            for x in range(4):
                S.op("pe", lambda e, x=x: e.transpose(out=tb[:, x, :], in_=qk[:, 4 + x, :], identity=identb[:]), [qk, identb], [tb])
            for x in range(4):
                S.op("pe", lambda e, x=x: e.transpose(out=tb[:, 4 + x, :], in_=m["vT"][:, x, :], identity=identb[:]), [m["vT"], identb], [tb])
            S.op("act", lambda e: e.activation(out=qk[:, 8:16, :], in_=tb[:], func=AF.Copy), [tb], [qk])

        def chunk_prep(m, gd, s, c, d):
            qk = m["qk"]; sm = m["sm"]; g_ = m["g"]; beta = m["beta"]; db_ = m["db"]
            n0 = s * gd["T"] + c * 128
            qT = lambda h: qk[:, h, :]
            kT = lambda h: qk[:, 4 + h, :]
            S.dma("sp", db_[:], gd["db"][n0:n0 + 128, :], writes=[db_])
            S.op("dve", lambda e: e.tensor_tensor(out=g_[:], in0=db_[:, 4 * d:4 * d + 4], in1=dtb[:, 4 * d:4 * d + 4], op=ALU.add), [db_, dtb], [g_])
            S.op("act", lambda e: e.activation(out=g_[:], in_=g_[:], func=AF.Exp), [g_], [g_])
            S.op("dve", lambda e: e.tensor_scalar(out=g_[:], in0=g_[:], scalar1=1.0, scalar2=None, op0=ALU.add), [g_], [g_])
            S.op("act", lambda e: e.activation(out=g_[:], in_=g_[:], func=AF.Ln), [g_], [g_])
            S.op("dve", lambda e: e.tensor_tensor(out=g_[:], in0=g_[:], in1=negA[:, 4 * d:4 * d + 4], op=ALU.mult), [g_, negA], [g_])
            S.op("act", lambda e: e.activation(out=beta[:], in_=db_[:, 8 + 4 * d:12 + 4 * d], func=AF.Exp, scale=-1.0), [db_], [beta])
            S.op("dve", lambda e: e.tensor_scalar(out=beta[:], in0=beta[:], scalar1=1.0, scalar2=None, op0=ALU.add), [beta], [beta])
            S.op("dve", lambda e: e.reciprocal(out=beta[:], in_=beta[:]), [beta], [beta])
            yield
            pg = S.bank()
            S.op("pe", lambda e: e.matmul(pg[:, 0:4], lhsT=Ud(d), rhs=g_[:], start=True, stop=True), [cstf, g_], [pg])
            S.op("pe", lambda e: e.matmul(pg[:, 4:8], lhsT=ONESf(), rhs=g_[:], start=True, stop=True), [cstf, g_], [pg])
            S.op("dve", lambda e: e.tensor_copy(out=sm[:, 0:2, :], in_=pg[:, 0:8].rearrange("p (a h) -> p a h", h=4)), [pg], [sm])
            yield
            gc = sm[:, 0, :]; glast = sm[:, 1, :]
            S.op("dve", lambda e: e.tensor_copy(out=m["Gb"][:], in_=bc(g_[:].unsqueeze(2), [128, 4, 128])), [g_], [m["Gb"]])
            pr_ = S.bank()
            for h in range(4):
                S.op("pe", lambda e, h=h: e.matmul(v4(pr_)[:, h, :], lhsT=m["Gb"][:, h, :], rhs=Ud(d), start=True, stop=True), [m["Gb"], cstf], [pr_])
            gcb = bc(gc.unsqueeze(2), [128, 4, 128])
            X1 = m["X1"]; X2 = m["X2"]; EG = m["EG"]
            S.op("dve", lambda e: e.tensor_tensor(out=X1[:], in0=bc(NEGL(d).unsqueeze(1), [128, 4, 128]), in1=v4(pr_), op=ALU.subtract), [cstf, pr_], [X1])
            S.op("dve", lambda e: e.tensor_tensor(out=X1[:], in0=X1[:], in1=gcb, op=ALU.add), [X1, sm], [X1])
            S.op("dve", lambda e: e.tensor_tensor(out=X2[:], in0=v4(pr_), in1=bc(NEGA(d).unsqueeze(1), [128, 4, 128]), op=ALU.add), [cstf, pr_], [X2])
            S.op("dve", lambda e: e.tensor_tensor(out=X2[:], in0=X2[:], in1=gcb, op=ALU.subtract), [X2, sm], [X2])
            yield
            S.op("act", lambda e: e.activation(out=EG[:], in_=v4(pr_), func=AF.Exp), [pr_], [EG])
            S.op("act", lambda e: e.activation(out=X1[:], in_=X1[:], func=AF.Exp), [X1], [X1])
            S.op("act", lambda e: e.activation(out=X2[:], in_=X2[:], func=AF.Exp), [X2], [X2])
            yield
            S.op("act", lambda e: e.activation(out=sm[:, 2, :], in_=sm[:, 0, :], func=AF.Exp), [sm], [sm])
            S.op("dve", lambda e: e.tensor_tensor(out=sm[:, 3, :], in0=sm[:, 2, :], in1=beta[:], op=ALU.mult), [sm, beta], [sm])
            S.op("dve", lambda e: e.tensor_tensor(out=sm[:, 4, :], in0=sm[:, 1, :], in1=sm[:, 0, :], op=ALU.subtract), [sm], [sm])
            S.op("act", lambda e: e.activation(out=sm[:, 4, :], in_=sm[:, 4, :], func=AF.Exp), [sm], [sm])
            S.op("act", lambda e: e.activation(out=sm[:, 5, :], in_=sm[:, 1, :], func=AF.Exp), [sm], [sm])
            yield
            pkk = S.bank(); pkq = S.bank()
            for h in range(4):
                S.op("pe", lambda e, h=h: e.matmul(v4(pkk)[:, h, :], lhsT=kT(h), rhs=kT(h), start=True, stop=True), [qk], [pkk])
            for h in range(4):
                S.op("pe", lambda e, h=h: e.matmul(v4(pkq)[:, h, :], lhsT=kT(h), rhs=qT(h), start=True, stop=True), [qk], [pkq])
            L0 = m["L"][0]; tf = m["tf"]
            S.op("dve", lambda e: e.tensor_tensor(out=tf[:], in0=v4(pkk), in1=X1[:], op=ALU.mult), [pkk, X1], [tf])
            S.op("dve", lambda e: e.tensor_tensor(out=L0[:], in0=tf[:], in1=bc(beta[:].unsqueeze(2), [128, 4, 128]), op=ALU.mult), [tf, beta], [L0])
            S.op("dve", lambda e: e.tensor_tensor(out=m["aT"][:], in0=v4(pkq), in1=X2[:], op=ALU.mult), [pkq, X2], [m["aT"]])
            yield
            S.op("pool", lambda e: e.tensor_tensor(out=m["qgT"][:], in0=qk[:, 0:4, :], in1=EG[:], op=ALU.mult), [qk, EG], [m["qgT"]])
            S.op("pool", lambda e: e.tensor_tensor(out=m["vb"][:], in0=qk[:, 12:16, :], in1=bc(beta[:].unsqueeze(2), [128, 4, 128]), op=ALU.mult), [qk, beta], [m["vb"]])
            S.op("pool", lambda e: e.tensor_tensor(out=m["kbg"][:], in0=qk[:, 8:12, :], in1=bc(sm[:, 3, :].unsqueeze(2), [128, 4, 128]), op=ALU.mult), [qk, sm], [m["kbg"]])
            S.op("pool", lambda e: e.tensor_tensor(out=m["kd"][:], in0=qk[:, 8:12, :], in1=bc(sm[:, 4, :].unsqueeze(2), [128, 4, 128]), op=ALU.mult), [qk, sm], [m["kd"]])
            yield
            LT0 = m["LT"][0]; TT0 = m["TT"][0]
            pl = S.bank()
            for h in range(4):
                S.op("pe", lambda e, h=h: e.matmul(v4(pl)[:, h, :], lhsT=L0[:, h, :], rhs=IDf(), start=True, stop=True), [L0, cstf], [pl])
            S.op("act", lambda e: e.activation(out=LT0[:], in_=v4(pl), func=AF.Copy), [pl], [LT0])
            S.op("dve", lambda e: e.tensor_tensor(out=TT0[:], in0=bc(IDf().unsqueeze(1), [128, 4, 128]), in1=v4(pl), op=ALU.subtract), [cstf, pl], [TT0])
            yield
            Lc, LTc, TTc = L0, LT0, TT0
            for lev in range(6):
                Ln_ = m["L"][(lev + 1) % 2]; LTn = m["LT"][(lev + 1) % 2]; TTn = m["TT"][(lev + 1) % 2]
                pp = S.bank()
                for h in range(4):
                    S.op("pe", lambda e, h=h, pp=pp, Lc=Lc, LTc=LTc: e.matmul(v4(pp)[:, h, :], lhsT=LTc[:, h, :], rhs=Lc[:, h, :], start=True, stop=True), [Lc, LTc], [pp])
                if lev < 5:
                    pt = S.bank()
                    for h in range(4):
                        S.op("pe", lambda e, h=h, pt=pt, Lc=Lc, LTc=LTc: e.matmul(v4(pt)[:, h, :], lhsT=Lc[:, h, :], rhs=LTc[:, h, :], start=True, stop=True), [Lc, LTc], [pt])
                yield
                S.op("act", lambda e, pp=pp, Ln_=Ln_: e.activation(out=Ln_[:], in_=v4(pp), func=AF.Copy), [pp], [Ln_])
                if lev < 5:
                    S.op("dve", lambda e, pt=pt, LTn=LTn: e.tensor_copy(out=LTn[:], in_=v4(pt)), [pt], [LTn])
                    yield
                pu = S.bank()
                for h in range(4):
                    S.op("pe", lambda e, h=h, pu=pu, Ln_=Ln_, TTc=TTc: e.matmul(v4(pu)[:, h, :], lhsT=Ln_[:, h, :], rhs=TTc[:, h, :], start=True, stop=True), [Ln_, TTc], [pu])
                S.op("dve", lambda e, pu=pu, TTc=TTc, TTn=TTn: e.tensor_tensor(out=TTn[:], in0=v4(pu), in1=TTc[:], op=ALU.add), [pu, TTc], [TTn])
                yield
                Lc, LTc, TTc = Ln_, LTn, TTn
            TTb = m["TTb"]
            S.op("act", lambda e, TTc=TTc: e.activation(out=TTb[:], in_=TTc[:], func=AF.Copy), [TTc], [TTb])
            yield
            pw = S.bank()
            for h in range(4):
                S.op("pe", lambda e, h=h: e.matmul(v4(pw)[:, h, :], lhsT=m["kbg"][:, h, :], rhs=TTb[:, h, :], start=True, stop=True), [m["kbg"], TTb], [pw])
            S.op("act", lambda e: e.activation(out=m["nwT"][:], in_=v4(pw), func=AF.Identity, scale=-1.0), [pw], [m["nwT"]])
            return TTb

        def scan_step(m, TT):
            Sf = m["S"]; Sb_ = m["Sb"]; vnew = m["vnew"]; sm = m["sm"]
            pv = S.bank()
            for h in range(4):
                S.op("pe", lambda e, h=h: e.matmul(v4(pv)[:, h, :], lhsT=TT[:, h, :], rhs=m["vb"][:, h, :], start=True, stop=False), [TT, m["vb"]], [pv])
                S.op("pe", lambda e, h=h: e.matmul(v4(pv)[:, h, :], lhsT=m["nwT"][:, h, :], rhs=Sb_[:, h, :], start=False, stop=True), [m["nwT"], Sb_], [pv])
            S.op("act", lambda e: e.activation(out=vnew[:], in_=v4(pv), func=AF.Copy), [pv], [vnew])
            po = S.bank()
            for h in range(4):
                S.op("pe", lambda e, h=h: e.matmul(v4(po)[:, h, :], lhsT=m["qgT"][:, h, :], rhs=Sb_[:, h, :], start=True, stop=False), [m["qgT"], Sb_], [po])
                S.op("pe", lambda e, h=h: e.matmul(v4(po)[:, h, :], lhsT=m["aT"][:, h, :], rhs=vnew[:, h, :], start=False, stop=True), [m["aT"], vnew], [po])
            pu = S.bank()
            for h in range(4):
                S.op("pe", lambda e, h=h: e.matmul(v4(pu)[:, h, :], lhsT=m["kd"][:, h, :], rhs=vnew[:, h, :], start=True, stop=True), [m["kd"], vnew], [pu])
            S.op("dve", lambda e: e.tensor_tensor(out=Sf[:], in0=Sf[:], in1=bc(sm[:, 5, :].unsqueeze(2), [128, 4, 128]), op=ALU.mult), [Sf, sm], [Sf])
            S.op("dve", lambda e: e.tensor_tensor(out=Sf[:], in0=Sf[:], in1=v4(pu), op=ALU.add), [Sf, pu], [Sf])
            S.op("act", lambda e: e.activation(out=Sb_[:], in_=Sf[:], func=AF.Copy), [Sf], [Sb_])
            return po

        def init_state(m, g, s0ap):
            if g == "s":
                S.dma("sp", m["S"][:], s0ap.rearrange("h k v -> k h v"), writes=[m["S"]])
            else:
                S.op("pool", lambda e: e.memset(m["S"][:], 0.0), [], [m["S"]])
            S.op("act", lambda e: e.activation(out=m["Sb"][:], in_=m["S"][:], func=AF.Copy), [m["S"]], [m["Sb"]])

        def run(gen):
            try:
                while True:
                    next(gen)
            except StopIteration as e_:
                return e_.value

        def interleave(gens):
            vals = [None] * len(gens)
            live = list(range(len(gens)))
            while live:
                for i in list(live):
                    try:
                        next(gens[i])
                    except StopIteration as e_:
                        vals[i] = e_.value
                        live.remove(i)
            return vals

        seqs = DBG_SEQS or ([("p", s) for s in range(4)] + [("s", 0)])

        with ExitStack() as ms:
          if STAGES >= 2:
            mA = mixer_alloc(ms)
            mB = mixer_alloc(ms)
            mB["S"] = mA["S"]; mB["Sb"] = mA["Sb"]
            ofo = [S.sb([128, 512], F32, "ofo", ms) for _ in range(2)]
            ring_save = S.ring
            S.ring = ring_save + [accN, accD]
            cc = 0

            def prep_both(m_, gd, s, c):
                yield from conv_prep(m_, gd, s, c)
                S.dma("pool", gd["QA"][(s * gd["T"] + c * 128) // 128], m_["qk"][:], reads=[m_["qk"]])
                TT_ = yield from chunk_prep(m_, gd, s, c, 0)
                return TT_

            for (g, s) in seqs:
                gd = G[g]
                init_state(mA, g, sf0)
                nch = min(gd["T"] // 128, DBG_NCH)
                for c0 in range(0, nch, 2):
                    ctxs = [(mA, c0)] + ([(mB, c0 + 1)] if c0 + 1 < nch else [])
                    TTs = interleave([prep_both(m_, gd, s, c) for (m_, c) in ctxs])
                    for (m_, c), TT in zip(ctxs, TTs):
                        n0 = s * gd["T"] + c * 128
                        po = scan_step(m_, TT)
                        oo = ofo[cc % 2]; cc += 1
                        S.op("act", lambda e, oo=oo, po=po: e.activation(out=oo[:], in_=po[:], func=AF.Copy), [po], [oo])
                        S.dma("pool", gd["of"][n0:n0 + 128, :], oo[:], reads=[oo])
                if g == "p":
                    finals.append(S.dma("pool", nsf[s].rearrange("h k v -> k h v"), mA["S"][:], reads=[mA["S"]]))
            S.ring = ring_save
            S.barrier()

        with ExitStack() as ms:
          if STAGES >= 3:
            m = mixer_alloc(ms)
            woa = S.sb([128, 4, D], BF16, "woa", ms)
            wob = S.sb([64, 8, D], BF16, "wob", ms)
            wout = S.sb([128, 8, D], BF16, "wout", ms)
            S.dma("pool", woa[:], w_oa.rearrange("(kc p) d -> p kc d", p=128), writes=[woa])
            S.dma("pool", wob[:], w_ob.rearrange("(h p) d -> p h d", p=64), writes=[wob])
            S.dma("pool", wout[:], w_out.rearrange("(kc p) d -> p kc d", p=128), writes=[wout])
            g2bc = [S.sb([128, D], F32, "g2bc", ms) for _ in range(2)]
            for v in range(2):
                S.dma("sp", g2bc[v][:], modrow[v, 1].partition_broadcast(128), writes=[g2bc[v]])
            ctxKT = S.sb([128, 2, 2, 128], BF16, "ctxKT", ms)
            ctxV = S.sb([128, 2, 128], BF16, "ctxV", ms)
            ckf = S.sb([128, 2, 128], F32, "ckf", ms)
            cvf = S.sb([128, 2, 128], F32, "cvf", ms)
            ckd = S.sb([128, 2, 2, 2, 64], BF16, "ckd", ms)
            S.dma("sp", ckf[:], ck.rearrange("(b p) f -> p b f", p=128), writes=[ckf])
            S.dma("sp", cvf[:], cv.rearrange("(b p) f -> p b f", p=128), writes=[cvf])
            S.op("dve", lambda e: e.tensor_copy(out=ctxV[:], in_=cvf[:]), [cvf], [ctxV])
            for dup in range(2):
                S.op("dve", lambda e, dup=dup: e.tensor_copy(out=ckd[:, :, :, dup, :], in_=ckf[:].rearrange("p b (k d) -> p b k d", d=64)), [ckf], [ckd])
            for b in range(2):
                for kv in range(2):
                    S.op("pe", lambda e, b=b, kv=kv: e.transpose(out=tb[:, b * 2 + kv, :], in_=ckd[:, b, kv].rearrange("p a d -> p (a d)"), identity=identb[:]),
                         [ckd, identb], [tb])
            S.op("act", lambda e: e.activation(out=ctxKT[:].rearrange("p b k n -> p (b k) n"), in_=tb[:, 0:4, :], func=AF.Copy), [tb], [ctxKT])

            ofl = S.sb([128, 4, 128], F32, "ofl", ms)
            zsl = S.sb([128, 4, 128], F32, "zsl", ms)
            ot = S.sb([128, 4, 128], F32, "ot", ms)
            junk = S.sb([128, 128], BF16, "junk", ms)
            ya = S.sb([128, 4, 128], BF16, "ya", ms)
            yaT = S.sb([128, 4, 128], BF16, "yaT", ms)
            QTlo = S.sb([128, 4, 128], BF16, "QTlo", ms)
            QThi = S.sb([128, 4, 128], BF16, "QThi", ms)
            S.op("pool", lambda e: e.memset(QTlo[:], 0.0), [], [QTlo])
            S.op("pool", lambda e: e.memset(QThi[:], 0.0), [], [QThi])
            KTt = [S.sb([128, 2, 128], BF16, "KTt", ms) for _ in range(2)]
            Vtt = [S.sb([128, 128], BF16, "Vtt", ms) for _ in range(2)]
            PTb = [S.sb([128, 4, 128], BF16, "PTb", ms) for _ in range(2)]
            den = S.sb([64, 4, 128], F32, "den", ms)
            ybT = S.sb([64, 8, 128], BF16, "ybT", ms)
            gTl = S.sb([128, 16, 128], BF16, "gTl", ms)
            t1 = S.sb([128, 4, 128], F32, "t1", ms)
            t2 = S.sb([128, 4, 128], F32, "t2", ms)
            mT = S.sb([128, 8, 128], BF16, "mT", ms)
            x1t = S.sb([128, D], F32, "x1t", ms)
            ytmp = S.sb([128, 512], F32, "ytmp", ms)
            kcnt = 0
            for (g, s) in seqs:
                gd = G[g]; v = gd["v"]; T = gd["T"]; nch = T // 128
                init_state(m, g, sb0)
                for c in range(nch - 1, -1, -1):
                    n0 = s * T + c * 128
                    S.dma("sp", m["qk"][:], gd["QA"][n0 // 128], writes=[m["qk"]])
                    TT = run(chunk_prep(m, gd, s, c, 1))
                    po = scan_step(m, TT)
                    S.dma("sp", ofl[:], gd["of"][n0:n0 + 128, :].rearrange("p (h d) -> p h d", h=4), writes=[ofl])
                    S.dma("sp", zsl[:], gd["zs"][n0:n0 + 128, :].rearrange("p (h d) -> p h d", h=4), writes=[zsl])
                    S.op("dve", lambda e, po=po: e.tensor_tensor(out=ot[:], in0=v4(po), in1=ofl[:], op=ALU.add), [po, ofl], [ot])
                    S.op("pool", lambda e: e.memset(ss[:, 0:4], 0.0), [], [ss])
                    for h in range(4):
                        S.op("act", lambda e, h=h: e.activation(out=junk[:], in_=ot[:, h, :], func=AF.Square, accum_out=ss[:, h:h + 1]), [ot], [junk, ss])
                    rstd_from_ss(slice(0, 4), 1.0 / 128)
                    S.op("dve", lambda e: e.tensor_tensor(out=ot[:], in0=ot[:], in1=bc(rs[:, 0:4].unsqueeze(2), [128, 4, 128]), op=ALU.mult), [ot, rs], [ot])
                    S.op("pool", lambda e: e.tensor_tensor(out=ot[:], in0=ot[:], in1=bc(onbc[:].unsqueeze(1), [128, 4, 128]), op=ALU.mult), [ot, onbc], [ot])
                    S.op("dve", lambda e: e.tensor_tensor(out=ya[:], in0=ot[:], in1=zsl[:], op=ALU.mult), [ot, zsl], [ya])
                    for h in range(4):
                        S.op("pe", lambda e, h=h: e.transpose(out=tb[:, h, :], in_=ya[:, h, :], identity=identb[:]), [ya, identb], [tb])
                    S.op("act", lambda e: e.activation(out=yaT[:], in_=tb[:, 0:4, :], func=AF.Copy), [tb], [yaT])
                    if CSUB < 2:
                        continue
                    qsrc = gd["QT"].rearrange("c p n -> p c n")
                    S.dma("sp", QTlo[0:64], qsrc[0:64, :, n0:n0 + 128], writes=[QTlo])
                    S.dma("sp", QThi[64:128], qsrc[64:128, :, n0:n0 + 128], writes=[QThi])
                    blocks = []
                    if g == "s":
                        if c > 0:
                            blocks.append(("loc", c - 1, mprev))
                        blocks.append(("loc", c, None))
                        if c < nch - 1:
                            blocks.append(("loc", c + 1, mnext))
                        blocks += [("ctx", 0, None), ("ctx", 1, None)]
                    else:
                        blocks = [("loc", 0, None), ("loc", 1, None)]
                    for gk in range(2):
                        for bi, (kind, idx, mask) in enumerate(blocks):
                            if kind == "loc":
                                kb0 = s * T + idx * 128
                                if gk == 0 or True:
                                    KT_ = KTt[kcnt % 2]; V_ = Vtt[kcnt % 2]; kcnt += 1
                                    S.dma("sp", KT_[:], gd["KT"].rearrange("c p n -> p c n")[:, :, kb0:kb0 + 128], writes=[KT_])
                                    S.dma("sp", V_[:], gd["Vt"][kb0:kb0 + 128, :], writes=[V_])
                                kfull = KT_[:, gk, :]; vv = V_[:, gk * 64:(gk + 1) * 64]
                                kres = [KT_]; vres = [V_]
                            else:
                                kfull = ctxKT[:, idx, gk, :]; vv = ctxV[:, idx, gk * 64:(gk + 1) * 64]
                                kres = [ctxKT]; vres = [ctxV]
                            pst_ = S.bank()
                            stv = pst_[:].rearrange("p (a b i) -> p a b i", a=2, b=2)
                            for a_ in range(2):
                                S.op("pe", lambda e, kfull=kfull, stv=stv, gk=gk, a_=a_: e.matmul(stv[:, a_, 0, :], lhsT=kfull, rhs=QTlo[:, 2 * gk + a_, :], start=True, stop=True),
                                     kres + [QTlo], [pst_])
                                S.op("pe", lambda e, kfull=kfull, stv=stv, gk=gk, a_=a_: e.matmul(stv[:, a_, 1, :], lhsT=kfull, rhs=QThi[:, 2 * gk + a_, :], start=True, stop=True),
                                     kres + [QThi], [pst_])
                            P_ = PTb[(gk * 8 + bi) % 2]
                            S.op("act", lambda e, P_=P_, pst_=pst_: e.activation(out=P_[:], in_=v4(pst_), func=AF.Exp, scale=0.125), [pst_], [P_])
                            if mask is not None:
                                S.op("pool", lambda e, P_=P_, mask=mask: e.tensor_tensor(out=P_[:], in0=P_[:], in1=bc(mask[:].unsqueeze(1), [128, 4, 128]), op=ALU.mult),
                                     [P_, mask], [P_])
                            first = (bi == 0); last = (bi == len(blocks) - 1)
                            S.op("pe", lambda e, P_=P_, vv=vv, first=first, last=last: e.matmul(accN[0:64, :], lhsT=vv, rhs=P_[:].rearrange("p h i -> p (h i)"),
                                                                                              start=first, stop=last), vres + [P_], [accN])
                            S.op("pe", lambda e, P_=P_, first=first, last=last: e.matmul(accD[0:64, :], lhsT=onesb[:, 0:64], rhs=P_[:].rearrange("p h i -> p (h i)"),
                                                                                       start=first, stop=last), [onesb, P_], [accD])
                        S.op("dve", lambda e, gk=gk: e.tensor_tensor(out=den[:], in0=accD[0:64, :].rearrange("p (h i) -> p h i", h=4),
                                                                    in1=bc(sexp[0:64, 4 * gk:4 * gk + 4].unsqueeze(2), [64, 4, 128]), op=ALU.add), [accD, sexp], [den])
                        S.op("dve", lambda e: e.reciprocal(out=den[:], in_=den[:]), [den], [den])
                        S.op("dve", lambda e, gk=gk: e.tensor_tensor(out=ybT[:, 4 * gk:4 * gk + 4, :], in0=accN[0:64, :].rearrange("p (h i) -> p h i", h=4),
                                                                    in1=den[:], op=ALU.mult), [accN, den], [ybT])
                    if CSUB < 3:
                        continue
                    S.dma("sp", gTl[:], gd["gT"].rearrange("c p n -> p c n")[:, :, n0:n0 + 128], writes=[gTl])
                    S.dma("sp", x1t[:], gd["x1"][n0:n0 + 128, :], writes=[x1t])
                    for hf in range(2):
                        pa = S.bank(); pb_ = S.bank()
                        for dj4 in range(4):
                            dj = hf * 4 + dj4
                            for kc in range(4):
                                S.op("pe", lambda e, pa=pa, dj=dj, dj4=dj4, kc=kc: e.matmul(v4(pa)[:, dj4, :], lhsT=woa[:, kc, dj * 128:(dj + 1) * 128], rhs=yaT[:, kc, :],
                                                                                         start=(kc == 0), stop=(kc == 3)), [woa, yaT], [pa])
                            for h in range(8):
                                S.op("pe", lambda e, pb_=pb_, dj=dj, dj4=dj4, h=h: e.matmul(v4(pb_)[:, dj4, :], lhsT=wob[0:64, h, dj * 128:(dj + 1) * 128], rhs=ybT[0:64, h, :],
                                                                                         start=(h == 0), stop=(h == 7)), [wob, ybT], [pb_])
                        S.op("dve", lambda e, pa=pa, hf=hf: e.tensor_tensor(out=t1[:], in0=v4(pa), in1=gTl[:, hf * 4:hf * 4 + 4, :], op=ALU.mult), [pa, gTl], [t1])
                        S.op("dve", lambda e, pb_=pb_, hf=hf: e.tensor_tensor(out=t2[:], in0=v4(pb_), in1=gTl[:, 8 + hf * 4:12 + hf * 4, :], op=ALU.mult), [pb_, gTl], [t2])
                        S.op("pool", lambda e, hf=hf: e.tensor_tensor(out=mT[:, hf * 4:hf * 4 + 4, :], in0=t1[:], in1=t2[:], op=ALU.add), [t1, t2], [mT])
                    for hf in range(2):
                        py = S.bank()
                        for kc in range(8):
                            S.op("pe", lambda e, py=py, kc=kc, hf=hf: e.matmul(py[:], lhsT=mT[:, kc, :], rhs=wout[:, kc, hf * 512:(hf + 1) * 512], start=(kc == 0), stop=(kc == 7)),
                                 [mT, wout], [py])
                        S.op("dve", lambda e, py=py, hf=hf, v=v: e.tensor_tensor(out=ytmp[:], in0=py[:], in1=g2bc[v][:, hf * 512:(hf + 1) * 512], op=ALU.mult), [py, g2bc[v]], [ytmp])
                        S.op("pool", lambda e, hf=hf: e.tensor_tensor(out=x1t[:, hf * 512:(hf + 1) * 512], in0=x1t[:, hf * 512:(hf + 1) * 512], in1=ytmp[:], op=ALU.add), [x1t, ytmp], [x1t])
                    S.dma("pool", gd["x1"][n0:n0 + 128, :], x1t[:], reads=[x1t])
                if g == "p":
                    finals.append(S.dma("pool", nsb[s].rearrange("h k v -> k h v"), m["S"][:], reads=[m["S"]]))
            S.barrier()

        def allocC(fs):
            ex = {}
            ex["nfbc"] = S.sb([128, D], F32, "nfbc", fs)
            S.dma("sp", ex["nfbc"][:], norm_final.partition_broadcast(128), writes=[ex["nfbc"]])
            ex["yo"] = [S.sb([128, D], F32, "yo", fs) for _ in range(2)]
            ex["cnt"] = 0
            return ex

        def tailC(tl, xt, xn, hT, ex):
            for sub in range(4):
                x = xt[sub]
                S.op("pool", lambda e: e.memset(ss[:, 0:1], 0.0), [], [ss])
                S.op("act", lambda e, x=x: e.activation(out=xn[:], in_=x[:], func=AF.Square, accum_out=ss[:, 0:1]), [x], [xn, ss])
                rstd_from_ss(slice(0, 1), 1.0 / D)
                yo = ex["yo"][ex["cnt"] % 2]; ex["cnt"] += 1
                S.op("dve", lambda e, x=x, yo=yo: e.scalar_tensor_tensor(out=yo[:], in0=x[:], scalar=rs[:, 0:1], in1=ex["nfbc"][:], op0=ALU.mult, op1=ALU.mult),
                     [x, rs, ex["nfbc"]], [yo])
                r0 = tl["n0"] + sub * 128
                finals.append(S.dma("pool", yout[tl["g"]][r0:r0 + 128, :], yo[:], reads=[yo]))

        tilesC = []
        for i in range(2):
            tilesC.append(dict(g="p", n0=i * 512, v=0, src=G["p"]["x1"][i * 512:(i + 1) * 512, :]))
        for i in range(8):
            tilesC.append(dict(g="s", n0=i * 512, v=1, src=G["s"]["x1"][i * 512:(i + 1) * 512, :]))
        if STAGES >= 4:
            ffn_stage(1, 2, 2, tilesC, tailC, allocC)

        S.emit(block, final_waits=finals)
    return nc


def _consts():
    i = np.arange(128)
    p = i[:, None]; f = i[None, :]
    c = np.zeros((128, 13, 128), np.float32)
    c[:, 0] = (p == f)
    c[:, 1] = (p <= f)
    c[:, 2] = (p >= f)
    c[:, 3] = np.where(f < p, 0.0, NEG)
    c[:, 4] = np.where(f > p, 0.0, NEG)
    c[:, 5] = np.where(p <= f, 0.0, NEG)
    c[:, 6] = np.where(p >= f, 0.0, NEG)
    c[:, 7] = 1.0
    c[:, 8] = (p >= f)
    c[:, 9] = (p <= f)
    dm = f % 64
    c[:, 10] = np.where((dm < 32) & (p == f + 32), -1.0, 0.0) + np.where((dm >= 32) & (p == f - 32), 1.0, 0.0)
    c[:, 11] = (p == (f % 64))
    c[:, 12] = (p == 64 + (f % 64))
    rows = 4096 // 64
    row = np.repeat(np.arange(rows, dtype=np.float32), 64)
    col = np.tile(np.arange(64, dtype=np.float32), rows)
    inv = np.power(np.float32(10000.0), -np.arange(16, dtype=np.float32) / np.float32(16)).astype(np.float32)
    ang = np.concatenate([row[:, None] * inv, col[:, None] * inv], axis=-1).astype(np.float32)
    cs = np.concatenate([np.cos(ang), np.sin(ang)], axis=-1).astype(np.float32)
    mi = (np.arange(128) % 64) % 32
    csT = np.stack([np.cos(ang)[:, mi].T, np.sin(ang)[:, mi].T], 0).astype(np.float32)
    return c, cs, np.ascontiguousarray(csT)


_NC_CACHE = {}


def kernel(x_prompt, x_sample, state_delta_fwd, state_delta_bwd, cache_k, cache_v, c, c_ctx,
           ada_w, ada_b, norm_ffn1, ffn1_w13, ffn1_w2, norm_mix, w_in, conv_w, a_log, dt_bias,
           onorm_a, w_oa, w_ob, w_out, sink, norm_ffn2, ffn2_w13, ffn2_w2, norm_final):
    A = lambda a: np.ascontiguousarray(np.asarray(a, dtype=np.float32))
    if "nc" not in _NC_CACHE:
        _NC_CACHE["nc"] = build_program()
    nc = _NC_CACHE["nc"]
    cst, cs, csT = _consts()
    shared = {
        "ada_w": A(ada_w)[0], "ada_b": A(ada_b).reshape(1, -1), "norm_ffn1": A(norm_ffn1).reshape(1, -1),
        "ffn1_w13": A(ffn1_w13)[0], "ffn1_w2": A(ffn1_w2)[0], "norm_mix": A(norm_mix).reshape(1, -1),
        "w_in": A(w_in)[0], "conv_w": A(conv_w)[0], "a_log": A(a_log).reshape(1, 8), "dt_bias": A(dt_bias).reshape(1, 8),
        "onorm_a": A(onorm_a).reshape(1, 128), "w_oa": A(w_oa)[0], "w_ob": A(w_ob)[0], "w_out": A(w_out)[0],
        "sink": A(sink).reshape(1, 8), "norm_ffn2": A(norm_ffn2).reshape(1, -1), "ffn2_w13": A(ffn2_w13)[0],
        "ffn2_w2": A(ffn2_w2)[0], "norm_final": A(norm_final).reshape(1, -1), "cst": cst, "cs": cs, "csT": csT,
    }
    xp = A(x_prompt); xs = A(x_sample)
    in_maps = []
    for k in range(8):
        d = dict(shared)
        d["xp"] = xp[4 * k:4 * k + 4].reshape(1024, D)
        d["xs"] = xs[k]
        d["sf0"] = A(state_delta_fwd)[k, 0]
        d["sb0"] = A(state_delta_bwd)[k, 0]
        d["ck"] = A(cache_k)[k, 0].reshape(256, 128)
        d["cv"] = A(cache_v)[k, 0].reshape(256, 128)
        d["cvec"] = np.ascontiguousarray(np.stack([A(c_ctx), A(c)[k]], axis=0))
        in_maps.append(d)
    res = run_bass_kernel_spmd(nc, in_maps, core_ids=list(range(8)))
    R = res.results
    if DEBUG:
        _NC_CACHE["dbg"] = R
    y_prompt = np.concatenate([R[k]["yp"].reshape(4, 256, D) for k in range(8)], axis=0)
    y_sample = np.stack([R[k]["ys"] for k in range(8)], axis=0)
    nsf = np.concatenate([R[k]["nsf"] for k in range(8)], axis=0)[:, None]
    nsb = np.concatenate([R[k]["nsb"] for k in range(8)], axis=0)[:, None]
    nck = np.concatenate([R[k]["nck"].reshape(4, 256, 2, 64) for k in range(8)], axis=0)[:, None]
    ncv = np.concatenate([R[k]["ncv"].reshape(4, 256, 2, 64) for k in range(8)], axis=0)[:, None]
    return (y_prompt.astype(np.float32), y_sample.astype(np.float32), nsf.astype(np.float32), nsb.astype(np.float32),
            nck.astype(np.float32), ncv.astype(np.float32))
```

```python
import numpy as np
from contextlib import ExitStack
import concourse.bass as bass
import concourse.mybir as mybir
from concourse.bass_utils import run_bass_kernel_spmd

F32 = mybir.dt.float32
BF16 = mybir.dt.bfloat16
AF = mybir.ActivationFunctionType
ALU = mybir.AluOpType

D = 1024
DFF = 2816
NFC = 22
EPS = 1e-6
NEG = -30000.0
DEBUG = False
STAGES = 99
NT_A = 10
SUB = 99
SUB2 = 99
NOASSERT = False
SKIPDB = False
PAD = 0
DBG_SEQS = None
DBG_NCH = 999
CSUB = 99


class Res:
    __slots__ = ("name", "lw", "rd")

    def __init__(self, name=""):
        self.name = name
        self.lw = []
        self.rd = []


class Buf(Res):
    __slots__ = ("t", "psum")

    def __init__(self, t, name="", psum=False):
        Res.__init__(self, name)
        self.t = t
        self.psum = psum

    def __getitem__(self, k):
        return self.t[k]


class Op:
    __slots__ = ("eng", "fn", "deps", "sig", "dma", "semv", "idx")


class Sched:
    ENG = ("pe", "act", "dve", "pool", "sp")
    DQ = ("sp", "pool")
    NDSEM = 8

    def __init__(self, nc, stack):
        self.nc = nc
        self.stack = stack
        self.ops = {e: [] for e in self.ENG}
        self.esem = {e: stack.enter_context(nc.semaphore("s_" + e)) for e in self.ENG}
        self.dsem = {e: [stack.enter_context(nc.semaphore("d_%s%d" % (e, i))) for i in range(self.NDSEM)]
                     for e in self.DQ}
        self.ndma = {e: 0 for e in self.DQ}
        self.dmatok = {e: [] for e in self.DQ}
        self.nbuf = 0
        self.pending = {e: [] for e in self.ENG}
        self.ring = []
        self.ringi = 0

    def sb(self, shape, dt, name=None, stack=None):
        self.nbuf += 1
        name = "%s_%d" % (name or "sb", self.nbuf)
        return Buf((stack or self.stack).enter_context(self.nc.sbuf_tensor(name, list(shape), dt)), name)

    def ps(self, shape, dt, name=None):
        self.nbuf += 1
        name = "%s_%d" % (name or "ps", self.nbuf)
        return Buf(self.stack.enter_context(self.nc.psum_tensor(name, list(shape), dt)), name, psum=True)

    def bank(self):
        b = self.ring[self.ringi % len(self.ring)]
        self.ringi += 1
        assert NOASSERT or (not b.lw) or b.rd, "ring bank reused before consumption: " + b.name
        return b

    def _rec(self, o, reads, writes, addw=False):
        deps = list(self.pending[o.eng])
        self.pending[o.eng] = []
        for r in reads:
            deps.extend(r.lw)
            if getattr(r, "psum", False):
                deps.extend([x for x in r.rd if x.eng != o.eng])
        for w in writes:
            deps.extend(w.lw)
            deps.extend(w.rd)
        if o.eng == "pe" and not o.dma:
            deps = [d for d in deps if d.dma or d.eng != "pe"]
        o.deps = deps
        o.idx = len(self.ops[o.eng])
        self.ops[o.eng].append(o)
        for r in reads:
            r.rd.append(o)
        for w in writes:
            if addw:
                w.lw = w.lw + [o]
            else:
                w.lw = [o]
                w.rd = []

    def op(self, eng, fn, reads=(), writes=()):
        o = Op()
        o.eng = eng; o.fn = fn; o.sig = False; o.dma = False; o.semv = None
        self._rec(o, reads, writes)
        return o

    def dma(self, q, out, in_, reads=(), writes=(), addw=False, **kw):
        o = Op()
        o.eng = q; o.dma = True; o.sig = True
        o.fn = lambda e: e.dma_start(out=out, in_=in_, **kw)
        self._rec(o, reads, writes, addw)
        n = self.ndma[q]
        self.ndma[q] += 1
        o.semv = (self.dsem[q][n % self.NDSEM], 16 * (n // self.NDSEM + 1))
        if n >= self.NDSEM:
            o.deps.append(self.dmatok[q][n - self.NDSEM])
        self.dmatok[q].append(o)
        return o

    def barrier(self):
        toks = []
        for e in self.ENG:
            comp = [o for o in self.ops[e] if not o.dma]
            if comp:
                toks.append(comp[-1])
        for q in self.DQ:
            toks.extend(self.dmatok[q][-self.NDSEM:])
        for e in self.ENG:
            self.pending[e] = self.pending[e] + toks

    def emit(self, block, final_waits=()):
        for e in self.ENG:
            for o in self.ops[e]:
                for d in o.deps:
                    if not d.dma:
                        d.sig = True
        for e in self.ENG:
            c = 0
            for o in self.ops[e]:
                if not o.dma and o.sig:
                    c += 1
                    o.semv = (self.esem[e], c)
        engobj = {"pe": "tensor", "act": "scalar", "dve": "vector", "pool": "gpsimd", "sp": "sync"}

        def make(e):
            def body(E):
                waited = {}

                def wait(tok):
                    s, v = tok.semv
                    k = id(s)
                    if waited.get(k, 0) >= v:
                        return
                    waited[k] = v
                    E.wait_ge(s, v)
                for o in self.ops[e]:
                    for d in o.deps:
                        if d is not o:
                            wait(d)
                    ins = o.fn(E)
                    if o.sig:
                        s, v = o.semv
                        ins.then_inc(s, 16 if o.dma else 1)
                if e == "sp":
                    for o in final_waits:
                        wait(o)
            return body
        for e in self.ENG:
            getattr(block, engobj[e])(make(e))


def bc(ap, shape):
    return ap.broadcast_to(list(shape))


def build_program():
    nc = bass.Bass("TRN2", target_bir_lowering=False)
    SK = "ExternalOutput" if DEBUG else "Internal"

    def din(name, shape, dt=F32):
        return nc.dram_tensor(name, list(shape), dt, kind="ExternalInput").ap()

    def dout(name, shape, dt=F32):
        return nc.dram_tensor(name, list(shape), dt, kind="ExternalOutput").ap()

    def dscr(name, shape, dt=F32, dbg=False):
        return nc.dram_tensor(name, list(shape), dt, kind=(SK if dbg else "Internal")).ap()

    xin = {"p": din("xp", [1024, D]), "s": din("xs", [4096, D])}
    sf0 = din("sf0", [4, 128, 128]); sb0 = din("sb0", [4, 128, 128])
    ck = din("ck", [256, 128]); cv = din("cv", [256, 128])
    cvec = din("cvec", [2, D])
    ada_w = din("ada_w", [D, 9 * D]); ada_b = din("ada_b", [1, 9 * D])
    normw = [din("norm_ffn1", [1, D]), din("norm_mix", [1, D]), din("norm_ffn2", [1, D])]
    w13 = [din("ffn1_w13", [D, 2 * DFF]), din("ffn2_w13", [D, 2 * DFF])]
    w2 = [din("ffn1_w2", [DFF, D]), din("ffn2_w2", [DFF, D])]
    w_in = din("w_in", [D, 4880])
    conv_w = din("conv_w", [5, 1536])
    a_log = din("a_log", [1, 8]); dt_bias = din("dt_bias", [1, 8])
    onorm = din("onorm_a", [1, 128])
    w_oa = din("w_oa", [512, D]); w_ob = din("w_ob", [512, D]); w_out = din("w_out", [D, D])
    sink = din("sink", [1, 8])
    norm_final = din("norm_final", [1, D])
    cst = din("cst", [128, 13, 128])
    csT = din("csT", [2, 128, 4096])
    cs = din("cs", [4096, 64])

    yout = {"p": dout("yp", [1024, D]), "s": dout("ys", [4096, D])}
    nsf = dout("nsf", [4, 4, 128, 128]); nsb = dout("nsb", [4, 4, 128, 128])
    nck = dout("nck", [1024, 128]); ncv = dout("ncv", [1024, 128])

    w13s = [dscr("w13s%d" % f, [NFC, 128, 2, 8, 128], BF16) for f in range(2)]
    w2s = [dscr("w2s%d" % f, [128, NFC, D], BF16) for f in range(2)]
    wina = dscr("wina", [33, 128, 8, 128], BF16)
    winb = dscr("winb", [128, 8, 1296], BF16)
    modrow = dscr("modrow", [2, 3, D])
    G = {"p": dict(nseq=4, T=256, v=0), "s": dict(nseq=1, T=4096, v=1)}
    for g, gd in G.items():
        N = gd["nseq"] * gd["T"]
        gd["N"] = N
        gd["x1"] = dscr("x1_" + g, [N, D], dbg=True)
        gd["rawT"] = dscr("rawT_" + g, [12, 128, gd["nseq"], gd["T"]], dbg=True)
        gd["zs"] = dscr("zs_" + g, [N, 512], dbg=True)
        gd["db"] = dscr("db_" + g, [N, 128], dbg=True)
        gd["QT"] = dscr("QT_" + g, [4, 128, N], BF16)
        gd["KT"] = dscr("KT_" + g, [2, 128, N], BF16)
        gd["Vt"] = dscr("Vt_" + g, [N, 128], BF16)
        gd["gT"] = dscr("gT_" + g, [16, 128, N], BF16)
        gd["of"] = dscr("of_" + g, [N, 512], dbg=True)
        gd["QA"] = dscr("QA_" + g, [N // 128, 128, 16, 128], BF16)

    with ExitStack() as st:
        S = Sched(nc, st)
        block = st.enter_context(nc.Block())
        finals = []
        tb = S.ps([128, 8, 128], BF16, "tb")
        accN = S.ps([128, 512], F32, "accN")
        accD = S.ps([128, 512], F32, "accD")
        S.ring = [S.ps([128, 512], F32, "rb") for _ in range(5)]

        def v4(b):
            return b[:].rearrange("p (h i) -> p h i", h=4)

        cstf = S.sb([128, 13, 128], F32, "cstf")
        S.dma("sp", cstf[:], cst, writes=[cstf])
        identb = S.sb([128, 128], BF16, "identb")
        onesb = S.sb([128, 128], BF16, "onesb")
        mprev = S.sb([128, 128], BF16, "mprev")
        mnext = S.sb([128, 128], BF16, "mnext")
        S.op("dve", lambda e: e.tensor_copy(out=identb[:], in_=cstf[:, 0, :]), [cstf], [identb])
        S.op("dve", lambda e: e.tensor_copy(out=onesb[:], in_=cstf[:, 7, :]), [cstf], [onesb])
        S.op("dve", lambda e: e.tensor_copy(out=mprev[:], in_=cstf[:, 8, :]), [cstf], [mprev])
        S.op("dve", lambda e: e.tensor_copy(out=mnext[:], in_=cstf[:, 9, :]), [cstf], [mnext])
        IDf = lambda: cstf[:, 0, :]
        Ud = lambda d: cstf[:, 1 + d, :]
        NEGL = lambda d: cstf[:, 3 + d, :]
        NEGA = lambda d: cstf[:, 5 + d, :]
        ONESf = lambda: cstf[:, 7, :]
        modA = S.sb([128, 3, 2, 8], F32, "modA")
        modB = S.sb([128, 3, 2, 8], F32, "modB")
        convw = S.sb([128, 5, 12], F32, "convw")
        onbc = S.sb([128, 128], F32, "onbc")
        sexp = S.sb([128, 8], F32, "sexp")
        negA = S.sb([128, 8], F32, "negA")
        dtb = S.sb([128, 8], F32, "dtb")
        ss = S.sb([128, 8], F32, "ss")
        rs = S.sb([128, 8], F32, "rs")

        Rw13 = [[Res() for _ in range(NFC)] for _ in range(2)]
        Rw2 = [Res(), Res()]
        Rwina = [Res() for _ in range(33)]
        Rwinb = Res()

        def conv_w13(f):
            src = w13[f].rearrange("(kc p) n -> p kc n", p=128)
            for j in range(NFC):
                for ab in range(2):
                    c0 = ab * DFF + j * 128
                    S.dma("pool", w13s[f][j, :, ab], src[:, :, c0:c0 + 128], writes=[Rw13[f][j]], addw=True)

        def conv_w2(f):
            src = w2[f].rearrange("(fc p) d -> p fc d", p=128)
            for h in range(2):
                S.dma("pool", w2s[f][:, h * 11:(h + 1) * 11, :], src[:, h * 11:(h + 1) * 11, :], writes=[Rw2[f]], addw=True)

        def conv_win():
            src = w_in.rearrange("(kc p) n -> p kc n", p=128)
            for j in range(33):
                c0 = j * 128 if j < 12 else (2832 + (j - 12) * 128 if j < 28 else 2064 + (j - 28) * 128)
                S.dma("pool", wina[j], src[:, :, c0:c0 + 128], writes=[Rwina[j]])
            S.dma("pool", winb, src[:, :, 1536:2832], writes=[Rwinb])

        with ExitStack() as pst:
            def load_T(dbuf, dst, src_rows, R):
                tmp_ = S.sb([R, 128], F32, "ldT", pst)
                S.dma("sp", tmp_[:], src_rows, writes=[tmp_])
                pb__ = S.bank()
                S.op("pe", lambda e: e.matmul(pb__[:, 0:R], lhsT=tmp_[:], rhs=cstf[0:R, 0, 0:R], start=True, stop=True), [tmp_, cstf], [pb__])
                S.op("dve", lambda e: e.tensor_copy(out=dst, in_=pb__[:, 0:R]), [pb__], [dbuf])
            cT = S.sb([128, 2, 8], F32, "cT", pst)
            load_T(cT, cT[:].rearrange("p v k -> p (v k)"), cvec.rearrange("v (kc p) -> (v kc) p", p=128), 16)
            scT = S.sb([128, 8, 2], BF16, "scT", pst)
            S.op("act", lambda e: e.activation(out=scT[:].rearrange("p k v -> p v k"), in_=cT[:], func=AF.Silu), [cT], [scT])
            adabT = S.sb([128, 72], F32, "adabT", pst)
            load_T(adabT, adabT[:], ada_b.rearrange("o (c p) -> (o c) p", p=128), 72)
            nwT = S.sb([128, 3, 8], F32, "nwT", pst)
            for n in range(3):
                load_T(nwT, nwT[:, n, :], normw[n].rearrange("o (c p) -> (o c) p", p=128), 8)
            load_T(convw, convw[:].rearrange("p w c -> p (w c)"), conv_w.rearrange("w (c p) -> (w c) p", p=128), 60)
            S.dma("sp", onbc[:], onorm.partition_broadcast(128), writes=[onbc])
            S.dma("sp", sexp[:], sink.partition_broadcast(128), writes=[sexp])
            S.dma("sp", negA[:], a_log.partition_broadcast(128), writes=[negA])
            S.dma("sp", dtb[:], dt_bias.partition_broadcast(128), writes=[dtb])
            S.op("act", lambda e: e.activation(out=sexp[:], in_=sexp[:], func=AF.Exp), [sexp], [sexp])
            S.op("act", lambda e: e.activation(out=negA[:], in_=negA[:], func=AF.Exp), [negA], [negA])
            S.op("dve", lambda e: e.tensor_scalar(out=negA[:], in0=negA[:], scalar1=-1.0, scalar2=None, op0=ALU.mult), [negA], [negA])
            adap = [S.sb([128, 8, 128], BF16, "adap", pst) for _ in range(3)]
            pm = S.bank()
            pmv = pm[:, 0:144].rearrange("p (c v) -> p c v", v=2)
            asrc = ada_w.rearrange("(kc p) n -> p kc n", p=128)
            conv_w13(0)
            for j in range(72):
                a = adap[j % 3]
                S.dma("pool", a[:], asrc[:, :, j * 128:(j + 1) * 128], writes=[a])
                for kc in range(8):
                    S.op("pe", lambda e, a=a, kc=kc, j=j: e.matmul(pmv[:, j, :], lhsT=a[:, kc, :], rhs=scT[:, kc, :],
                                                                  start=(kc == 0), stop=(kc == 7)), [a, scT], [pm])
                if j == 24:
                    conv_w2(0)
            modT = S.sb([128, 72, 2], F32, "modT", pst)
            S.op("dve", lambda e: e.tensor_tensor(out=modT[:], in0=pmv, in1=bc(adabT[:].unsqueeze(2), [128, 72, 2]), op=ALU.add),
                 [pm, adabT], [modT])
            gs = S.sb([128, 2, 3, 8], F32, "gs", pst)
            for n in range(3):
                for v in range(2):
                    c_sh, c_sc, c_g = (3 * n) * 8, (3 * n + 1) * 8, (3 * n + 2) * 8
                    S.op("dve", lambda e, n=n, v=v, c=c_sc: e.scalar_tensor_tensor(
                        out=modA[:, n, v, :], in0=modT[:, c:c + 8, v], scalar=1.0, in1=nwT[:, n, :], op0=ALU.add, op1=ALU.mult),
                        [modT, nwT], [modA])
                    S.op("dve", lambda e, n=n, v=v, c=c_sh: e.tensor_copy(out=modB[:, n, v, :], in_=modT[:, c:c + 8, v]), [modT], [modB])
                    S.op("dve", lambda e, n=n, v=v, c=c_g: e.tensor_scalar(
                        out=gs[:, v, n, :], in0=modT[:, c:c + 8, v], scalar1=(1.0 if n == 1 else 0.5), scalar2=None, op0=ALU.mult),
                        [modT], [gs])
            Rmod = Res()
            pgs = S.bank()
            gsT = S.sb([48, 128], F32, "gsT", pst)
            S.op("pe", lambda e: e.matmul(pgs[0:48, 0:128], lhsT=gs[:].rearrange("p v n c -> p (v n c)"), rhs=cstf[:, 0, :], start=True, stop=True), [gs, cstf], [pgs])
            S.op("dve", lambda e: e.tensor_copy(out=gsT[:], in_=pgs[0:48, 0:128]), [pgs], [gsT])
            S.dma("sp", modrow.rearrange("v n (c p) -> (v n c) p", p=128), gsT[:], reads=[gsT], writes=[Rmod])
            conv_win()
            conv_w13(1)
            conv_w2(1)
            S.barrier()

        def rstd_from_ss(col, scale):
            S.op("dve", lambda e: e.tensor_scalar(out=rs[:, col], in0=ss[:, col], scalar1=scale, scalar2=EPS, op0=ALU.mult, op1=ALU.add),
                 [ss], [rs])
            S.op("act", lambda e: e.activation(out=rs[:, col], in_=rs[:, col], func=AF.Ln), [rs], [rs])
            S.op("act", lambda e: e.activation(out=rs[:, col], in_=rs[:, col], func=AF.Exp, scale=-0.5), [rs], [rs])

        def norm_to_hT(xt, n, v, xn, hT, nsub):
            for sub in range(nsub):
                x = xt[sub]
                S.op("pool", lambda e: e.memset(ss[:, 0:1], 0.0), [], [ss])
                S.op("act", lambda e, x=x: e.activation(out=xn[:], in_=x[:], func=AF.Square, accum_out=ss[:, 0:1]), [x], [xn, ss])
                rstd_from_ss(slice(0, 1), 1.0 / D)
                S.op("dve", lambda e, x=x: e.tensor_scalar(out=xn[:], in0=x[:], scalar1=rs[:, 0:1], scalar2=None, op0=ALU.mult), [x, rs], [xn])
                for kc in range(8):
                    S.op("pe", lambda e, kc=kc: e.transpose(out=tb[:, kc, :], in_=xn[:, kc * 128:(kc + 1) * 128], identity=identb[:]),
                         [xn, identb], [tb])
                hs = hT[:, :, sub * 128:(sub + 1) * 128]
                S.op("dve", lambda e, hs=hs: e.tensor_tensor(out=hs, in0=tb[:], in1=bc(modA[:, n, v, :].unsqueeze(2), [128, 8, 128]), op=ALU.mult),
                     [tb, modA], [hT])
                S.op("dve", lambda e, hs=hs: e.tensor_tensor(out=hs, in0=hs, in1=bc(modB[:, n, v, :].unsqueeze(2), [128, 8, 128]), op=ALU.add),
                     [hT, modB], [hT])

        def ffn_stage(f, n, gi, tiles, tail, extra_alloc):
            with ExitStack() as fs:
                xt = [S.sb([128, D], F32, "xt", fs) for _ in range(4)]
                xn = S.sb([128, D], BF16, "xn", fs)
                hT = S.sb([128, 8, 512], BF16, "hT", fs)
                w13p = [S.sb([128, 2, 8, 128], BF16, "w13p", fs) for _ in range(3)]
                sa = [S.sb([128, 512], F32, "sa", fs) for _ in range(2)]
                gTt = S.sb([128, NFC, 512], BF16, "gTt", fs)
                w2q = [S.sb([128, NFC, 256], BF16, "w2q", fs) for _ in range(2)]
                tmp = [S.sb([128, 256], F32, "tmp", fs) for _ in range(2)]
                gbc = [S.sb([128, D], F32, "gbc", fs) for _ in range(2)]
                ex = extra_alloc(fs)
                cnt = dict(w13=0, w2=0, sa=0, tmp=0)
                for ti, tl in enumerate(tiles):
                    v = tl["v"]
                    for sub in range(4):
                        S.dma("sp", xt[sub][:], tl["src"][sub * 128:(sub + 1) * 128, :], writes=[xt[sub]])
                    gb_ = gbc[ti % 2]
                    S.dma("sp", gb_[:], modrow[v, n].partition_broadcast(128), reads=[Rmod], writes=[gb_])
                    norm_to_hT(xt, n, v, xn, hT, 4)
                    for j in range(NFC if SUB >= 1 else 0):
                        wp = w13p[cnt["w13"] % 3]; cnt["w13"] += 1
                        S.dma("sp", wp[:], w13s[f][j], reads=[Rw13[f][j]], writes=[wp])
                        pa = S.bank(); pb = S.bank()
                        for ab, pp in ((0, pa), (1, pb)):
                            for kc in range(8):
                                S.op("pe", lambda e, wp=wp, ab=ab, pp=pp, kc=kc: e.matmul(pp[:], lhsT=wp[:, ab, kc, :], rhs=hT[:, kc, :],
                                                                                     start=(kc == 0), stop=(kc == 7)), [wp, hT], [pp])
                        s_ = sa[cnt["sa"] % 2]; cnt["sa"] += 1
                        S.op("act", lambda e, s_=s_, pa=pa: e.activation(out=s_[:], in_=pa[:], func=AF.Silu), [pa], [s_])
                        S.op("dve", lambda e, s_=s_, pb=pb, j=j: e.tensor_tensor(out=gTt[:, j, :], in0=pb[:], in1=s_[:], op=ALU.mult),
                             [pb, s_], [gTt])
                    for q in range(4 if SUB >= 2 else 0):
                        wq = w2q[cnt["w2"] % 2]; cnt["w2"] += 1
                        S.dma("sp", wq[:], w2s[f][:, :, q * 256:(q + 1) * 256], reads=[Rw2[f]], writes=[wq])
                        for sub in range(4):
                            pd = S.bank()
                            for fc in range(NFC):
                                S.op("pe", lambda e, pd=pd, fc=fc, sub=sub, wq=wq: e.matmul(
                                    pd[:, 0:256], lhsT=gTt[:, fc, sub * 128:(sub + 1) * 128], rhs=wq[:, fc, :],
                                    start=(fc == 0), stop=(fc == NFC - 1)), [gTt, wq], [pd])
                            t_ = tmp[cnt["tmp"] % 2]; cnt["tmp"] += 1
                            S.op("dve", lambda e, t_=t_, pd=pd, q=q, gb_=gb_: e.tensor_tensor(
                                out=t_[:], in0=pd[:, 0:256], in1=gb_[:, q * 256:(q + 1) * 256], op=ALU.mult), [pd, gb_], [t_])
                            xs_ = xt[sub]
                            S.op("pool", lambda e, t_=t_, xs_=xs_, q=q: e.tensor_tensor(
                                out=xs_[:, q * 256:(q + 1) * 256], in0=xs_[:, q * 256:(q + 1) * 256], in1=t_[:], op=ALU.add), [xs_, t_], [xs_])
                    if SUB >= 3:
                        tail(tl, xt, xn, hT, ex)
                S.barrier()

        def allocA(fs):
            ex = {}
            ex["winp"] = [S.sb([128, 8, 128], BF16, "winp", fs) for _ in range(3)]
            ex["winb"] = S.sb([128, 8, 1296], BF16, "winb", fs)
            ex["rawo"] = [S.sb([128, 512], F32, "rawo", fs) for _ in range(2)]
            ex["sgo"] = [S.sb([128, 512], BF16, "sgo", fs) for _ in range(2)]
            ex["zo"] = [S.sb([128, 512], F32, "zo", fs) for _ in range(2)]
            ex["dbo"] = [S.sb([128, 128], F32, "dbo", fs) for _ in range(2)]
            ex["qk"] = S.sb([128, 10, 64], F32, "qk", fs)
            ex["kvo"] = S.sb([128, 256], F32, "kvo", fs)
            ex["vo"] = [S.sb([128, 128], BF16, "vo", fs) for _ in range(2)]
            ex["qr"] = S.sb([128, 12, 64], BF16, "qr", fs)
            ex["ta"] = S.sb([128, 10, 32], F32, "ta", fs)
            ex["tb2"] = S.sb([128, 10, 32], F32, "tb2", fs)
            ex["cst"] = S.sb([128, 64], F32, "cst", fs)
            ex["qkT"] = [S.sb([128, 6, 128], BF16, "qkT", fs) for _ in range(2)]
            ex["xf"] = S.sb([128, 512], F32, "xf", fs)
            ex["xr"] = S.sb([128, 512], F32, "xr", fs)
            ex["rt"] = S.sb([128, 512], F32, "rt", fs)
            ex["cosT"] = S.sb([128, 512], F32, "cosT", fs)
            ex["sinT"] = S.sb([128, 512], F32, "sinT", fs)
            ex["cnt"] = 0
            return ex

        def tailA(tl, xt, xn, hT, ex):
            g = tl["g"]; gd = G[g]; n0 = tl["n0"]; v = tl["v"]
            for sub in range(4):
                S.dma("pool", gd["x1"][n0 + sub * 128:n0 + (sub + 1) * 128, :], xt[sub][:], reads=[xt[sub]])
            norm_to_hT(xt, 1, v, xn, hT, 4)
            if SUB < 4:
                return
            wb = ex["winb"]
            S.dma("sp", wb[:], winb, reads=[Rwinb], writes=[wb])
            if g == "s":
                S.dma("sp", ex["cosT"][:], csT[0, :, n0:n0 + 512], writes=[ex["cosT"]])
                S.dma("sp", ex["sinT"][:], csT[1, :, n0:n0 + 512], writes=[ex["sinT"]])
            for j in range(33):
                wp = ex["winp"][j % 3]
                S.dma("sp", wp[:], wina[j], reads=[Rwina[j]], writes=[wp])
                pp = S.bank()
                for kc in range(8):
                    S.op("pe", lambda e, wp=wp, pp=pp, kc=kc: e.matmul(pp[:], lhsT=wp[:, kc, :], rhs=hT[:, kc, :], start=(kc == 0), stop=(kc == 7)),
                         [wp, hT], [pp])
                if j < 12:
                    ro = ex["rawo"][j % 2]
                    S.op("act", lambda e, ro=ro, pp=pp: e.activation(out=ro[:], in_=pp[:], func=AF.Copy), [pp], [ro])
                    if g == "p":
                        s0 = n0 // 256
                        S.dma("pool", gd["rawT"][j, :, s0:s0 + 2, :], ro[:].rearrange("p (s t) -> p s t", s=2), reads=[ro])
                    else:
                        S.dma("pool", gd["rawT"][j, :, 0, n0:n0 + 512], ro[:], reads=[ro])
                elif j < 28:
                    so = ex["sgo"][j % 2]
                    S.op("act", lambda e, so=so, pp=pp: e.activation(out=so[:], in_=pp[:], func=AF.Sigmoid), [pp], [so])
                    S.dma("pool", gd["gT"][j - 12, :, n0:n0 + 512], so[:], reads=[so])
                else:
                    xf = ex["xf"]; xr = ex["xr"]; rt = ex["rt"]
                    if g == "s":
                        S.op("act", lambda e, pp=pp: e.activation(out=xf[:], in_=pp[:], func=AF.Copy), [pp], [xf])
                        prot = S.bank()
                        S.op("pe", lambda e, prot=prot: e.matmul(prot[:], lhsT=cstf[:, 10, :], rhs=xf[:], start=True, stop=True), [cstf, xf], [prot])
                        S.op("dve", lambda e, prot=prot: e.tensor_tensor(out=rt[:], in0=prot[:], in1=ex["sinT"][:], op=ALU.mult), [prot, ex["sinT"]], [rt])
                        S.op("pool", lambda e: e.tensor_tensor(out=xr[:], in0=xf[:], in1=ex["cosT"][:], op=ALU.mult), [xf, ex["cosT"]], [xr])
                        src = xr
                        if j < 32:
                            so = ex["sgo"][j % 2]
                            S.op("dve", lambda e, so=so: e.tensor_tensor(out=so[:], in0=xr[:], in1=rt[:], op=ALU.add), [xr, rt], [so])
                        else:
                            S.op("dve", lambda e: e.tensor_tensor(out=xr[:], in0=xr[:], in1=rt[:], op=ALU.add), [xr, rt], [xr])
                    else:
                        if j < 32:
                            so = ex["sgo"][j % 2]
                            S.op("act", lambda e, so=so, pp=pp: e.activation(out=so[:], in_=pp[:], func=AF.Copy), [pp], [so])
                        else:
                            S.op("act", lambda e, pp=pp: e.activation(out=xr[:], in_=pp[:], func=AF.Copy), [pp], [xr])
                    if j < 32:
                        S.dma("pool", gd["QT"][j - 28, :, n0:n0 + 512], so[:], reads=[so])
                    else:
                        for gk in range(2):
                            psel = S.bank()
                            S.op("pe", lambda e, psel=psel, gk=gk: e.matmul(psel[:], lhsT=cstf[:, 11 + gk, :], rhs=xr[:], start=True, stop=True), [cstf, xr], [psel])
                            so = ex["sgo"][gk]
                            S.op("act", lambda e, so=so, psel=psel: e.activation(out=so[:], in_=psel[:], func=AF.Copy), [psel], [so])
                            S.dma("pool", gd["KT"][gk, :, n0:n0 + 512], so[:], reads=[so])
            for sub in range(4 if SUB >= 5 else 0):
                r0 = n0 + sub * 128
                hs = lambda kc, sub=sub: hT[:, kc, sub * 128:(sub + 1) * 128]
                c = ex["cnt"]; ex["cnt"] += 1
                pz = S.bank()
                for kc in range(8):
                    S.op("pe", lambda e, pz=pz, kc=kc, hs=hs: e.matmul(pz[:], lhsT=hs(kc), rhs=wb[:, kc, 0:512], start=(kc == 0), stop=(kc == 7)), [hT, wb], [pz])
                zo = ex["zo"][c % 2]
                S.op("act", lambda e, zo=zo, pz=pz: e.activation(out=zo[:], in_=pz[:], func=AF.Silu), [pz], [zo])
                S.dma("pool", gd["zs"][r0:r0 + 128, :], zo[:], reads=[zo])
                if SUB2 < 2:
                    continue
                pk = S.bank()
                for kc in range(8):
                    S.op("pe", lambda e, pk=pk, kc=kc, hs=hs: e.matmul(pk[:, 0:256], lhsT=hs(kc), rhs=wb[:, kc, 1040:1296], start=(kc == 0), stop=(kc == 7)), [hT, wb], [pk])
                if not SKIPDB:
                    pdb = S.bank()
                    for kc in range(8):
                        S.op("pe", lambda e, pdb=pdb, kc=kc, hs=hs: e.matmul(pdb[:, 0:128], lhsT=hs(kc), rhs=wb[:, kc, 512:640], start=(kc == 0), stop=(kc == 7)), [hT, wb], [pdb])
                    dbo = ex["dbo"][c % 2]
                    S.op("dve", lambda e, dbo=dbo, pdb=pdb: e.tensor_copy(out=dbo[:], in_=pdb[:, 0:128]), [pdb], [dbo])
                    S.dma("pool", gd["db"][r0:r0 + 128, :], dbo[:], reads=[dbo])
                if SUB2 < 3:
                    continue
                vo = ex["vo"][c % 2]
                S.op("dve", lambda e, vo=vo, pk=pk: e.tensor_copy(out=vo[:], in_=pk[:, 128:256]), [pk], [vo])
                S.dma("pool", gd["Vt"][r0:r0 + 128, :], vo[:], reads=[vo])
                if SUB2 < 4:
                    continue
                if g == "p":
                    kvo = ex["kvo"]
                    S.op("act", lambda e, pk=pk: e.activation(out=kvo[:], in_=pk[:, 0:256], func=AF.Copy), [pk], [kvo])
                    finals.append(S.dma("pool", nck[r0:r0 + 128, :], kvo[:, 0:128], reads=[kvo]))
                    finals.append(S.dma("pool", ncv[r0:r0 + 128, :], kvo[:, 128:256], reads=[kvo]))

        tilesA = []
        for i in range(2):
            tilesA.append(dict(g="p", n0=i * 512, v=0, src=xin["p"][i * 512:(i + 1) * 512, :]))
        for i in range(8):
            tilesA.append(dict(g="s", n0=i * 512, v=1, src=xin["s"][i * 512:(i + 1) * 512, :]))
        if STAGES >= 1:
            ffn_stage(0, 0, 0, tilesA[:NT_A], tailA, allocA)
        for _ in range(PAD):
            S.op("pe", lambda e: e.matmul(accN[:, 0:128], lhsT=identb[:], rhs=identb[:], start=True, stop=True), [identb], [accN])

        def mixer_alloc(ms, conv=True):
            m = {}
            if conv:
                m["raw"] = S.sb([128, 12, 132], F32, "raw", ms)
                m["cT"] = S.sb([128, 12, 128], F32, "cTt", ms)
                m["sq"] = S.sb([128, 8, 128], BF16, "sq", ms)
                m["rinv"] = S.sb([128, 8, 128], F32, "rinv", ms)
            m["qk"] = S.sb([128, 16, 128], BF16, "qkA", ms)
            m["vT"] = S.sb([128, 4, 128], BF16, "vT", ms)
            m["db"] = S.sb([128, 128], F32, "dbt", ms)
            m["g"] = S.sb([128, 4], F32, "g", ms)
            m["beta"] = S.sb([128, 4], F32, "beta", ms)
            m["sm"] = S.sb([128, 8, 4], F32, "sm", ms)
            m["Gb"] = S.sb([128, 4, 128], F32, "Gb", ms)
            m["X1"] = S.sb([128, 4, 128], F32, "X1", ms)
            m["X2"] = S.sb([128, 4, 128], F32, "X2", ms)
            m["EG"] = S.sb([128, 4, 128], F32, "EG", ms)
            m["tf"] = S.sb([128, 4, 128], F32, "tf", ms)
            m["L"] = [S.sb([128, 4, 128], F32, "L", ms) for _ in range(2)]
            m["LT"] = [S.sb([128, 4, 128], F32, "LT", ms) for _ in range(2)]
            m["TT"] = [S.sb([128, 4, 128], F32, "TT", ms) for _ in range(2)]
            m["TTb"] = S.sb([128, 4, 128], BF16, "TTb", ms)
            m["aT"] = S.sb([128, 4, 128], BF16, "aT", ms)
            m["qgT"] = S.sb([128, 4, 128], BF16, "qgT", ms)
            m["vb"] = S.sb([128, 4, 128], BF16, "vb", ms)
            m["kbg"] = S.sb([128, 4, 128], BF16, "kbg", ms)
            m["kd"] = S.sb([128, 4, 128], BF16, "kd", ms)
            m["nwT"] = S.sb([128, 4, 128], BF16, "nwTm", ms)
            m["vnew"] = S.sb([128, 4, 128], BF16, "vnew", ms)
            m["S"] = S.sb([128, 4, 128], F32, "S", ms)
            m["Sb"] = S.sb([128, 4, 128], BF16, "Sb", ms)
            return m

        def conv_prep(m, gd, s, c):
            raw = m["raw"]; cT_ = m["cT"]; qk = m["qk"]
            lo = max(c * 128 - 2, 0); hi = min(c * 128 + 130, gd["T"])
            o0 = lo - (c * 128 - 2)
            if o0 > 0:
                S.op("pool", lambda e: e.memset(raw[:, :, 0:2], 0.0), [], [raw])
            if hi < c * 128 + 130:
                S.op("pool", lambda e: e.memset(raw[:, :, 130:132], 0.0), [], [raw])
            S.dma("sp", raw[:, :, o0:o0 + hi - lo], gd["rawT"].rearrange("j p s t -> p j s t")[:, :, s, lo:hi], writes=[raw], addw=(o0 > 0 or hi < c * 128 + 130))
            for jc in range(12):
                S.op("act", lambda e, jc=jc: e.activation(out=cT_[:, jc, :], in_=raw[:, jc, 0:128], func=AF.Identity, scale=convw[:, 0, jc:jc + 1]),
                     [raw, convw], [cT_])
                for tap in range(1, 5):
                    S.op("dve", lambda e, jc=jc, tap=tap: e.scalar_tensor_tensor(
                        out=cT_[:, jc, :], in0=raw[:, jc, tap:tap + 128], scalar=convw[:, tap, jc:jc + 1], in1=cT_[:, jc, :],
                        op0=ALU.mult, op1=ALU.add), [raw, convw, cT_], [cT_])
            S.op("act", lambda e: e.activation(out=cT_[:], in_=cT_[:], func=AF.Silu), [cT_], [cT_])
            yield
            S.op("act", lambda e: e.activation(out=m["sq"][:], in_=cT_[:, 0:8, :], func=AF.Square), [cT_], [m["sq"]])
            yield
            for hf in range(2):
                pb_ = S.bank()
                for x in range(4):
                    S.op("pe", lambda e, pb_=pb_, x=x, hf=hf: e.matmul(v4(pb_)[:, x, :], lhsT=onesb[:], rhs=m["sq"][:, hf * 4 + x, :], start=True, stop=True),
                         [onesb, m["sq"]], [pb_])
                ri = m["rinv"][:, hf * 4:(hf + 1) * 4, :]
                S.op("dve", lambda e, pb_=pb_, ri=ri: e.tensor_scalar(out=ri, in0=v4(pb_), scalar1=EPS, scalar2=None, op0=ALU.add), [pb_], [m["rinv"]])
            S.op("act", lambda e: e.activation(out=m["rinv"][:], in_=m["rinv"][:], func=AF.Ln), [m["rinv"]], [m["rinv"]])
            yield
            S.op("act", lambda e: e.activation(out=m["rinv"][:], in_=m["rinv"][:], func=AF.Exp, scale=-0.5), [m["rinv"]], [m["rinv"]])
            yield
            S.op("dve", lambda e: e.scalar_tensor_tensor(out=qk[:, 0:4, :], in0=cT_[:, 0:4, :], scalar=128.0 ** -0.5, in1=m["rinv"][:, 0:4, :],
                                                         op0=ALU.mult, op1=ALU.mult), [cT_, m["rinv"]], [qk])
            S.op("dve", lambda e: e.tensor_tensor(out=qk[:, 4:8, :], in0=cT_[:, 4:8, :], in1=m["rinv"][:, 4:8, :], op=ALU.mult), [cT_, m["rinv"]], [qk])
            S.op("act", lambda e: e.activation(out=m["vT"][:], in_=cT_[:, 8:12, :], func=AF.Copy), [cT_], [m["vT"]])
            yield
            for x in range(4):
                S.op("pe", lambda e, x=x: e.transpose(out=tb[:, x, :], in_=qk[:, 4 + x, :], identity=identb[:]), [qk, identb], [tb])
            for x in range(4):
                S.op("pe", lambda e, x=x: e.transpose(out=tb[:, 4 + x, :], in_=m["vT"][:, x, :], identity=identb[:]), [m["vT"], identb], [tb])
            S.op("act", lambda e: e.activation(out=qk[:, 8:16, :], in_=tb[:], func=AF.Copy), [tb], [qk])

        def chunk_prep(m, gd, s, c, d):
            qk = m["qk"]; sm = m["sm"]; g_ = m["g"]; beta = m["beta"]; db_ = m["db"]
            n0 = s * gd["T"] + c * 128
            qT = lambda h: qk[:, h, :]
            kT = lambda h: qk[:, 4 + h, :]
            S.dma("sp", db_[:], gd["db"][n0:n0 + 128, :], writes=[db_])
            S.op("dve", lambda e: e.tensor_tensor(out=g_[:], in0=db_[:, 4 * d:4 * d + 4], in1=dtb[:, 4 * d:4 * d + 4], op=ALU.add), [db_, dtb], [g_])
            S.op("act", lambda e: e.activation(out=g_[:], in_=g_[:], func=AF.Exp), [g_], [g_])
            S.op("dve", lambda e: e.tensor_scalar(out=g_[:], in0=g_[:], scalar1=1.0, scalar2=None, op0=ALU.add), [g_], [g_])
            S.op("act", lambda e: e.activation(out=g_[:], in_=g_[:], func=AF.Ln), [g_], [g_])
            S.op("dve", lambda e: e.tensor_tensor(out=g_[:], in0=g_[:], in1=negA[:, 4 * d:4 * d + 4], op=ALU.mult), [g_, negA], [g_])
            S.op("act", lambda e: e.activation(out=beta[:], in_=db_[:, 8 + 4 * d:12 + 4 * d], func=AF.Exp, scale=-1.0), [db_], [beta])
            S.op("dve", lambda e: e.tensor_scalar(out=beta[:], in0=beta[:], scalar1=1.0, scalar2=None, op0=ALU.add), [beta], [beta])
            S.op("dve", lambda e: e.reciprocal(out=beta[:], in_=beta[:]), [beta], [beta])
            yield
            pg = S.bank()
            S.op("pe", lambda e: e.matmul(pg[:, 0:4], lhsT=Ud(d), rhs=g_[:], start=True, stop=True), [cstf, g_], [pg])
            S.op("pe", lambda e: e.matmul(pg[:, 4:8], lhsT=ONESf(), rhs=g_[:], start=True, stop=True), [cstf, g_], [pg])
            S.op("dve", lambda e: e.tensor_copy(out=sm[:, 0:2, :], in_=pg[:, 0:8].rearrange("p (a h) -> p a h", h=4)), [pg], [sm])
            yield
            gc = sm[:, 0, :]; glast = sm[:, 1, :]
            S.op("dve", lambda e: e.tensor_copy(out=m["Gb"][:], in_=bc(g_[:].unsqueeze(2), [128, 4, 128])), [g_], [m["Gb"]])
            pr_ = S.bank()
            for h in range(4):
                S.op("pe", lambda e, h=h: e.matmul(v4(pr_)[:, h, :], lhsT=m["Gb"][:, h, :], rhs=Ud(d), start=True, stop=True), [m["Gb"], cstf], [pr_])
            gcb = bc(gc.unsqueeze(2), [128, 4, 128])
            X1 = m["X1"]; X2 = m["X2"]; EG = m["EG"]
            S.op("dve", lambda e: e.tensor_tensor(out=X1[:], in0=bc(NEGL(d).unsqueeze(1), [128, 4, 128]), in1=v4(pr_), op=ALU.subtract), [cstf, pr_], [X1])
            S.op("dve", lambda e: e.tensor_tensor(out=X1[:], in0=X1[:], in1=gcb, op=ALU.add), [X1, sm], [X1])
            S.op("dve", lambda e: e.tensor_tensor(out=X2[:], in0=v4(pr_), in1=bc(NEGA(d).unsqueeze(1), [128, 4, 128]), op=ALU.add), [cstf, pr_], [X2])
            S.op("dve", lambda e: e.tensor_tensor(out=X2[:], in0=X2[:], in1=gcb, op=ALU.subtract), [X2, sm], [X2])
            yield
            S.op("act", lambda e: e.activation(out=EG[:], in_=v4(pr_), func=AF.Exp), [pr_], [EG])
            S.op("act", lambda e: e.activation(out=X1[:], in_=X1[:], func=AF.Exp), [X1], [X1])
            S.op("act", lambda e: e.activation(out=X2[:], in_=X2[:], func=AF.Exp), [X2], [X2])
            yield
            S.op("act", lambda e: e.activation(out=sm[:, 2, :], in_=sm[:, 0, :], func=AF.Exp), [sm], [sm])
            S.op("dve", lambda e: e.tensor_tensor(out=sm[:, 3, :], in0=sm[:, 2, :], in1=beta[:], op=ALU.mult), [sm, beta], [sm])
            S.op("dve", lambda e: e.tensor_tensor(out=sm[:, 4, :], in0=sm[:, 1, :], in1=sm[:, 0, :], op=ALU.subtract), [sm], [sm])
            S.op("act", lambda e: e.activation(out=sm[:, 4, :], in_=sm[:, 4, :], func=AF.Exp), [sm], [sm])
            S.op("act", lambda e: e.activation(out=sm[:, 5, :], in_=sm[:, 1, :], func=AF.Exp), [sm], [sm])
            yield
            pkk = S.bank(); pkq = S.bank()
            for h in range(4):
                S.op("pe", lambda e, h=h: e.matmul(v4(pkk)[:, h, :], lhsT=kT(h), rhs=kT(h), start=True, stop=True), [qk], [pkk])
            for h in range(4):
                S.op("pe", lambda e, h=h: e.matmul(v4(pkq)[:, h, :], lhsT=kT(h), rhs=qT(h), start=True, stop=True), [qk], [pkq])
            L0 = m["L"][0]; tf = m["tf"]
            S.op("dve", lambda e: e.tensor_tensor(out=tf[:], in0=v4(pkk), in1=X1[:], op=ALU.mult), [pkk, X1], [tf])
            S.op("dve", lambda e: e.tensor_tensor(out=L0[:], in0=tf[:], in1=bc(beta[:].unsqueeze(2), [128, 4, 128]), op=ALU.mult), [tf, beta], [L0])
            S.op("dve", lambda e: e.tensor_tensor(out=m["aT"][:], in0=v4(pkq), in1=X2[:], op=ALU.mult), [pkq, X2], [m["aT"]])
            yield
            S.op("pool", lambda e: e.tensor_tensor(out=m["qgT"][:], in0=qk[:, 0:4, :], in1=EG[:], op=ALU.mult), [qk, EG], [m["qgT"]])
            S.op("pool", lambda e: e.tensor_tensor(out=m["vb"][:], in0=qk[:, 12:16, :], in1=bc(beta[:].unsqueeze(2), [128, 4, 128]), op=ALU.mult), [qk, beta], [m["vb"]])
            S.op("pool", lambda e: e.tensor_tensor(out=m["kbg"][:], in0=qk[:, 8:12, :], in1=bc(sm[:, 3, :].unsqueeze(2), [128, 4, 128]), op=ALU.mult), [qk, sm], [m["kbg"]])
            S.op("pool", lambda e: e.tensor_tensor(out=m["kd"][:], in0=qk[:, 8:12, :], in1=bc(sm[:, 4, :].unsqueeze(2), [128, 4, 128]), op=ALU.mult), [qk, sm], [m["kd"]])
            yield
            LT0 = m["LT"][0]; TT0 = m["TT"][0]
            pl = S.bank()
            for h in range(4):
                S.op("pe", lambda e, h=h: e.matmul(v4(pl)[:, h, :], lhsT=L0[:, h, :], rhs=IDf(), start=True, stop=True), [L0, cstf], [pl])
            S.op("act", lambda e: e.activation(out=LT0[:], in_=v4(pl), func=AF.Copy), [pl], [LT0])
            S.op("dve", lambda e: e.tensor_tensor(out=TT0[:], in0=bc(IDf().unsqueeze(1), [128, 4, 128]), in1=v4(pl), op=ALU.subtract), [cstf, pl], [TT0])
            yield
            Lc, LTc, TTc = L0, LT0, TT0
            for lev in range(6):
                Ln_ = m["L"][(lev + 1) % 2]; LTn = m["LT"][(lev + 1) % 2]; TTn = m["TT"][(lev + 1) % 2]
                pp = S.bank()
                for h in range(4):
                    S.op("pe", lambda e, h=h, pp=pp, Lc=Lc, LTc=LTc: e.matmul(v4(pp)[:, h, :], lhsT=LTc[:, h, :], rhs=Lc[:, h, :], start=True, stop=True), [Lc, LTc], [pp])
                if lev < 5:
                    pt = S.bank()
                    for h in range(4):
                        S.op("pe", lambda e, h=h, pt=pt, Lc=Lc, LTc=LTc: e.matmul(v4(pt)[:, h, :], lhsT=Lc[:, h, :], rhs=LTc[:, h, :], start=True, stop=True), [Lc, LTc], [pt])
                yield
                S.op("act", lambda e, pp=pp, Ln_=Ln_: e.activation(out=Ln_[:], in_=v4(pp), func=AF.Copy), [pp], [Ln_])
                if lev < 5:
                    S.op("dve", lambda e, pt=pt, LTn=LTn: e.tensor_copy(out=LTn[:], in_=v4(pt)), [pt], [LTn])
                    yield
                pu = S.bank()
                for h in range(4):
                    S.op("pe", lambda e, h=h, pu=pu, Ln_=Ln_, TTc=TTc: e.matmul(v4(pu)[:, h, :], lhsT=Ln_[:, h, :], rhs=TTc[:, h, :], start=True, stop=True), [Ln_, TTc], [pu])
                S.op("dve", lambda e, pu=pu, TTc=TTc, TTn=TTn: e.tensor_tensor(out=TTn[:], in0=v4(pu), in1=TTc[:], op=ALU.add), [pu, TTc], [TTn])
                yield
                Lc, LTc, TTc = Ln_, LTn, TTn
            TTb = m["TTb"]
            S.op("act", lambda e, TTc=TTc: e.activation(out=TTb[:], in_=TTc[:], func=AF.Copy), [TTc], [TTb])
            yield
            pw = S.bank()
            for h in range(4):
                S.op("pe", lambda e, h=h: e.matmul(v4(pw)[:, h, :], lhsT=m["kbg"][:, h, :], rhs=TTb[:, h, :], start=True, stop=True), [m["kbg"], TTb], [pw])
            S.op("act", lambda e: e.activation(out=m["nwT"][:], in_=v4(pw), func=AF.Identity, scale=-1.0), [pw], [m["nwT"]])
            return TTb

        def scan_step(m, TT):
            Sf = m["S"]; Sb_ = m["Sb"]; vnew = m["vnew"]; sm = m["sm"]
            pv = S.bank()
            for h in range(4):
                S.op("pe", lambda e, h=h: e.matmul(v4(pv)[:, h, :], lhsT=TT[:, h, :], rhs=m["vb"][:, h, :], start=True, stop=False), [TT, m["vb"]], [pv])
                S.op("pe", lambda e, h=h: e.matmul(v4(pv)[:, h, :], lhsT=m["nwT"][:, h, :], rhs=Sb_[:, h, :], start=False, stop=True), [m["nwT"], Sb_], [pv])
            S.op("act", lambda e: e.activation(out=vnew[:], in_=v4(pv), func=AF.Copy), [pv], [vnew])
            po = S.bank()
            for h in range(4):
                S.op("pe", lambda e, h=h: e.matmul(v4(po)[:, h, :], lhsT=m["qgT"][:, h, :], rhs=Sb_[:, h, :], start=True, stop=False), [m["qgT"], Sb_], [po])
                S.op("pe", lambda e, h=h: e.matmul(v4(po)[:, h, :], lhsT=m["aT"][:, h, :], rhs=vnew[:, h, :], start=False, stop=True), [m["aT"], vnew], [po])
            pu = S.bank()
            for h in range(4):
                S.op("pe", lambda e, h=h: e.matmul(v4(pu)[:, h, :], lhsT=m["kd"][:, h, :], rhs=vnew[:, h, :], start=True, stop=True), [m["kd"], vnew], [pu])
            S.op("dve", lambda e: e.tensor_tensor(out=Sf[:], in0=Sf[:], in1=bc(sm[:, 5, :].unsqueeze(2), [128, 4, 128]), op=ALU.mult), [Sf, sm], [Sf])
            S.op("dve", lambda e: e.tensor_tensor(out=Sf[:], in0=Sf[:], in1=v4(pu), op=ALU.add), [Sf, pu], [Sf])
            S.op("act", lambda e: e.activation(out=Sb_[:], in_=Sf[:], func=AF.Copy), [Sf], [Sb_])
            return po

        def init_state(m, g, s0ap):
            if g == "s":
                S.dma("sp", m["S"][:], s0ap.rearrange("h k v -> k h v"), writes=[m["S"]])
            else:
                S.op("pool", lambda e: e.memset(m["S"][:], 0.0), [], [m["S"]])
            S.op("act", lambda e: e.activation(out=m["Sb"][:], in_=m["S"][:], func=AF.Copy), [m["S"]], [m["Sb"]])

        def run(gen):
            try:
                while True:
                    next(gen)
            except StopIteration as e_:
                return e_.value

        def interleave(gens):
            vals = [None] * len(gens)
            live = list(range(len(gens)))
            while live:
                for i in list(live):
                    try:
                        next(gens[i])
                    except StopIteration as e_:
                        vals[i] = e_.value
                        live.remove(i)
            return vals

        seqs = DBG_SEQS or ([("p", s) for s in range(4)] + [("s", 0)])

        with ExitStack() as ms:
          if STAGES >= 2:
            mA = mixer_alloc(ms)
            mB = mixer_alloc(ms)
            mB["S"] = mA["S"]; mB["Sb"] = mA["Sb"]
            ofo = [S.sb([128, 512], F32, "ofo", ms) for _ in range(2)]
            ring_save = S.ring
            S.ring = ring_save + [accN, accD]
            cc = 0

            def prep_both(m_, gd, s, c):
                yield from conv_prep(m_, gd, s, c)
                S.dma("pool", gd["QA"][(s * gd["T"] + c * 128) // 128], m_["qk"][:], reads=[m_["qk"]])
                TT_ = yield from chunk_prep(m_, gd, s, c, 0)
                return TT_

            for (g, s) in seqs:
                gd = G[g]
                init_state(mA, g, sf0)
                nch = min(gd["T"] // 128, DBG_NCH)
                for c0 in range(0, nch, 2):
                    ctxs = [(mA, c0)] + ([(mB, c0 + 1)] if c0 + 1 < nch else [])
                    TTs = interleave([prep_both(m_, gd, s, c) for (m_, c) in ctxs])
                    for (m_, c), TT in zip(ctxs, TTs):
                        n0 = s * gd["T"] + c * 128
                        po = scan_step(m_, TT)
                        oo = ofo[cc % 2]; cc += 1
                        S.op("act", lambda e, oo=oo, po=po: e.activation(out=oo[:], in_=po[:], func=AF.Copy), [po], [oo])
                        S.dma("pool", gd["of"][n0:n0 + 128, :], oo[:], reads=[oo])
                if g == "p":
                    finals.append(S.dma("pool", nsf[s].rearrange("h k v -> k h v"), mA["S"][:], reads=[mA["S"]]))
            S.ring = ring_save
            S.barrier()

        with ExitStack() as ms:
          if STAGES >= 3:
            m = mixer_alloc(ms, conv=False)
            m2 = mixer_alloc(ms, conv=False)
            m2["S"] = m["S"]; m2["Sb"] = m["Sb"]
            woa = S.sb([128, 4, D], BF16, "woa", ms)
            wob = S.sb([64, 8, D], BF16, "wob", ms)
            wout = S.sb([128, 8, D], BF16, "wout", ms)
            S.dma("pool", woa[:], w_oa.rearrange("(kc p) d -> p kc d", p=128), writes=[woa])
            S.dma("pool", wob[:], w_ob.rearrange("(h p) d -> p h d", p=64), writes=[wob])
            S.dma("pool", wout[:], w_out.rearrange("(kc p) d -> p kc d", p=128), writes=[wout])
            g2bc = [S.sb([128, D], F32, "g2bc", ms) for _ in range(2)]
            for v in range(2):
                S.dma("sp", g2bc[v][:], modrow[v, 1].partition_broadcast(128), writes=[g2bc[v]])
            ctxKT = S.sb([128, 2, 2, 128], BF16, "ctxKT", ms)
            ctxV = S.sb([128, 2, 128], BF16, "ctxV", ms)
            ckf = S.sb([128, 2, 128], F32, "ckf", ms)
            cvf = S.sb([128, 2, 128], F32, "cvf", ms)
            ckd = S.sb([128, 2, 2, 2, 64], BF16, "ckd", ms)
            S.dma("sp", ckf[:], ck.rearrange("(b p) f -> p b f", p=128), writes=[ckf])
            S.dma("sp", cvf[:], cv.rearrange("(b p) f -> p b f", p=128), writes=[cvf])
            S.op("dve", lambda e: e.tensor_copy(out=ctxV[:], in_=cvf[:]), [cvf], [ctxV])
            for dup in range(2):
                S.op("dve", lambda e, dup=dup: e.tensor_copy(out=ckd[:, :, :, dup, :], in_=ckf[:].rearrange("p b (k d) -> p b k d", d=64)), [ckf], [ckd])
            for b in range(2):
                for kv in range(2):
                    S.op("pe", lambda e, b=b, kv=kv: e.transpose(out=tb[:, b * 2 + kv, :], in_=ckd[:, b, kv].rearrange("p a d -> p (a d)"), identity=identb[:]),
                         [ckd, identb], [tb])
            S.op("act", lambda e: e.activation(out=ctxKT[:].rearrange("p b k n -> p (b k) n"), in_=tb[:, 0:4, :], func=AF.Copy), [tb], [ctxKT])

            ofl = S.sb([128, 4, 128], F32, "ofl", ms)
            zsl = S.sb([128, 4, 128], F32, "zsl", ms)
            ot = S.sb([128, 4, 128], F32, "ot", ms)
            junk = S.sb([128, 128], BF16, "junk", ms)
            ya = S.sb([128, 4, 128], BF16, "ya", ms)
            yaT = S.sb([128, 4, 128], BF16, "yaT", ms)
            QTlo = S.sb([128, 4, 128], BF16, "QTlo", ms)
            QThi = S.sb([128, 4, 128], BF16, "QThi", ms)
            S.op("pool", lambda e: e.memset(QTlo[:], 0.0), [], [QTlo])
            S.op("pool", lambda e: e.memset(QThi[:], 0.0), [], [QThi])
            KTt = [S.sb([128, 2, 128], BF16, "KTt", ms) for _ in range(2)]
            Vtt = [S.sb([128, 128], BF16, "Vtt", ms) for _ in range(2)]
            PTb = [S.sb([128, 4, 128], BF16, "PTb", ms) for _ in range(2)]
            den = S.sb([64, 4, 128], F32, "den", ms)
            ybT = S.sb([64, 8, 128], BF16, "ybT", ms)
            gTl = S.sb([128, 16, 128], BF16, "gTl", ms)
            t1 = S.sb([128, 4, 128], F32, "t1", ms)
            t2 = S.sb([128, 4, 128], F32, "t2", ms)
            mT = S.sb([128, 8, 128], BF16, "mT", ms)
            x1t = S.sb([128, D], F32, "x1t", ms)
            ytmp = S.sb([128, 512], F32, "ytmp", ms)
            kcnt = 0
            for (g, s) in seqs:
                gd = G[g]; v = gd["v"]; T = gd["T"]; nch = T // 128
                init_state(m, g, sb0)
                m_all = {}
                for c in range(nch - 1, -1, -1):
                    n0 = s * T + c * 128
                    if c not in m_all:
                        ctxs = [(m, c)] + ([(m2, c - 1)] if c - 1 >= 0 else [])
                        for (m_, c_) in ctxs:
                            S.dma("sp", m_["qk"][:], gd["QA"][(s * T + c_ * 128) // 128], writes=[m_["qk"]])
                        ring_save = S.ring
                        S.ring = ring_save + [accN, accD]
                        TTs = interleave([chunk_prep(m_, gd, s, c_, 1) for (m_, c_) in ctxs])
                        S.ring = ring_save
                        for (m_, c_), TT_ in zip(ctxs, TTs):
                            m_all[c_] = (m_, TT_)
                    mc, TT = m_all.pop(c)
                    po = scan_step(mc, TT)
                    S.dma("sp", ofl[:], gd["of"][n0:n0 + 128, :].rearrange("p (h d) -> p h d", h=4), writes=[ofl])
                    S.dma("sp", zsl[:], gd["zs"][n0:n0 + 128, :].rearrange("p (h d) -> p h d", h=4), writes=[zsl])
                    S.op("dve", lambda e, po=po: e.tensor_tensor(out=ot[:], in0=v4(po), in1=ofl[:], op=ALU.add), [po, ofl], [ot])
                    S.op("pool", lambda e: e.memset(ss[:, 0:4], 0.0), [], [ss])
                    for h in range(4):
                        S.op("act", lambda e, h=h: e.activation(out=junk[:], in_=ot[:, h, :], func=AF.Square, accum_out=ss[:, h:h + 1]), [ot], [junk, ss])
                    rstd_from_ss(slice(0, 4), 1.0 / 128)
                    S.op("dve", lambda e: e.tensor_tensor(out=ot[:], in0=ot[:], in1=bc(rs[:, 0:4].unsqueeze(2), [128, 4, 128]), op=ALU.mult), [ot, rs], [ot])
                    S.op("pool", lambda e: e.tensor_tensor(out=ot[:], in0=ot[:], in1=bc(onbc[:].unsqueeze(1), [128, 4, 128]), op=ALU.mult), [ot, onbc], [ot])
                    S.op("dve", lambda e: e.tensor_tensor(out=ya[:], in0=ot[:], in1=zsl[:], op=ALU.mult), [ot, zsl], [ya])
                    for h in range(4):
                        S.op("pe", lambda e, h=h: e.transpose(out=tb[:, h, :], in_=ya[:, h, :], identity=identb[:]), [ya, identb], [tb])
                    S.op("act", lambda e: e.activation(out=yaT[:], in_=tb[:, 0:4, :], func=AF.Copy), [tb], [yaT])
                    if CSUB < 2:
                        continue
                    qsrc = gd["QT"].rearrange("c p n -> p c n")
                    S.dma("sp", QTlo[0:64], qsrc[0:64, :, n0:n0 + 128], writes=[QTlo])
                    S.dma("sp", QThi[64:128], qsrc[64:128, :, n0:n0 + 128], writes=[QThi])
                    blocks = []
                    if g == "s":
                        if c > 0:
                            blocks.append(("loc", c - 1, mprev))
                        blocks.append(("loc", c, None))
                        if c < nch - 1:
                            blocks.append(("loc", c + 1, mnext))
                        blocks += [("ctx", 0, None), ("ctx", 1, None)]
                    else:
                        blocks = [("loc", 0, None), ("loc", 1, None)]
                    for gk in range(2):
                        for bi, (kind, idx, mask) in enumerate(blocks):
                            if kind == "loc":
                                kb0 = s * T + idx * 128
                                if gk == 0 or True:
                                    KT_ = KTt[kcnt % 2]; V_ = Vtt[kcnt % 2]; kcnt += 1
                                    S.dma("sp", KT_[:], gd["KT"].rearrange("c p n -> p c n")[:, :, kb0:kb0 + 128], writes=[KT_])
                                    S.dma("sp", V_[:], gd["Vt"][kb0:kb0 + 128, :], writes=[V_])
                                kfull = KT_[:, gk, :]; vv = V_[:, gk * 64:(gk + 1) * 64]
                                kres = [KT_]; vres = [V_]
                            else:
                                kfull = ctxKT[:, idx, gk, :]; vv = ctxV[:, idx, gk * 64:(gk + 1) * 64]
                                kres = [ctxKT]; vres = [ctxV]
                            pst_ = S.bank()
                            stv = pst_[:].rearrange("p (a b i) -> p a b i", a=2, b=2)
                            for a_ in range(2):
                                S.op("pe", lambda e, kfull=kfull, stv=stv, gk=gk, a_=a_: e.matmul(stv[:, a_, 0, :], lhsT=kfull, rhs=QTlo[:, 2 * gk + a_, :], start=True, stop=True),
                                     kres + [QTlo], [pst_])
                                S.op("pe", lambda e, kfull=kfull, stv=stv, gk=gk, a_=a_: e.matmul(stv[:, a_, 1, :], lhsT=kfull, rhs=QThi[:, 2 * gk + a_, :], start=True, stop=True),
                                     kres + [QThi], [pst_])
                            P_ = PTb[(gk * 8 + bi) % 2]
                            S.op("act", lambda e, P_=P_, pst_=pst_: e.activation(out=P_[:], in_=v4(pst_), func=AF.Exp, scale=0.125), [pst_], [P_])
                            if mask is not None:
                                S.op("pool", lambda e, P_=P_, mask=mask: e.tensor_tensor(out=P_[:], in0=P_[:], in1=bc(mask[:].unsqueeze(1), [128, 4, 128]), op=ALU.mult),
                                     [P_, mask], [P_])
                            first = (bi == 0); last = (bi == len(blocks) - 1)
                            S.op("pe", lambda e, P_=P_, vv=vv, first=first, last=last: e.matmul(accN[0:64, :], lhsT=vv, rhs=P_[:].rearrange("p h i -> p (h i)"),
                                                                                              start=first, stop=last), vres + [P_], [accN])
                            S.op("pe", lambda e, P_=P_, first=first, last=last: e.matmul(accD[0:64, :], lhsT=onesb[:, 0:64], rhs=P_[:].rearrange("p h i -> p (h i)"),
                                                                                       start=first, stop=last), [onesb, P_], [accD])
                        S.op("dve", lambda e, gk=gk: e.tensor_tensor(out=den[:], in0=accD[0:64, :].rearrange("p (h i) -> p h i", h=4),
                                                                    in1=bc(sexp[0:64, 4 * gk:4 * gk + 4].unsqueeze(2), [64, 4, 128]), op=ALU.add), [accD, sexp], [den])
                        S.op("dve", lambda e: e.reciprocal(out=den[:], in_=den[:]), [den], [den])
                        S.op("dve", lambda e, gk=gk: e.tensor_tensor(out=ybT[:, 4 * gk:4 * gk + 4, :], in0=accN[0:64, :].rearrange("p (h i) -> p h i", h=4),
                                                                    in1=den[:], op=ALU.mult), [accN, den], [ybT])
                    if CSUB < 3:
                        continue
                    S.dma("sp", gTl[:], gd["gT"].rearrange("c p n -> p c n")[:, :, n0:n0 + 128], writes=[gTl])
                    S.dma("sp", x1t[:], gd["x1"][n0:n0 + 128, :], writes=[x1t])
                    for hf in range(2):
                        pa = S.bank(); pb_ = S.bank()
                        for dj4 in range(4):
                            dj = hf * 4 + dj4
                            for kc in range(4):
                                S.op("pe", lambda e, pa=pa, dj=dj, dj4=dj4, kc=kc: e.matmul(v4(pa)[:, dj4, :], lhsT=woa[:, kc, dj * 128:(dj + 1) * 128], rhs=yaT[:, kc, :],
                                                                                         start=(kc == 0), stop=(kc == 3)), [woa, yaT], [pa])
                            for h in range(8):
                                S.op("pe", lambda e, pb_=pb_, dj=dj, dj4=dj4, h=h: e.matmul(v4(pb_)[:, dj4, :], lhsT=wob[0:64, h, dj * 128:(dj + 1) * 128], rhs=ybT[0:64, h, :],
                                                                                         start=(h == 0), stop=(h == 7)), [wob, ybT], [pb_])
                        S.op("dve", lambda e, pa=pa, hf=hf: e.tensor_tensor(out=t1[:], in0=v4(pa), in1=gTl[:, hf * 4:hf * 4 + 4, :], op=ALU.mult), [pa, gTl], [t1])
                        S.op("dve", lambda e, pb_=pb_, hf=hf: e.tensor_tensor(out=t2[:], in0=v4(pb_), in1=gTl[:, 8 + hf * 4:12 + hf * 4, :], op=ALU.mult), [pb_, gTl], [t2])
                        S.op("pool", lambda e, hf=hf: e.tensor_tensor(out=mT[:, hf * 4:hf * 4 + 4, :], in0=t1[:], in1=t2[:], op=ALU.add), [t1, t2], [mT])
                    for hf in range(2):
                        py = S.bank()
                        for kc in range(8):
                            S.op("pe", lambda e, py=py, kc=kc, hf=hf: e.matmul(py[:], lhsT=mT[:, kc, :], rhs=wout[:, kc, hf * 512:(hf + 1) * 512], start=(kc == 0), stop=(kc == 7)),
                                 [mT, wout], [py])
                        S.op("dve", lambda e, py=py, hf=hf, v=v: e.tensor_tensor(out=ytmp[:], in0=py[:], in1=g2bc[v][:, hf * 512:(hf + 1) * 512], op=ALU.mult), [py, g2bc[v]], [ytmp])
                        S.op("pool", lambda e, hf=hf: e.tensor_tensor(out=x1t[:, hf * 512:(hf + 1) * 512], in0=x1t[:, hf * 512:(hf + 1) * 512], in1=ytmp[:], op=ALU.add), [x1t, ytmp], [x1t])
                    S.dma("pool", gd["x1"][n0:n0 + 128, :], x1t[:], reads=[x1t])
                if g == "p":
                    finals.append(S.dma("pool", nsb[s].rearrange("h k v -> k h v"), m["S"][:], reads=[m["S"]]))
            S.barrier()

        def allocC(fs):
            ex = {}
            ex["nfbc"] = S.sb([128, D], F32, "nfbc", fs)
            S.dma("sp", ex["nfbc"][:], norm_final.partition_broadcast(128), writes=[ex["nfbc"]])
            ex["yo"] = [S.sb([128, D], F32, "yo", fs) for _ in range(2)]
            ex["cnt"] = 0
            return ex

        def tailC(tl, xt, xn, hT, ex):
            for sub in range(4):
                x = xt[sub]
                S.op("pool", lambda e: e.memset(ss[:, 0:1], 0.0), [], [ss])
                S.op("act", lambda e, x=x: e.activation(out=xn[:], in_=x[:], func=AF.Square, accum_out=ss[:, 0:1]), [x], [xn, ss])
                rstd_from_ss(slice(0, 1), 1.0 / D)
                yo = ex["yo"][ex["cnt"] % 2]; ex["cnt"] += 1
                S.op("dve", lambda e, x=x, yo=yo: e.scalar_tensor_tensor(out=yo[:], in0=x[:], scalar=rs[:, 0:1], in1=ex["nfbc"][:], op0=ALU.mult, op1=ALU.mult),
                     [x, rs, ex["nfbc"]], [yo])
                r0 = tl["n0"] + sub * 128
                finals.append(S.dma("pool", yout[tl["g"]][r0:r0 + 128, :], yo[:], reads=[yo]))

        tilesC = []
        for i in range(2):
            tilesC.append(dict(g="p", n0=i * 512, v=0, src=G["p"]["x1"][i * 512:(i + 1) * 512, :]))
        for i in range(8):
            tilesC.append(dict(g="s", n0=i * 512, v=1, src=G["s"]["x1"][i * 512:(i + 1) * 512, :]))
        if STAGES >= 4:
            ffn_stage(1, 2, 2, tilesC, tailC, allocC)

        S.emit(block, final_waits=finals)
    return nc


def _consts():
    i = np.arange(128)
    p = i[:, None]; f = i[None, :]
    c = np.zeros((128, 13, 128), np.float32)
    c[:, 0] = (p == f)
    c[:, 1] = (p <= f)
    c[:, 2] = (p >= f)
    c[:, 3] = np.where(f < p, 0.0, NEG)
    c[:, 4] = np.where(f > p, 0.0, NEG)
    c[:, 5] = np.where(p <= f, 0.0, NEG)
    c[:, 6] = np.where(p >= f, 0.0, NEG)
    c[:, 7] = 1.0
    c[:, 8] = (p >= f)
    c[:, 9] = (p <= f)
    dm = f % 64
    c[:, 10] = np.where((dm < 32) & (p == f + 32), -1.0, 0.0) + np.where((dm >= 32) & (p == f - 32), 1.0, 0.0)
    c[:, 11] = (p == (f % 64))
    c[:, 12] = (p == 64 + (f % 64))
    rows = 4096 // 64
    row = np.repeat(np.arange(rows, dtype=np.float32), 64)
    col = np.tile(np.arange(64, dtype=np.float32), rows)
    inv = np.power(np.float32(10000.0), -np.arange(16, dtype=np.float32) / np.float32(16)).astype(np.float32)
    ang = np.concatenate([row[:, None] * inv, col[:, None] * inv], axis=-1).astype(np.float32)
    cs = np.concatenate([np.cos(ang), np.sin(ang)], axis=-1).astype(np.float32)
    mi = (np.arange(128) % 64) % 32
    csT = np.stack([np.cos(ang)[:, mi].T, np.sin(ang)[:, mi].T], 0).astype(np.float32)
    return c, cs, np.ascontiguousarray(csT)


_NC_CACHE = {}


def kernel(x_prompt, x_sample, state_delta_fwd, state_delta_bwd, cache_k, cache_v, c, c_ctx,
           ada_w, ada_b, norm_ffn1, ffn1_w13, ffn1_w2, norm_mix, w_in, conv_w, a_log, dt_bias,
           onorm_a, w_oa, w_ob, w_out, sink, norm_ffn2, ffn2_w13, ffn2_w2, norm_final):
    A = lambda a: np.ascontiguousarray(np.asarray(a, dtype=np.float32))
    if "nc" not in _NC_CACHE:
        _NC_CACHE["nc"] = build_program()
    nc = _NC_CACHE["nc"]
    cst, cs, csT = _consts()
    shared = {
        "ada_w": A(ada_w)[0], "ada_b": A(ada_b).reshape(1, -1), "norm_ffn1": A(norm_ffn1).reshape(1, -1),
        "ffn1_w13": A(ffn1_w13)[0], "ffn1_w2": A(ffn1_w2)[0], "norm_mix": A(norm_mix).reshape(1, -1),
        "w_in": A(w_in)[0], "conv_w": A(conv_w)[0], "a_log": A(a_log).reshape(1, 8), "dt_bias": A(dt_bias).reshape(1, 8),
        "onorm_a": A(onorm_a).reshape(1, 128), "w_oa": A(w_oa)[0], "w_ob": A(w_ob)[0], "w_out": A(w_out)[0],
        "sink": A(sink).reshape(1, 8), "norm_ffn2": A(norm_ffn2).reshape(1, -1), "ffn2_w13": A(ffn2_w13)[0],
        "ffn2_w2": A(ffn2_w2)[0], "norm_final": A(norm_final).reshape(1, -1), "cst": cst, "cs": cs, "csT": csT,
    }
    xp = A(x_prompt); xs = A(x_sample)
    in_maps = []
    for k in range(8):
        d = dict(shared)
        d["xp"] = xp[4 * k:4 * k + 4].reshape(1024, D)
        d["xs"] = xs[k]
        d["sf0"] = A(state_delta_fwd)[k, 0]
        d["sb0"] = A(state_delta_bwd)[k, 0]
        d["ck"] = A(cache_k)[k, 0].reshape(256, 128)
        d["cv"] = A(cache_v)[k, 0].reshape(256, 128)
        d["cvec"] = np.ascontiguousarray(np.stack([A(c_ctx), A(c)[k]], axis=0))
        in_maps.append(d)
    res = run_bass_kernel_spmd(nc, in_maps, core_ids=list(range(8)))
    R = res.results
    if DEBUG:
        _NC_CACHE["dbg"] = R
    y_prompt = np.concatenate([R[k]["yp"].reshape(4, 256, D) for k in range(8)], axis=0)
    y_sample = np.stack([R[k]["ys"] for k in range(8)], axis=0)
    nsf = np.concatenate([R[k]["nsf"] for k in range(8)], axis=0)[:, None]
    nsb = np.concatenate([R[k]["nsb"] for k in range(8)], axis=0)[:, None]
    nck = np.concatenate([R[k]["nck"].reshape(4, 256, 2, 64) for k in range(8)], axis=0)[:, None]
    ncv = np.concatenate([R[k]["ncv"].reshape(4, 256, 2, 64) for k in range(8)], axis=0)[:, None]
    return (y_prompt.astype(np.float32), y_sample.astype(np.float32), nsf.astype(np.float32), nsb.astype(np.float32),
            nck.astype(np.float32), ncv.astype(np.float32))
```

```python
import numpy as np
from contextlib import ExitStack
import concourse.bass as bass
import concourse.mybir as mybir
from concourse.bass_utils import run_bass_kernel_spmd

F32 = mybir.dt.float32
BF16 = mybir.dt.bfloat16
AF = mybir.ActivationFunctionType
ALU = mybir.AluOpType

D = 1024
DFF = 2816
NFC = 22
EPS = 1e-6
NEG = -30000.0
DEBUG = False
STAGES = 99
NT_A = 10
SUB = 99
SUB2 = 99
NOASSERT = False
SKIPDB = False
PAD = 0
DBG_SEQS = None
DBG_NCH = 999
CSUB = 99


class Res:
    __slots__ = ("name", "lw", "rd")

    def __init__(self, name=""):
        self.name = name
        self.lw = []
        self.rd = []


class Buf(Res):
    __slots__ = ("t", "psum")

    def __init__(self, t, name="", psum=False):
        Res.__init__(self, name)
        self.t = t
        self.psum = psum

    def __getitem__(self, k):
        return self.t[k]


class Op:
    __slots__ = ("eng", "fn", "deps", "sig", "dma", "semv", "idx")


class Sched:
    ENG = ("pe", "act", "dve", "pool", "sp")
    DQ = ("sp", "pool")
    NDSEM = 8

    def __init__(self, nc, stack):
        self.nc = nc
        self.stack = stack
        self.ops = {e: [] for e in self.ENG}
        self.esem = {e: stack.enter_context(nc.semaphore("s_" + e)) for e in self.ENG}
        self.dsem = {e: [stack.enter_context(nc.semaphore("d_%s%d" % (e, i))) for i in range(self.NDSEM)]
                     for e in self.DQ}
        self.ndma = {e: 0 for e in self.DQ}
        self.dmatok = {e: [] for e in self.DQ}
        self.nbuf = 0
        self.pending = {e: [] for e in self.ENG}
        self.ring = []
        self.ringi = 0

    def sb(self, shape, dt, name=None, stack=None):
        self.nbuf += 1
        name = "%s_%d" % (name or "sb", self.nbuf)
        return Buf((stack or self.stack).enter_context(self.nc.sbuf_tensor(name, list(shape), dt)), name)

    def ps(self, shape, dt, name=None):
        self.nbuf += 1
        name = "%s_%d" % (name or "ps", self.nbuf)
        return Buf(self.stack.enter_context(self.nc.psum_tensor(name, list(shape), dt)), name, psum=True)

    def bank(self):
        b = self.ring[self.ringi % len(self.ring)]
        self.ringi += 1
        assert NOASSERT or (not b.lw) or b.rd, "ring bank reused before consumption: " + b.name
        return b

    def _rec(self, o, reads, writes, addw=False):
        deps = list(self.pending[o.eng])
        self.pending[o.eng] = []
        for r in reads:
            deps.extend(r.lw)
            if getattr(r, "psum", False):
                deps.extend([x for x in r.rd if x.eng != o.eng])
        for w in writes:
            deps.extend(w.lw)
            deps.extend(w.rd)
        if o.eng == "pe" and not o.dma:
            deps = [d for d in deps if d.dma or d.eng != "pe"]
        o.deps = deps
        o.idx = len(self.ops[o.eng])
        self.ops[o.eng].append(o)
        for r in reads:
            r.rd.append(o)
        for w in writes:
            if addw:
                w.lw = w.lw + [o]
            else:
                w.lw = [o]
                w.rd = []

    def op(self, eng, fn, reads=(), writes=()):
        o = Op()
        o.eng = eng; o.fn = fn; o.sig = False; o.dma = False; o.semv = None
        self._rec(o, reads, writes)
        return o

    def dma(self, q, out, in_, reads=(), writes=(), addw=False, **kw):
        o = Op()
        o.eng = q; o.dma = True; o.sig = True
        o.fn = lambda e: e.dma_start(out=out, in_=in_, **kw)
        self._rec(o, reads, writes, addw)
        n = self.ndma[q]
        self.ndma[q] += 1
        o.semv = (self.dsem[q][n % self.NDSEM], 16 * (n // self.NDSEM + 1))
        if n >= self.NDSEM:
            o.deps.append(self.dmatok[q][n - self.NDSEM])
        self.dmatok[q].append(o)
        return o

    def barrier(self):
        toks = []
        for e in self.ENG:
            comp = [o for o in self.ops[e] if not o.dma]
            if comp:
                toks.append(comp[-1])
        for q in self.DQ:
            toks.extend(self.dmatok[q][-self.NDSEM:])
        for e in self.ENG:
            self.pending[e] = self.pending[e] + toks

    def emit(self, block, final_waits=()):
        for e in self.ENG:
            for o in self.ops[e]:
                for d in o.deps:
                    if not d.dma:
                        d.sig = True
        for e in self.ENG:
            c = 0
            for o in self.ops[e]:
                if not o.dma and o.sig:
                    c += 1
                    o.semv = (self.esem[e], c)
        engobj = {"pe": "tensor", "act": "scalar", "dve": "vector", "pool": "gpsimd", "sp": "sync"}

        def make(e):
            def body(E):
                waited = {}

                def wait(tok):
                    s, v = tok.semv
                    k = id(s)
                    if waited.get(k, 0) >= v:
                        return
                    waited[k] = v
                    E.wait_ge(s, v)
                for o in self.ops[e]:
                    for d in o.deps:
                        if d is not o:
                            wait(d)
                    ins = o.fn(E)
                    if o.sig:
                        s, v = o.semv
                        ins.then_inc(s, 16 if o.dma else 1)
                if e == "sp":
                    for o in final_waits:
                        wait(o)
            return body
        for e in self.ENG:
            getattr(block, engobj[e])(make(e))


def bc(ap, shape):
    return ap.broadcast_to(list(shape))


def build_program():
    nc = bass.Bass("TRN2", target_bir_lowering=False)
    SK = "ExternalOutput" if DEBUG else "Internal"

    def din(name, shape, dt=F32):
        return nc.dram_tensor(name, list(shape), dt, kind="ExternalInput").ap()

    def dout(name, shape, dt=F32):
        return nc.dram_tensor(name, list(shape), dt, kind="ExternalOutput").ap()

    def dscr(name, shape, dt=F32, dbg=False):
        return nc.dram_tensor(name, list(shape), dt, kind=(SK if dbg else "Internal")).ap()

    xin = {"p": din("xp", [1024, D]), "s": din("xs", [4096, D])}
    sf0 = din("sf0", [4, 128, 128]); sb0 = din("sb0", [4, 128, 128])
    ck = din("ck", [256, 128]); cv = din("cv", [256, 128])
    cvec = din("cvec", [2, D])
    ada_w = din("ada_w", [D, 9 * D]); ada_b = din("ada_b", [1, 9 * D])
    normw = [din("norm_ffn1", [1, D]), din("norm_mix", [1, D]), din("norm_ffn2", [1, D])]
    w13 = [din("ffn1_w13", [D, 2 * DFF]), din("ffn2_w13", [D, 2 * DFF])]
    w2 = [din("ffn1_w2", [DFF, D]), din("ffn2_w2", [DFF, D])]
    w_in = din("w_in", [D, 4880])
    conv_w = din("conv_w", [5, 1536])
    a_log = din("a_log", [1, 8]); dt_bias = din("dt_bias", [1, 8])
    onorm = din("onorm_a", [1, 128])
    w_oa = din("w_oa", [512, D]); w_ob = din("w_ob", [512, D]); w_out = din("w_out", [D, D])
    sink = din("sink", [1, 8])
    norm_final = din("norm_final", [1, D])
    cst = din("cst", [128, 13, 128])
    csT = din("csT", [2, 128, 4096])
    cs = din("cs", [4096, 64])

    yout = {"p": dout("yp", [1024, D]), "s": dout("ys", [4096, D])}
    nsf = dout("nsf", [4, 4, 128, 128]); nsb = dout("nsb", [4, 4, 128, 128])
    nck = dout("nck", [1024, 128]); ncv = dout("ncv", [1024, 128])

    w13s = [dscr("w13s%d" % f, [NFC, 128, 2, 8, 128], BF16) for f in range(2)]
    w2s = [dscr("w2s%d" % f, [128, NFC, D], BF16) for f in range(2)]
    wina = dscr("wina", [33, 128, 8, 128], BF16)
    winb = dscr("winb", [128, 8, 1296], BF16)
    modrow = dscr("modrow", [2, 3, D])
    G = {"p": dict(nseq=4, T=256, v=0), "s": dict(nseq=1, T=4096, v=1)}
    for g, gd in G.items():
        N = gd["nseq"] * gd["T"]
        gd["N"] = N
        gd["x1"] = dscr("x1_" + g, [N, D], dbg=True)
        gd["rawT"] = dscr("rawT_" + g, [12, 128, gd["nseq"], gd["T"]], dbg=True)
        gd["zs"] = dscr("zs_" + g, [N, 512], dbg=True)
        gd["db"] = dscr("db_" + g, [N, 128], dbg=True)
        gd["QT"] = dscr("QT_" + g, [4, 128, N], BF16)
        gd["KT"] = dscr("KT_" + g, [2, 128, N], BF16)
        gd["Vt"] = dscr("Vt_" + g, [N, 128], BF16)
        gd["gT"] = dscr("gT_" + g, [16, 128, N], BF16)
        gd["of"] = dscr("of_" + g, [N, 512], dbg=True)
        gd["QA"] = dscr("QA_" + g, [N // 128, 128, 16, 128], BF16)

    with ExitStack() as st:
        S = Sched(nc, st)
        block = st.enter_context(nc.Block())
        finals = []
        tb = S.ps([128, 8, 128], BF16, "tb")
        accN = S.ps([128, 512], F32, "accN")
        accD = S.ps([128, 512], F32, "accD")
        S.ring = [S.ps([128, 512], F32, "rb") for _ in range(5)]

        def v4(b):
            return b[:].rearrange("p (h i) -> p h i", h=4)

        cstf = S.sb([128, 13, 128], F32, "cstf")
        S.dma("sp", cstf[:], cst, writes=[cstf])
        identb = S.sb([128, 128], BF16, "identb")
        onesb = S.sb([128, 128], BF16, "onesb")
        mprev = S.sb([128, 128], BF16, "mprev")
        mnext = S.sb([128, 128], BF16, "mnext")
        S.op("dve", lambda e: e.tensor_copy(out=identb[:], in_=cstf[:, 0, :]), [cstf], [identb])
        S.op("dve", lambda e: e.tensor_copy(out=onesb[:], in_=cstf[:, 7, :]), [cstf], [onesb])
        S.op("dve", lambda e: e.tensor_copy(out=mprev[:], in_=cstf[:, 8, :]), [cstf], [mprev])
        S.op("dve", lambda e: e.tensor_copy(out=mnext[:], in_=cstf[:, 9, :]), [cstf], [mnext])
        IDf = lambda: cstf[:, 0, :]
        Ud = lambda d: cstf[:, 1 + d, :]
        NEGL = lambda d: cstf[:, 3 + d, :]
        NEGA = lambda d: cstf[:, 5 + d, :]
        ONESf = lambda: cstf[:, 7, :]
        modA = S.sb([128, 3, 2, 8], F32, "modA")
        modB = S.sb([128, 3, 2, 8], F32, "modB")
        convw = S.sb([128, 5, 12], F32, "convw")
        onbc = S.sb([128, 128], F32, "onbc")
        sexp = S.sb([128, 8], F32, "sexp")
        negA = S.sb([128, 8], F32, "negA")
        dtb = S.sb([128, 8], F32, "dtb")
        ss = S.sb([128, 8], F32, "ss")
        rs = S.sb([128, 8], F32, "rs")

        Rw13 = [[Res() for _ in range(NFC)] for _ in range(2)]
        Rw2 = [Res(), Res()]
        Rwina = [Res() for _ in range(33)]
        Rwinb = Res()

        def conv_w13(f):
            src = w13[f].rearrange("(kc p) n -> p kc n", p=128)
            for j in range(NFC):
                for ab in range(2):
                    c0 = ab * DFF + j * 128
                    S.dma("pool", w13s[f][j, :, ab], src[:, :, c0:c0 + 128], writes=[Rw13[f][j]], addw=True)

        def conv_w2(f):
            src = w2[f].rearrange("(fc p) d -> p fc d", p=128)
            for h in range(2):
                S.dma("pool", w2s[f][:, h * 11:(h + 1) * 11, :], src[:, h * 11:(h + 1) * 11, :], writes=[Rw2[f]], addw=True)

        def conv_win():
            src = w_in.rearrange("(kc p) n -> p kc n", p=128)
            for j in range(33):
                c0 = j * 128 if j < 12 else (2832 + (j - 12) * 128 if j < 28 else 2064 + (j - 28) * 128)
                S.dma("pool", wina[j], src[:, :, c0:c0 + 128], writes=[Rwina[j]])
            S.dma("pool", winb, src[:, :, 1536:2832], writes=[Rwinb])

        with ExitStack() as pst:
            def load_T(dbuf, dst, src_rows, R):
                tmp_ = S.sb([R, 128], F32, "ldT", pst)
                S.dma("sp", tmp_[:], src_rows, writes=[tmp_])
                pb__ = S.bank()
                S.op("pe", lambda e: e.matmul(pb__[:, 0:R], lhsT=tmp_[:], rhs=cstf[0:R, 0, 0:R], start=True, stop=True), [tmp_, cstf], [pb__])
                S.op("dve", lambda e: e.tensor_copy(out=dst, in_=pb__[:, 0:R]), [pb__], [dbuf])
            cT = S.sb([128, 2, 8], F32, "cT", pst)
            load_T(cT, cT[:].rearrange("p v k -> p (v k)"), cvec.rearrange("v (kc p) -> (v kc) p", p=128), 16)
            scT = S.sb([128, 8, 2], BF16, "scT", pst)
            S.op("act", lambda e: e.activation(out=scT[:].rearrange("p k v -> p v k"), in_=cT[:], func=AF.Silu), [cT], [scT])
            adabT = S.sb([128, 72], F32, "adabT", pst)
            load_T(adabT, adabT[:], ada_b.rearrange("o (c p) -> (o c) p", p=128), 72)
            nwT = S.sb([128, 3, 8], F32, "nwT", pst)
            for n in range(3):
                load_T(nwT, nwT[:, n, :], normw[n].rearrange("o (c p) -> (o c) p", p=128), 8)
            load_T(convw, convw[:].rearrange("p w c -> p (w c)"), conv_w.rearrange("w (c p) -> (w c) p", p=128), 60)
            S.dma("sp", onbc[:], onorm.partition_broadcast(128), writes=[onbc])
            S.dma("sp", sexp[:], sink.partition_broadcast(128), writes=[sexp])
            S.dma("sp", negA[:], a_log.partition_broadcast(128), writes=[negA])
            S.dma("sp", dtb[:], dt_bias.partition_broadcast(128), writes=[dtb])
            S.op("act", lambda e: e.activation(out=sexp[:], in_=sexp[:], func=AF.Exp), [sexp], [sexp])
            S.op("act", lambda e: e.activation(out=negA[:], in_=negA[:], func=AF.Exp), [negA], [negA])
            S.op("dve", lambda e: e.tensor_scalar(out=negA[:], in0=negA[:], scalar1=-1.0, scalar2=None, op0=ALU.mult), [negA], [negA])
            adap = [S.sb([128, 8, 128], BF16, "adap", pst) for _ in range(3)]
            pm = S.bank()
            pmv = pm[:, 0:144].rearrange("p (c v) -> p c v", v=2)
            asrc = ada_w.rearrange("(kc p) n -> p kc n", p=128)
            conv_w13(0)
            for j in range(72):
                a = adap[j % 3]
                S.dma("pool", a[:], asrc[:, :, j * 128:(j + 1) * 128], writes=[a])
                for kc in range(8):
                    S.op("pe", lambda e, a=a, kc=kc, j=j: e.matmul(pmv[:, j, :], lhsT=a[:, kc, :], rhs=scT[:, kc, :],
                                                                  start=(kc == 0), stop=(kc == 7)), [a, scT], [pm])
                if j == 24:
                    conv_w2(0)
            modT = S.sb([128, 72, 2], F32, "modT", pst)
            S.op("dve", lambda e: e.tensor_tensor(out=modT[:], in0=pmv, in1=bc(adabT[:].unsqueeze(2), [128, 72, 2]), op=ALU.add),
                 [pm, adabT], [modT])
            gs = S.sb([128, 2, 3, 8], F32, "gs", pst)
            for n in range(3):
                for v in range(2):
                    c_sh, c_sc, c_g = (3 * n) * 8, (3 * n + 1) * 8, (3 * n + 2) * 8
                    S.op("dve", lambda e, n=n, v=v, c=c_sc: e.scalar_tensor_tensor(
                        out=modA[:, n, v, :], in0=modT[:, c:c + 8, v], scalar=1.0, in1=nwT[:, n, :], op0=ALU.add, op1=ALU.mult),
                        [modT, nwT], [modA])
                    S.op("dve", lambda e, n=n, v=v, c=c_sh: e.tensor_copy(out=modB[:, n, v, :], in_=modT[:, c:c + 8, v]), [modT], [modB])
                    S.op("dve", lambda e, n=n, v=v, c=c_g: e.tensor_scalar(
                        out=gs[:, v, n, :], in0=modT[:, c:c + 8, v], scalar1=(1.0 if n == 1 else 0.5), scalar2=None, op0=ALU.mult),
                        [modT], [gs])
            Rmod = Res()
            pgs = S.bank()
            gsT = S.sb([48, 128], F32, "gsT", pst)
            S.op("pe", lambda e: e.matmul(pgs[0:48, 0:128], lhsT=gs[:].rearrange("p v n c -> p (v n c)"), rhs=cstf[:, 0, :], start=True, stop=True), [gs, cstf], [pgs])
            S.op("dve", lambda e: e.tensor_copy(out=gsT[:], in_=pgs[0:48, 0:128]), [pgs], [gsT])
            S.dma("sp", modrow.rearrange("v n (c p) -> (v n c) p", p=128), gsT[:], reads=[gsT], writes=[Rmod])
            conv_win()
            conv_w13(1)
            conv_w2(1)
            S.barrier()

        def rstd_from_ss(col, scale):
            S.op("dve", lambda e: e.tensor_scalar(out=rs[:, col], in0=ss[:, col], scalar1=scale, scalar2=EPS, op0=ALU.mult, op1=ALU.add),
                 [ss], [rs])
            S.op("act", lambda e: e.activation(out=rs[:, col], in_=rs[:, col], func=AF.Ln), [rs], [rs])
            S.op("act", lambda e: e.activation(out=rs[:, col], in_=rs[:, col], func=AF.Exp, scale=-0.5), [rs], [rs])

        def norm_to_hT(xt, n, v, xn, hT, nsub):
            for sub in range(nsub):
                x = xt[sub]
                S.op("pool", lambda e: e.memset(ss[:, 0:1], 0.0), [], [ss])
                S.op("act", lambda e, x=x: e.activation(out=xn[:], in_=x[:], func=AF.Square, accum_out=ss[:, 0:1]), [x], [xn, ss])
                rstd_from_ss(slice(0, 1), 1.0 / D)
                S.op("dve", lambda e, x=x: e.tensor_scalar(out=xn[:], in0=x[:], scalar1=rs[:, 0:1], scalar2=None, op0=ALU.mult), [x, rs], [xn])
                for kc in range(8):
                    S.op("pe", lambda e, kc=kc: e.transpose(out=tb[:, kc, :], in_=xn[:, kc * 128:(kc + 1) * 128], identity=identb[:]),
                         [xn, identb], [tb])
                hs = hT[:, :, sub * 128:(sub + 1) * 128]
                S.op("dve", lambda e, hs=hs: e.tensor_tensor(out=hs, in0=tb[:], in1=bc(modA[:, n, v, :].unsqueeze(2), [128, 8, 128]), op=ALU.mult),
                     [tb, modA], [hT])
                S.op("dve", lambda e, hs=hs: e.tensor_tensor(out=hs, in0=hs, in1=bc(modB[:, n, v, :].unsqueeze(2), [128, 8, 128]), op=ALU.add),
                     [hT, modB], [hT])

        def ffn_stage(f, n, gi, tiles, tail, extra_alloc):
            with ExitStack() as fs:
                xt = [S.sb([128, D], F32, "xt", fs) for _ in range(4)]
                xn = S.sb([128, D], BF16, "xn", fs)
                hT = S.sb([128, 8, 512], BF16, "hT", fs)
                w13p = [S.sb([128, 2, 8, 128], BF16, "w13p", fs) for _ in range(3)]
                sa = [S.sb([128, 512], F32, "sa", fs) for _ in range(2)]
                gTt = S.sb([128, NFC, 512], BF16, "gTt", fs)
                w2q = [S.sb([128, NFC, 256], BF16, "w2q", fs) for _ in range(2)]
                tmp = [S.sb([128, 256], F32, "tmp", fs) for _ in range(2)]
                gbc = [S.sb([128, D], F32, "gbc", fs) for _ in range(2)]
                ex = extra_alloc(fs)
                cnt = dict(w13=0, w2=0, sa=0, tmp=0)
                for ti, tl in enumerate(tiles):
                    v = tl["v"]
                    for sub in range(4):
                        S.dma("sp", xt[sub][:], tl["src"][sub * 128:(sub + 1) * 128, :], writes=[xt[sub]])
                    gb_ = gbc[ti % 2]
                    S.dma("sp", gb_[:], modrow[v, n].partition_broadcast(128), reads=[Rmod], writes=[gb_])
                    norm_to_hT(xt, n, v, xn, hT, 4)
                    for j in range(NFC if SUB >= 1 else 0):
                        wp = w13p[cnt["w13"] % 3]; cnt["w13"] += 1
                        S.dma("sp", wp[:], w13s[f][j], reads=[Rw13[f][j]], writes=[wp])
                        pa = S.bank(); pb = S.bank()
                        for ab, pp in ((0, pa), (1, pb)):
                            for kc in range(8):
                                S.op("pe", lambda e, wp=wp, ab=ab, pp=pp, kc=kc: e.matmul(pp[:], lhsT=wp[:, ab, kc, :], rhs=hT[:, kc, :],
                                                                                     start=(kc == 0), stop=(kc == 7)), [wp, hT], [pp])
                        s_ = sa[cnt["sa"] % 2]; cnt["sa"] += 1
                        S.op("act", lambda e, s_=s_, pa=pa: e.activation(out=s_[:], in_=pa[:], func=AF.Silu), [pa], [s_])
                        S.op("dve", lambda e, s_=s_, pb=pb, j=j: e.tensor_tensor(out=gTt[:, j, :], in0=pb[:], in1=s_[:], op=ALU.mult),
                             [pb, s_], [gTt])
                    for q in range(4 if SUB >= 2 else 0):
                        wq = w2q[cnt["w2"] % 2]; cnt["w2"] += 1
                        S.dma("sp", wq[:], w2s[f][:, :, q * 256:(q + 1) * 256], reads=[Rw2[f]], writes=[wq])
                        for sub in range(4):
                            pd = S.bank()
                            for fc in range(NFC):
                                S.op("pe", lambda e, pd=pd, fc=fc, sub=sub, wq=wq: e.matmul(
                                    pd[:, 0:256], lhsT=gTt[:, fc, sub * 128:(sub + 1) * 128], rhs=wq[:, fc, :],
                                    start=(fc == 0), stop=(fc == NFC - 1)), [gTt, wq], [pd])
                            t_ = tmp[cnt["tmp"] % 2]; cnt["tmp"] += 1
                            S.op("dve", lambda e, t_=t_, pd=pd, q=q, gb_=gb_: e.tensor_tensor(
                                out=t_[:], in0=pd[:, 0:256], in1=gb_[:, q * 256:(q + 1) * 256], op=ALU.mult), [pd, gb_], [t_])
                            xs_ = xt[sub]
                            S.op("pool", lambda e, t_=t_, xs_=xs_, q=q: e.tensor_tensor(
                                out=xs_[:, q * 256:(q + 1) * 256], in0=xs_[:, q * 256:(q + 1) * 256], in1=t_[:], op=ALU.add), [xs_, t_], [xs_])
                    if SUB >= 3:
                        tail(tl, xt, xn, hT, ex)
                S.barrier()

        def allocA(fs):
            ex = {}
            ex["winp"] = [S.sb([128, 8, 128], BF16, "winp", fs) for _ in range(3)]
            ex["winb"] = S.sb([128, 8, 1296], BF16, "winb", fs)
            ex["rawo"] = [S.sb([128, 512], F32, "rawo", fs) for _ in range(2)]
            ex["sgo"] = [S.sb([128, 512], BF16, "sgo", fs) for _ in range(2)]
            ex["zo"] = [S.sb([128, 512], F32, "zo", fs) for _ in range(2)]
            ex["dbo"] = [S.sb([128, 128], F32, "dbo", fs) for _ in range(2)]
            ex["qk"] = S.sb([128, 10, 64], F32, "qk", fs)
            ex["kvo"] = S.sb([128, 256], F32, "kvo", fs)
            ex["vo"] = [S.sb([128, 128], BF16, "vo", fs) for _ in range(2)]
            ex["qr"] = S.sb([128, 12, 64], BF16, "qr", fs)
            ex["ta"] = S.sb([128, 10, 32], F32, "ta", fs)
            ex["tb2"] = S.sb([128, 10, 32], F32, "tb2", fs)
            ex["cst"] = S.sb([128, 64], F32, "cst", fs)
            ex["qkT"] = [S.sb([128, 6, 128], BF16, "qkT", fs) for _ in range(2)]
            ex["xf"] = S.sb([128, 512], F32, "xf", fs)
            ex["xr"] = S.sb([128, 512], F32, "xr", fs)
            ex["rt"] = S.sb([128, 512], F32, "rt", fs)
            ex["cosT"] = S.sb([128, 512], F32, "cosT", fs)
            ex["sinT"] = S.sb([128, 512], F32, "sinT", fs)
            ex["cnt"] = 0
            return ex

        def tailA(tl, xt, xn, hT, ex):
            g = tl["g"]; gd = G[g]; n0 = tl["n0"]; v = tl["v"]
            for sub in range(4):
                S.dma("pool", gd["x1"][n0 + sub * 128:n0 + (sub + 1) * 128, :], xt[sub][:], reads=[xt[sub]])
            norm_to_hT(xt, 1, v, xn, hT, 4)
            if SUB < 4:
                return
            wb = ex["winb"]
            S.dma("sp", wb[:], winb, reads=[Rwinb], writes=[wb])
            if g == "s":
                S.dma("sp", ex["cosT"][:], csT[0, :, n0:n0 + 512], writes=[ex["cosT"]])
                S.dma("sp", ex["sinT"][:], csT[1, :, n0:n0 + 512], writes=[ex["sinT"]])
            for j in range(33):
                wp = ex["winp"][j % 3]
                S.dma("sp", wp[:], wina[j], reads=[Rwina[j]], writes=[wp])
                pp = S.bank()
                for kc in range(8):
                    S.op("pe", lambda e, wp=wp, pp=pp, kc=kc: e.matmul(pp[:], lhsT=wp[:, kc, :], rhs=hT[:, kc, :], start=(kc == 0), stop=(kc == 7)),
                         [wp, hT], [pp])
                if j < 12:
                    ro = ex["rawo"][j % 2]
                    S.op("act", lambda e, ro=ro, pp=pp: e.activation(out=ro[:], in_=pp[:], func=AF.Copy), [pp], [ro])
                    if g == "p":
                        s0 = n0 // 256
                        S.dma("pool", gd["rawT"][j, :, s0:s0 + 2, :], ro[:].rearrange("p (s t) -> p s t", s=2), reads=[ro])
                    else:
                        S.dma("pool", gd["rawT"][j, :, 0, n0:n0 + 512], ro[:], reads=[ro])
                elif j < 28:
                    so = ex["sgo"][j % 2]
                    S.op("act", lambda e, so=so, pp=pp: e.activation(out=so[:], in_=pp[:], func=AF.Sigmoid), [pp], [so])
                    S.dma("pool", gd["gT"][j - 12, :, n0:n0 + 512], so[:], reads=[so])
                else:
                    xf = ex["xf"]; xr = ex["xr"]; rt = ex["rt"]
                    if g == "s":
                        S.op("act", lambda e, pp=pp: e.activation(out=xf[:], in_=pp[:], func=AF.Copy), [pp], [xf])
                        prot = S.bank()
                        S.op("pe", lambda e, prot=prot: e.matmul(prot[:], lhsT=cstf[:, 10, :], rhs=xf[:], start=True, stop=True), [cstf, xf], [prot])
                        S.op("dve", lambda e, prot=prot: e.tensor_tensor(out=rt[:], in0=prot[:], in1=ex["sinT"][:], op=ALU.mult), [prot, ex["sinT"]], [rt])
                        S.op("pool", lambda e: e.tensor_tensor(out=xr[:], in0=xf[:], in1=ex["cosT"][:], op=ALU.mult), [xf, ex["cosT"]], [xr])
                        src = xr
                        if j < 32:
                            so = ex["sgo"][j % 2]
                            S.op("dve", lambda e, so=so: e.tensor_tensor(out=so[:], in0=xr[:], in1=rt[:], op=ALU.add), [xr, rt], [so])
                        else:
                            S.op("dve", lambda e: e.tensor_tensor(out=xr[:], in0=xr[:], in1=rt[:], op=ALU.add), [xr, rt], [xr])
                    else:
                        if j < 32:
                            so = ex["sgo"][j % 2]
                            S.op("act", lambda e, so=so, pp=pp: e.activation(out=so[:], in_=pp[:], func=AF.Copy), [pp], [so])
                        else:
                            S.op("act", lambda e, pp=pp: e.activation(out=xr[:], in_=pp[:], func=AF.Copy), [pp], [xr])
                    if j < 32:
                        S.dma("pool", gd["QT"][j - 28, :, n0:n0 + 512], so[:], reads=[so])
                    else:
                        for gk in range(2):
                            psel = S.bank()
                            S.op("pe", lambda e, psel=psel, gk=gk: e.matmul(psel[:], lhsT=cstf[:, 11 + gk, :], rhs=xr[:], start=True, stop=True), [cstf, xr], [psel])
                            so = ex["sgo"][gk]
                            S.op("act", lambda e, so=so, psel=psel: e.activation(out=so[:], in_=psel[:], func=AF.Copy), [psel], [so])
                            S.dma("pool", gd["KT"][gk, :, n0:n0 + 512], so[:], reads=[so])
            for sub in range(4 if SUB >= 5 else 0):
                r0 = n0 + sub * 128
                hs = lambda kc, sub=sub: hT[:, kc, sub * 128:(sub + 1) * 128]
                c = ex["cnt"]; ex["cnt"] += 1
                pz = S.bank()
                for kc in range(8):
                    S.op("pe", lambda e, pz=pz, kc=kc, hs=hs: e.matmul(pz[:], lhsT=hs(kc), rhs=wb[:, kc, 0:512], start=(kc == 0), stop=(kc == 7)), [hT, wb], [pz])
                zo = ex["zo"][c % 2]
                S.op("act", lambda e, zo=zo, pz=pz: e.activation(out=zo[:], in_=pz[:], func=AF.Silu), [pz], [zo])
                S.dma("pool", gd["zs"][r0:r0 + 128, :], zo[:], reads=[zo])
                if SUB2 < 2:
                    continue
                pk = S.bank()
                for kc in range(8):
                    S.op("pe", lambda e, pk=pk, kc=kc, hs=hs: e.matmul(pk[:, 0:256], lhsT=hs(kc), rhs=wb[:, kc, 1040:1296], start=(kc == 0), stop=(kc == 7)), [hT, wb], [pk])
                if not SKIPDB:
                    pdb = S.bank()
                    for kc in range(8):
                        S.op("pe", lambda e, pdb=pdb, kc=kc, hs=hs: e.matmul(pdb[:, 0:128], lhsT=hs(kc), rhs=wb[:, kc, 512:640], start=(kc == 0), stop=(kc == 7)), [hT, wb], [pdb])
                    dbo = ex["dbo"][c % 2]
                    S.op("dve", lambda e, dbo=dbo, pdb=pdb: e.tensor_copy(out=dbo[:], in_=pdb[:, 0:128]), [pdb], [dbo])
                    S.dma("pool", gd["db"][r0:r0 + 128, :], dbo[:], reads=[dbo])
                if SUB2 < 3:
                    continue
                vo = ex["vo"][c % 2]
                S.op("dve", lambda e, vo=vo, pk=pk: e.tensor_copy(out=vo[:], in_=pk[:, 128:256]), [pk], [vo])
                S.dma("pool", gd["Vt"][r0:r0 + 128, :], vo[:], reads=[vo])
                if SUB2 < 4:
                    continue
                if g == "p":
                    kvo = ex["kvo"]
                    S.op("act", lambda e, pk=pk: e.activation(out=kvo[:], in_=pk[:, 0:256], func=AF.Copy), [pk], [kvo])
                    finals.append(S.dma("pool", nck[r0:r0 + 128, :], kvo[:, 0:128], reads=[kvo]))
                    finals.append(S.dma("pool", ncv[r0:r0 + 128, :], kvo[:, 128:256], reads=[kvo]))

        tilesA = []
        for i in range(2):
            tilesA.append(dict(g="p", n0=i * 512, v=0, src=xin["p"][i * 512:(i + 1) * 512, :]))
        for i in range(8):
            tilesA.append(dict(g="s", n0=i * 512, v=1, src=xin["s"][i * 512:(i + 1) * 512, :]))
        if STAGES >= 1:
            ffn_stage(0, 0, 0, tilesA[:NT_A], tailA, allocA)
        for _ in range(PAD):
            S.op("pe", lambda e: e.matmul(accN[:, 0:128], lhsT=identb[:], rhs=identb[:], start=True, stop=True), [identb], [accN])

        def mixer_alloc(ms, conv=True):
            m = {}
            if conv:
                m["raw"] = S.sb([128, 12, 132], F32, "raw", ms)
                m["cT"] = S.sb([128, 12, 128], F32, "cTt", ms)
                m["sq"] = S.sb([128, 8, 128], BF16, "sq", ms)
                m["rinv"] = S.sb([128, 8, 128], F32, "rinv", ms)
            m["qk"] = S.sb([128, 16, 128], BF16, "qkA", ms)
            m["vT"] = S.sb([128, 4, 128], BF16, "vT", ms)
            m["db"] = S.sb([128, 128], F32, "dbt", ms)
            m["g"] = S.sb([128, 4], F32, "g", ms)
            m["beta"] = S.sb([128, 4], F32, "beta", ms)
            m["sm"] = S.sb([128, 8, 4], F32, "sm", ms)
            m["Gb"] = S.sb([128, 4, 128], F32, "Gb", ms)
            m["X1"] = S.sb([128, 4, 128], F32, "X1", ms)
            m["X2"] = S.sb([128, 4, 128], F32, "X2", ms)
            m["EG"] = S.sb([128, 4, 128], F32, "EG", ms)
            m["tf"] = S.sb([128, 4, 128], F32, "tf", ms)
            m["L"] = [S.sb([128, 4, 128], F32, "L", ms) for _ in range(2)]
            m["LT"] = [S.sb([128, 4, 128], F32, "LT", ms) for _ in range(2)]
            m["TT"] = [S.sb([128, 4, 128], F32, "TT", ms) for _ in range(2)]
            m["TTb"] = S.sb([128, 4, 128], BF16, "TTb", ms)
            m["aT"] = S.sb([128, 4, 128], BF16, "aT", ms)
            m["qgT"] = S.sb([128, 4, 128], BF16, "qgT", ms)
            m["vb"] = S.sb([128, 4, 128], BF16, "vb", ms)
            m["kbg"] = S.sb([128, 4, 128], BF16, "kbg", ms)
            m["kd"] = S.sb([128, 4, 128], BF16, "kd", ms)
            m["nwT"] = S.sb([128, 4, 128], BF16, "nwTm", ms)
            m["vnew"] = S.sb([128, 4, 128], BF16, "vnew", ms)
            m["S"] = S.sb([128, 4, 128], F32, "S", ms)
            m["Sb"] = S.sb([128, 4, 128], BF16, "Sb", ms)
            return m

        def conv_prep(m, gd, s, c):
            raw = m["raw"]; cT_ = m["cT"]; qk = m["qk"]
            lo = max(c * 128 - 2, 0); hi = min(c * 128 + 130, gd["T"])
            o0 = lo - (c * 128 - 2)
            if o0 > 0:
                S.op("pool", lambda e: e.memset(raw[:, :, 0:2], 0.0), [], [raw])
            if hi < c * 128 + 130:
                S.op("pool", lambda e: e.memset(raw[:, :, 130:132], 0.0), [], [raw])
            S.dma("sp", raw[:, :, o0:o0 + hi - lo], gd["rawT"].rearrange("j p s t -> p j s t")[:, :, s, lo:hi], writes=[raw], addw=(o0 > 0 or hi < c * 128 + 130))
            for jc in range(12):
                S.op("act", lambda e, jc=jc: e.activation(out=cT_[:, jc, :], in_=raw[:, jc, 0:128], func=AF.Identity, scale=convw[:, 0, jc:jc + 1]),
                     [raw, convw], [cT_])
                for tap in range(1, 5):
                    S.op("dve", lambda e, jc=jc, tap=tap: e.scalar_tensor_tensor(
                        out=cT_[:, jc, :], in0=raw[:, jc, tap:tap + 128], scalar=convw[:, tap, jc:jc + 1], in1=cT_[:, jc, :],
                        op0=ALU.mult, op1=ALU.add), [raw, convw, cT_], [cT_])
            S.op("act", lambda e: e.activation(out=cT_[:], in_=cT_[:], func=AF.Silu), [cT_], [cT_])
            yield
            S.op("act", lambda e: e.activation(out=m["sq"][:], in_=cT_[:, 0:8, :], func=AF.Square), [cT_], [m["sq"]])
            yield
            for hf in range(2):
                pb_ = S.bank()
                for x in range(4):
                    S.op("pe", lambda e, pb_=pb_, x=x, hf=hf: e.matmul(v4(pb_)[:, x, :], lhsT=onesb[:], rhs=m["sq"][:, hf * 4 + x, :], start=True, stop=True),
                         [onesb, m["sq"]], [pb_])
                ri = m["rinv"][:, hf * 4:(hf + 1) * 4, :]
                S.op("dve", lambda e, pb_=pb_, ri=ri: e.tensor_scalar(out=ri, in0=v4(pb_), scalar1=EPS, scalar2=None, op0=ALU.add), [pb_], [m["rinv"]])
            S.op("act", lambda e: e.activation(out=m["rinv"][:], in_=m["rinv"][:], func=AF.Ln), [m["rinv"]], [m["rinv"]])
            yield
            S.op("act", lambda e: e.activation(out=m["rinv"][:], in_=m["rinv"][:], func=AF.Exp, scale=-0.5), [m["rinv"]], [m["rinv"]])
            yield
            S.op("dve", lambda e: e.scalar_tensor_tensor(out=qk[:, 0:4, :], in0=cT_[:, 0:4, :], scalar=128.0 ** -0.5, in1=m["rinv"][:, 0:4, :],
                                                         op0=ALU.mult, op1=ALU.mult), [cT_, m["rinv"]], [qk])
            S.op("dve", lambda e: e.tensor_tensor(out=qk[:, 4:8, :], in0=cT_[:, 4:8, :], in1=m["rinv"][:, 4:8, :], op=ALU.mult), [cT_, m["rinv"]], [qk])
            S.op("act", lambda e: e.activation(out=m["vT"][:], in_=cT_[:, 8:12, :], func=AF.Copy), [cT_], [m["vT"]])
            yield
            for x in range(4):
                S.op("pe", lambda e, x=x: e.transpose(out=tb[:, x, :], in_=qk[:, 4 + x, :], identity=identb[:]), [qk, identb], [tb])
            for x in range(4):
                S.op("pe", lambda e, x=x: e.transpose(out=tb[:, 4 + x, :], in_=m["vT"][:, x, :], identity=identb[:]), [m["vT"], identb], [tb])
            S.op("act", lambda e: e.activation(out=qk[:, 8:16, :], in_=tb[:], func=AF.Copy), [tb], [qk])

        def chunk_prep(m, gd, s, c, d):
            qk = m["qk"]; sm = m["sm"]; g_ = m["g"]; beta = m["beta"]; db_ = m["db"]
            n0 = s * gd["T"] + c * 128
            qT = lambda h: qk[:, h, :]
            kT = lambda h: qk[:, 4 + h, :]
            S.dma("sp", db_[:], gd["db"][n0:n0 + 128, :], writes=[db_])
            S.op("dve", lambda e: e.tensor_tensor(out=g_[:], in0=db_[:, 4 * d:4 * d + 4], in1=dtb[:, 4 * d:4 * d + 4], op=ALU.add), [db_, dtb], [g_])
            S.op("act", lambda e: e.activation(out=g_[:], in_=g_[:], func=AF.Exp), [g_], [g_])
            S.op("dve", lambda e: e.tensor_scalar(out=g_[:], in0=g_[:], scalar1=1.0, scalar2=None, op0=ALU.add), [g_], [g_])
            S.op("act", lambda e: e.activation(out=g_[:], in_=g_[:], func=AF.Ln), [g_], [g_])
            S.op("dve", lambda e: e.tensor_tensor(out=g_[:], in0=g_[:], in1=negA[:, 4 * d:4 * d + 4], op=ALU.mult), [g_, negA], [g_])
            S.op("act", lambda e: e.activation(out=beta[:], in_=db_[:, 8 + 4 * d:12 + 4 * d], func=AF.Exp, scale=-1.0), [db_], [beta])
            S.op("dve", lambda e: e.tensor_scalar(out=beta[:], in0=beta[:], scalar1=1.0, scalar2=None, op0=ALU.add), [beta], [beta])
            S.op("dve", lambda e: e.reciprocal(out=beta[:], in_=beta[:]), [beta], [beta])
            yield
            pg = S.bank()
            S.op("pe", lambda e: e.matmul(pg[:, 0:4], lhsT=Ud(d), rhs=g_[:], start=True, stop=True), [cstf, g_], [pg])
            S.op("pe", lambda e: e.matmul(pg[:, 4:8], lhsT=ONESf(), rhs=g_[:], start=True, stop=True), [cstf, g_], [pg])
            S.op("dve", lambda e: e.tensor_copy(out=sm[:, 0:2, :], in_=pg[:, 0:8].rearrange("p (a h) -> p a h", h=4)), [pg], [sm])
            yield
            gc = sm[:, 0, :]; glast = sm[:, 1, :]
            S.op("dve", lambda e: e.tensor_copy(out=m["Gb"][:], in_=bc(g_[:].unsqueeze(2), [128, 4, 128])), [g_], [m["Gb"]])
            pr_ = S.bank()
            for h in range(4):
                S.op("pe", lambda e, h=h: e.matmul(v4(pr_)[:, h, :], lhsT=m["Gb"][:, h, :], rhs=Ud(d), start=True, stop=True), [m["Gb"], cstf], [pr_])
            gcb = bc(gc.unsqueeze(2), [128, 4, 128])
            X1 = m["X1"]; X2 = m["X2"]; EG = m["EG"]
            S.op("dve", lambda e: e.tensor_tensor(out=X1[:], in0=bc(NEGL(d).unsqueeze(1), [128, 4, 128]), in1=v4(pr_), op=ALU.subtract), [cstf, pr_], [X1])
            S.op("dve", lambda e: e.tensor_tensor(out=X1[:], in0=X1[:], in1=gcb, op=ALU.add), [X1, sm], [X1])
            S.op("dve", lambda e: e.tensor_tensor(out=X2[:], in0=v4(pr_), in1=bc(NEGA(d).unsqueeze(1), [128, 4, 128]), op=ALU.add), [cstf, pr_], [X2])
            S.op("dve", lambda e: e.tensor_tensor(out=X2[:], in0=X2[:], in1=gcb, op=ALU.subtract), [X2, sm], [X2])
            yield
            S.op("act", lambda e: e.activation(out=EG[:], in_=v4(pr_), func=AF.Exp), [pr_], [EG])
            S.op("act", lambda e: e.activation(out=X1[:], in_=X1[:], func=AF.Exp), [X1], [X1])
            S.op("act", lambda e: e.activation(out=X2[:], in_=X2[:], func=AF.Exp), [X2], [X2])
            yield
            S.op("act", lambda e: e.activation(out=sm[:, 2, :], in_=sm[:, 0, :], func=AF.Exp), [sm], [sm])
            S.op("dve", lambda e: e.tensor_tensor(out=sm[:, 3, :], in0=sm[:, 2, :], in1=beta[:], op=ALU.mult), [sm, beta], [sm])
            S.op("dve", lambda e: e.tensor_tensor(out=sm[:, 4, :], in0=sm[:, 1, :], in1=sm[:, 0, :], op=ALU.subtract), [sm], [sm])
            S.op("act", lambda e: e.activation(out=sm[:, 4, :], in_=sm[:, 4, :], func=AF.Exp), [sm], [sm])
            S.op("act", lambda e: e.activation(out=sm[:, 5, :], in_=sm[:, 1, :], func=AF.Exp), [sm], [sm])
            yield
            pkk = S.bank(); pkq = S.bank()
            for h in range(4):
                S.op("pe", lambda e, h=h: e.matmul(v4(pkk)[:, h, :], lhsT=kT(h), rhs=kT(h), start=True, stop=True), [qk], [pkk])
            for h in range(4):
                S.op("pe", lambda e, h=h: e.matmul(v4(pkq)[:, h, :], lhsT=kT(h), rhs=qT(h), start=True, stop=True), [qk], [pkq])
            L0 = m["L"][0]; tf = m["tf"]
            S.op("dve", lambda e: e.tensor_tensor(out=tf[:], in0=v4(pkk), in1=X1[:], op=ALU.mult), [pkk, X1], [tf])
            S.op("dve", lambda e: e.tensor_tensor(out=L0[:], in0=tf[:], in1=bc(beta[:].unsqueeze(2), [128, 4, 128]), op=ALU.mult), [tf, beta], [L0])
            S.op("dve", lambda e: e.tensor_tensor(out=m["aT"][:], in0=v4(pkq), in1=X2[:], op=ALU.mult), [pkq, X2], [m["aT"]])
            yield
            S.op("pool", lambda e: e.tensor_tensor(out=m["qgT"][:], in0=qk[:, 0:4, :], in1=EG[:], op=ALU.mult), [qk, EG], [m["qgT"]])
            S.op("pool", lambda e: e.tensor_tensor(out=m["vb"][:], in0=qk[:, 12:16, :], in1=bc(beta[:].unsqueeze(2), [128, 4, 128]), op=ALU.mult), [qk, beta], [m["vb"]])
            S.op("pool", lambda e: e.tensor_tensor(out=m["kbg"][:], in0=qk[:, 8:12, :], in1=bc(sm[:, 3, :].unsqueeze(2), [128, 4, 128]), op=ALU.mult), [qk, sm], [m["kbg"]])
            S.op("pool", lambda e: e.tensor_tensor(out=m["kd"][:], in0=qk[:, 8:12, :], in1=bc(sm[:, 4, :].unsqueeze(2), [128, 4, 128]), op=ALU.mult), [qk, sm], [m["kd"]])
            yield
            LT0 = m["LT"][0]; TT0 = m["TT"][0]
            pl = S.bank()
            for h in range(4):
                S.op("pe", lambda e, h=h: e.matmul(v4(pl)[:, h, :], lhsT=L0[:, h, :], rhs=IDf(), start=True, stop=True), [L0, cstf], [pl])
            S.op("act", lambda e: e.activation(out=LT0[:], in_=v4(pl), func=AF.Copy), [pl], [LT0])
            S.op("dve", lambda e: e.tensor_tensor(out=TT0[:], in0=bc(IDf().unsqueeze(1), [128, 4, 128]), in1=v4(pl), op=ALU.subtract), [cstf, pl], [TT0])
            yield
            Lc, LTc, TTc = L0, LT0, TT0
            for lev in range(6):
                Ln_ = m["L"][(lev + 1) % 2]; LTn = m["LT"][(lev + 1) % 2]; TTn = m["TT"][(lev + 1) % 2]
                pp = S.bank()
                for h in range(4):
                    S.op("pe", lambda e, h=h, pp=pp, Lc=Lc, LTc=LTc: e.matmul(v4(pp)[:, h, :], lhsT=LTc[:, h, :], rhs=Lc[:, h, :], start=True, stop=True), [Lc, LTc], [pp])
                if lev < 5:
                    pt = S.bank()
                    for h in range(4):
                        S.op("pe", lambda e, h=h, pt=pt, Lc=Lc, LTc=LTc: e.matmul(v4(pt)[:, h, :], lhsT=Lc[:, h, :], rhs=LTc[:, h, :], start=True, stop=True), [Lc, LTc], [pt])
                yield
                S.op("act", lambda e, pp=pp, Ln_=Ln_: e.activation(out=Ln_[:], in_=v4(pp), func=AF.Copy), [pp], [Ln_])
                if lev < 5:
                    S.op("dve", lambda e, pt=pt, LTn=LTn: e.tensor_copy(out=LTn[:], in_=v4(pt)), [pt], [LTn])
                    yield
                pu = S.bank()
                for h in range(4):
                    S.op("pe", lambda e, h=h, pu=pu, Ln_=Ln_, TTc=TTc: e.matmul(v4(pu)[:, h, :], lhsT=Ln_[:, h, :], rhs=TTc[:, h, :], start=True, stop=True), [Ln_, TTc], [pu])
                S.op("dve", lambda e, pu=pu, TTc=TTc, TTn=TTn: e.tensor_tensor(out=TTn[:], in0=v4(pu), in1=TTc[:], op=ALU.add), [pu, TTc], [TTn])
                yield
                Lc, LTc, TTc = Ln_, LTn, TTn
            TTb = m["TTb"]
            S.op("act", lambda e, TTc=TTc: e.activation(out=TTb[:], in_=TTc[:], func=AF.Copy), [TTc], [TTb])
            yield
            pw = S.bank()
            for h in range(4):
                S.op("pe", lambda e, h=h: e.matmul(v4(pw)[:, h, :], lhsT=m["kbg"][:, h, :], rhs=TTb[:, h, :], start=True, stop=True), [m["kbg"], TTb], [pw])
            S.op("act", lambda e: e.activation(out=m["nwT"][:], in_=v4(pw), func=AF.Identity, scale=-1.0), [pw], [m["nwT"]])
            return TTb

        def scan_step(m, TT):
            Sf = m["S"]; Sb_ = m["Sb"]; vnew = m["vnew"]; sm = m["sm"]
            pv = S.bank()
            for h in range(4):
                S.op("pe", lambda e, h=h: e.matmul(v4(pv)[:, h, :], lhsT=TT[:, h, :], rhs=m["vb"][:, h, :], start=True, stop=False), [TT, m["vb"]], [pv])
                S.op("pe", lambda e, h=h: e.matmul(v4(pv)[:, h, :], lhsT=m["nwT"][:, h, :], rhs=Sb_[:, h, :], start=False, stop=True), [m["nwT"], Sb_], [pv])
            S.op("act", lambda e: e.activation(out=vnew[:], in_=v4(pv), func=AF.Copy), [pv], [vnew])
            po = S.bank()
            for h in range(4):
                S.op("pe", lambda e, h=h: e.matmul(v4(po)[:, h, :], lhsT=m["qgT"][:, h, :], rhs=Sb_[:, h, :], start=True, stop=False), [m["qgT"], Sb_], [po])
                S.op("pe", lambda e, h=h: e.matmul(v4(po)[:, h, :], lhsT=m["aT"][:, h, :], rhs=vnew[:, h, :], start=False, stop=True), [m["aT"], vnew], [po])
            pu = S.bank()
            for h in range(4):
                S.op("pe", lambda e, h=h: e.matmul(v4(pu)[:, h, :], lhsT=m["kd"][:, h, :], rhs=vnew[:, h, :], start=True, stop=True), [m["kd"], vnew], [pu])
            S.op("dve", lambda e: e.tensor_tensor(out=Sf[:], in0=Sf[:], in1=bc(sm[:, 5, :].unsqueeze(2), [128, 4, 128]), op=ALU.mult), [Sf, sm], [Sf])
            S.op("dve", lambda e: e.tensor_tensor(out=Sf[:], in0=Sf[:], in1=v4(pu), op=ALU.add), [Sf, pu], [Sf])
            S.op("act", lambda e: e.activation(out=Sb_[:], in_=Sf[:], func=AF.Copy), [Sf], [Sb_])
            return po

        def init_state(m, g, s0ap):
            if g == "s":
                S.dma("sp", m["S"][:], s0ap.rearrange("h k v -> k h v"), writes=[m["S"]])
            else:
                S.op("pool", lambda e: e.memset(m["S"][:], 0.0), [], [m["S"]])
            S.op("act", lambda e: e.activation(out=m["Sb"][:], in_=m["S"][:], func=AF.Copy), [m["S"]], [m["Sb"]])

        def run(gen):
            try:
                while True:
                    next(gen)
            except StopIteration as e_:
                return e_.value

        def interleave(gens):
            vals = [None] * len(gens)
            live = list(range(len(gens)))
            while live:
                for i in list(live):
                    try:
                        next(gens[i])
                    except StopIteration as e_:
                        vals[i] = e_.value
                        live.remove(i)
            return vals

        seqs = DBG_SEQS or ([("p", s) for s in range(4)] + [("s", 0)])

        with ExitStack() as ms:
          if STAGES >= 2:
            mA = mixer_alloc(ms)
            mB = mixer_alloc(ms)
            mC = mixer_alloc(ms)
            mB["S"] = mA["S"]; mB["Sb"] = mA["Sb"]
            mC["S"] = mA["S"]; mC["Sb"] = mA["Sb"]
            NWAY = 3
            ofo = [S.sb([128, 512], F32, "ofo", ms) for _ in range(2)]
            ring_save = S.ring
            S.ring = ring_save + [accN, accD]
            cc = 0

            def prep_both(m_, gd, s, c):
                yield from conv_prep(m_, gd, s, c)
                S.dma("pool", gd["QA"][(s * gd["T"] + c * 128) // 128], m_["qk"][:], reads=[m_["qk"]])
                TT_ = yield from chunk_prep(m_, gd, s, c, 0)
                return TT_

            for (g, s) in seqs:
                gd = G[g]
                init_state(mA, g, sf0)
                nch = min(gd["T"] // 128, DBG_NCH)
                for c0 in range(0, nch, NWAY):
                    ctxs = [(mm, c0 + i_) for i_, mm in enumerate((mA, mB, mC)[:NWAY]) if c0 + i_ < nch]
                    TTs = interleave([prep_both(m_, gd, s, c) for (m_, c) in ctxs])
                    for (m_, c), TT in zip(ctxs, TTs):
                        n0 = s * gd["T"] + c * 128
                        po = scan_step(m_, TT)
                        oo = ofo[cc % 2]; cc += 1
                        S.op("act", lambda e, oo=oo, po=po: e.activation(out=oo[:], in_=po[:], func=AF.Copy), [po], [oo])
                        S.dma("pool", gd["of"][n0:n0 + 128, :], oo[:], reads=[oo])
                if g == "p":
                    finals.append(S.dma("pool", nsf[s].rearrange("h k v -> k h v"), mA["S"][:], reads=[mA["S"]]))
            S.ring = ring_save
            S.barrier()

        with ExitStack() as ms:
          if STAGES >= 3:
            m = mixer_alloc(ms, conv=False)
            m2 = mixer_alloc(ms, conv=False)
            m2["S"] = m["S"]; m2["Sb"] = m["Sb"]
            woa = S.sb([128, 4, D], BF16, "woa", ms)
            wob = S.sb([64, 8, D], BF16, "wob", ms)
            wout = S.sb([128, 8, D], BF16, "wout", ms)
            S.dma("pool", woa[:], w_oa.rearrange("(kc p) d -> p kc d", p=128), writes=[woa])
            S.dma("pool", wob[:], w_ob.rearrange("(h p) d -> p h d", p=64), writes=[wob])
            S.dma("pool", wout[:], w_out.rearrange("(kc p) d -> p kc d", p=128), writes=[wout])
            g2bc = [S.sb([128, D], F32, "g2bc", ms) for _ in range(2)]
            for v in range(2):
                S.dma("sp", g2bc[v][:], modrow[v, 1].partition_broadcast(128), writes=[g2bc[v]])
            ctxKT = S.sb([128, 2, 2, 128], BF16, "ctxKT", ms)
            ctxV = S.sb([128, 2, 128], BF16, "ctxV", ms)
            ckf = S.sb([128, 2, 128], F32, "ckf", ms)
            cvf = S.sb([128, 2, 128], F32, "cvf", ms)
            ckd = S.sb([128, 2, 2, 2, 64], BF16, "ckd", ms)
            S.dma("sp", ckf[:], ck.rearrange("(b p) f -> p b f", p=128), writes=[ckf])
            S.dma("sp", cvf[:], cv.rearrange("(b p) f -> p b f", p=128), writes=[cvf])
            S.op("dve", lambda e: e.tensor_copy(out=ctxV[:], in_=cvf[:]), [cvf], [ctxV])
            for dup in range(2):
                S.op("dve", lambda e, dup=dup: e.tensor_copy(out=ckd[:, :, :, dup, :], in_=ckf[:].rearrange("p b (k d) -> p b k d", d=64)), [ckf], [ckd])
            for b in range(2):
                for kv in range(2):
                    S.op("pe", lambda e, b=b, kv=kv: e.transpose(out=tb[:, b * 2 + kv, :], in_=ckd[:, b, kv].rearrange("p a d -> p (a d)"), identity=identb[:]),
                         [ckd, identb], [tb])
            S.op("act", lambda e: e.activation(out=ctxKT[:].rearrange("p b k n -> p (b k) n"), in_=tb[:, 0:4, :], func=AF.Copy), [tb], [ctxKT])

            ofl = S.sb([128, 4, 128], F32, "ofl", ms)
            zsl = S.sb([128, 4, 128], F32, "zsl", ms)
            ot = S.sb([128, 4, 128], F32, "ot", ms)
            junk = S.sb([128, 128], BF16, "junk", ms)
            ya = S.sb([128, 4, 128], BF16, "ya", ms)
            yaT = S.sb([128, 4, 128], BF16, "yaT", ms)
            QTlo = S.sb([128, 4, 128], BF16, "QTlo", ms)
            QThi = S.sb([128, 4, 128], BF16, "QThi", ms)
            S.op("pool", lambda e: e.memset(QTlo[:], 0.0), [], [QTlo])
            S.op("pool", lambda e: e.memset(QThi[:], 0.0), [], [QThi])
            KTt = [S.sb([128, 2, 128], BF16, "KTt", ms) for _ in range(2)]
            Vtt = [S.sb([128, 128], BF16, "Vtt", ms) for _ in range(2)]
            PTb = [S.sb([128, 4, 128], BF16, "PTb", ms) for _ in range(2)]
            den = S.sb([64, 4, 128], F32, "den", ms)
            ybT = S.sb([64, 8, 128], BF16, "ybT", ms)
            gTl = S.sb([128, 16, 128], BF16, "gTl", ms)
            t1 = S.sb([128, 4, 128], F32, "t1", ms)
            t2 = S.sb([128, 4, 128], F32, "t2", ms)
            mT = S.sb([128, 8, 128], BF16, "mT", ms)
            x1t = S.sb([128, D], F32, "x1t", ms)
            ytmp = S.sb([128, 512], F32, "ytmp", ms)
            kcnt = 0
            for (g, s) in seqs:
                gd = G[g]; v = gd["v"]; T = gd["T"]; nch = T // 128
                init_state(m, g, sb0)
                m_all = {}
                for c in range(nch - 1, -1, -1):
                    n0 = s * T + c * 128
                    if c not in m_all:
                        ctxs = [(m, c)] + ([(m2, c - 1)] if c - 1 >= 0 else [])
                        for (m_, c_) in ctxs:
                            S.dma("sp", m_["qk"][:], gd["QA"][(s * T + c_ * 128) // 128], writes=[m_["qk"]])
                        ring_save = S.ring
                        S.ring = ring_save + [accN, accD]
                        TTs = interleave([chunk_prep(m_, gd, s, c_, 1) for (m_, c_) in ctxs])
                        S.ring = ring_save
                        for (m_, c_), TT_ in zip(ctxs, TTs):
                            m_all[c_] = (m_, TT_)
                    mc, TT = m_all.pop(c)
                    po = scan_step(mc, TT)
                    S.dma("sp", ofl[:], gd["of"][n0:n0 + 128, :].rearrange("p (h d) -> p h d", h=4), writes=[ofl])
                    S.dma("sp", zsl[:], gd["zs"][n0:n0 + 128, :].rearrange("p (h d) -> p h d", h=4), writes=[zsl])
                    S.op("dve", lambda e, po=po: e.tensor_tensor(out=ot[:], in0=v4(po), in1=ofl[:], op=ALU.add), [po, ofl], [ot])
                    S.op("pool", lambda e: e.memset(ss[:, 0:4], 0.0), [], [ss])
                    for h in range(4):
                        S.op("act", lambda e, h=h: e.activation(out=junk[:], in_=ot[:, h, :], func=AF.Square, accum_out=ss[:, h:h + 1]), [ot], [junk, ss])
                    rstd_from_ss(slice(0, 4), 1.0 / 128)
                    S.op("dve", lambda e: e.tensor_tensor(out=ot[:], in0=ot[:], in1=bc(rs[:, 0:4].unsqueeze(2), [128, 4, 128]), op=ALU.mult), [ot, rs], [ot])
                    S.op("pool", lambda e: e.tensor_tensor(out=ot[:], in0=ot[:], in1=bc(onbc[:].unsqueeze(1), [128, 4, 128]), op=ALU.mult), [ot, onbc], [ot])
                    S.op("dve", lambda e: e.tensor_tensor(out=ya[:], in0=ot[:], in1=zsl[:], op=ALU.mult), [ot, zsl], [ya])
                    for h in range(4):
                        S.op("pe", lambda e, h=h: e.transpose(out=tb[:, h, :], in_=ya[:, h, :], identity=identb[:]), [ya, identb], [tb])
                    S.op("act", lambda e: e.activation(out=yaT[:], in_=tb[:, 0:4, :], func=AF.Copy), [tb], [yaT])
                    if CSUB < 2:
                        continue
                    qsrc = gd["QT"].rearrange("c p n -> p c n")
                    S.dma("sp", QTlo[0:64], qsrc[0:64, :, n0:n0 + 128], writes=[QTlo])
                    S.dma("sp", QThi[64:128], qsrc[64:128, :, n0:n0 + 128], writes=[QThi])
                    blocks = []
                    if g == "s":
                        if c > 0:
                            blocks.append(("loc", c - 1, mprev))
                        blocks.append(("loc", c, None))
                        if c < nch - 1:
                            blocks.append(("loc", c + 1, mnext))
                        blocks += [("ctx", 0, None), ("ctx", 1, None)]
                    else:
                        blocks = [("loc", 0, None), ("loc", 1, None)]
                    for gk in range(2):
                        for bi, (kind, idx, mask) in enumerate(blocks):
                            if kind == "loc":
                                kb0 = s * T + idx * 128
                                if gk == 0 or True:
                                    KT_ = KTt[kcnt % 2]; V_ = Vtt[kcnt % 2]; kcnt += 1
                                    S.dma("sp", KT_[:], gd["KT"].rearrange("c p n -> p c n")[:, :, kb0:kb0 + 128], writes=[KT_])
                                    S.dma("sp", V_[:], gd["Vt"][kb0:kb0 + 128, :], writes=[V_])
                                kfull = KT_[:, gk, :]; vv = V_[:, gk * 64:(gk + 1) * 64]
                                kres = [KT_]; vres = [V_]
                            else:
                                kfull = ctxKT[:, idx, gk, :]; vv = ctxV[:, idx, gk * 64:(gk + 1) * 64]
                                kres = [ctxKT]; vres = [ctxV]
                            pst_ = S.bank()
                            stv = pst_[:].rearrange("p (a b i) -> p a b i", a=2, b=2)
                            for a_ in range(2):
                                S.op("pe", lambda e, kfull=kfull, stv=stv, gk=gk, a_=a_: e.matmul(stv[:, a_, 0, :], lhsT=kfull, rhs=QTlo[:, 2 * gk + a_, :], start=True, stop=True),
                                     kres + [QTlo], [pst_])
                                S.op("pe", lambda e, kfull=kfull, stv=stv, gk=gk, a_=a_: e.matmul(stv[:, a_, 1, :], lhsT=kfull, rhs=QThi[:, 2 * gk + a_, :], start=True, stop=True),
                                     kres + [QThi], [pst_])
                            P_ = PTb[(gk * 8 + bi) % 2]
                            S.op("act", lambda e, P_=P_, pst_=pst_: e.activation(out=P_[:], in_=v4(pst_), func=AF.Exp, scale=0.125), [pst_], [P_])
                            if mask is not None:
                                S.op("pool", lambda e, P_=P_, mask=mask: e.tensor_tensor(out=P_[:], in0=P_[:], in1=bc(mask[:].unsqueeze(1), [128, 4, 128]), op=ALU.mult),
                                     [P_, mask], [P_])
                            first = (bi == 0); last = (bi == len(blocks) - 1)
                            S.op("pe", lambda e, P_=P_, vv=vv, first=first, last=last: e.matmul(accN[0:64, :], lhsT=vv, rhs=P_[:].rearrange("p h i -> p (h i)"),
                                                                                              start=first, stop=last), vres + [P_], [accN])
                            S.op("pe", lambda e, P_=P_, first=first, last=last: e.matmul(accD[0:64, :], lhsT=onesb[:, 0:64], rhs=P_[:].rearrange("p h i -> p (h i)"),
                                                                                       start=first, stop=last), [onesb, P_], [accD])
                        S.op("dve", lambda e, gk=gk: e.tensor_tensor(out=den[:], in0=accD[0:64, :].rearrange("p (h i) -> p h i", h=4),
                                                                    in1=bc(sexp[0:64, 4 * gk:4 * gk + 4].unsqueeze(2), [64, 4, 128]), op=ALU.add), [accD, sexp], [den])
                        S.op("dve", lambda e: e.reciprocal(out=den[:], in_=den[:]), [den], [den])
                        S.op("dve", lambda e, gk=gk: e.tensor_tensor(out=ybT[:, 4 * gk:4 * gk + 4, :], in0=accN[0:64, :].rearrange("p (h i) -> p h i", h=4),
                                                                    in1=den[:], op=ALU.mult), [accN, den], [ybT])
                    if CSUB < 3:
                        continue
                    S.dma("sp", gTl[:], gd["gT"].rearrange("c p n -> p c n")[:, :, n0:n0 + 128], writes=[gTl])
                    S.dma("sp", x1t[:], gd["x1"][n0:n0 + 128, :], writes=[x1t])
                    for hf in range(2):
                        pa = S.bank(); pb_ = S.bank()
                        for dj4 in range(4):
                            dj = hf * 4 + dj4
                            for kc in range(4):
                                S.op("pe", lambda e, pa=pa, dj=dj, dj4=dj4, kc=kc: e.matmul(v4(pa)[:, dj4, :], lhsT=woa[:, kc, dj * 128:(dj + 1) * 128], rhs=yaT[:, kc, :],
                                                                                         start=(kc == 0), stop=(kc == 3)), [woa, yaT], [pa])
                            for h in range(8):
                                S.op("pe", lambda e, pb_=pb_, dj=dj, dj4=dj4, h=h: e.matmul(v4(pb_)[:, dj4, :], lhsT=wob[0:64, h, dj * 128:(dj + 1) * 128], rhs=ybT[0:64, h, :],
                                                                                         start=(h == 0), stop=(h == 7)), [wob, ybT], [pb_])
                        S.op("dve", lambda e, pa=pa, hf=hf: e.tensor_tensor(out=t1[:], in0=v4(pa), in1=gTl[:, hf * 4:hf * 4 + 4, :], op=ALU.mult), [pa, gTl], [t1])
                        S.op("dve", lambda e, pb_=pb_, hf=hf: e.tensor_tensor(out=t2[:], in0=v4(pb_), in1=gTl[:, 8 + hf * 4:12 + hf * 4, :], op=ALU.mult), [pb_, gTl], [t2])
                        S.op("pool", lambda e, hf=hf: e.tensor_tensor(out=mT[:, hf * 4:hf * 4 + 4, :], in0=t1[:], in1=t2[:], op=ALU.add), [t1, t2], [mT])
                    for hf in range(2):
                        py = S.bank()
                        for kc in range(8):
                            S.op("pe", lambda e, py=py, kc=kc, hf=hf: e.matmul(py[:], lhsT=mT[:, kc, :], rhs=wout[:, kc, hf * 512:(hf + 1) * 512], start=(kc == 0), stop=(kc == 7)),
                                 [mT, wout], [py])
                        S.op("dve", lambda e, py=py, hf=hf, v=v: e.tensor_tensor(out=ytmp[:], in0=py[:], in1=g2bc[v][:, hf * 512:(hf + 1) * 512], op=ALU.mult), [py, g2bc[v]], [ytmp])
                        S.op("pool", lambda e, hf=hf: e.tensor_tensor(out=x1t[:, hf * 512:(hf + 1) * 512], in0=x1t[:, hf * 512:(hf + 1) * 512], in1=ytmp[:], op=ALU.add), [x1t, ytmp], [x1t])
                    S.dma("pool", gd["x1"][n0:n0 + 128, :], x1t[:], reads=[x1t])
                if g == "p":
                    finals.append(S.dma("pool", nsb[s].rearrange("h k v -> k h v"), m["S"][:], reads=[m["S"]]))
            S.barrier()

        def allocC(fs):
            ex = {}
            ex["nfbc"] = S.sb([128, D], F32, "nfbc", fs)
            S.dma("sp", ex["nfbc"][:], norm_final.partition_broadcast(128), writes=[ex["nfbc"]])
            ex["yo"] = [S.sb([128, D], F32, "yo", fs) for _ in range(2)]
            ex["cnt"] = 0
            return ex

        def tailC(tl, xt, xn, hT, ex):
            for sub in range(4):
                x = xt[sub]
                S.op("pool", lambda e: e.memset(ss[:, 0:1], 0.0), [], [ss])
                S.op("act", lambda e, x=x: e.activation(out=xn[:], in_=x[:], func=AF.Square, accum_out=ss[:, 0:1]), [x], [xn, ss])
                rstd_from_ss(slice(0, 1), 1.0 / D)
                yo = ex["yo"][ex["cnt"] % 2]; ex["cnt"] += 1
                S.op("dve", lambda e, x=x, yo=yo: e.scalar_tensor_tensor(out=yo[:], in0=x[:], scalar=rs[:, 0:1], in1=ex["nfbc"][:], op0=ALU.mult, op1=ALU.mult),
                     [x, rs, ex["nfbc"]], [yo])
                r0 = tl["n0"] + sub * 128
                finals.append(S.dma("pool", yout[tl["g"]][r0:r0 + 128, :], yo[:], reads=[yo]))

        tilesC = []
        for i in range(2):
            tilesC.append(dict(g="p", n0=i * 512, v=0, src=G["p"]["x1"][i * 512:(i + 1) * 512, :]))
        for i in range(8):
            tilesC.append(dict(g="s", n0=i * 512, v=1, src=G["s"]["x1"][i * 512:(i + 1) * 512, :]))
        if STAGES >= 4:
            ffn_stage(1, 2, 2, tilesC, tailC, allocC)

        S.emit(block, final_waits=finals)
    return nc


def _consts():
    i = np.arange(128)
    p = i[:, None]; f = i[None, :]
    c = np.zeros((128, 13, 128), np.float32)
    c[:, 0] = (p == f)
    c[:, 1] = (p <= f)
    c[:, 2] = (p >= f)
    c[:, 3] = np.where(f < p, 0.0, NEG)
    c[:, 4] = np.where(f > p, 0.0, NEG)
    c[:, 5] = np.where(p <= f, 0.0, NEG)
    c[:, 6] = np.where(p >= f, 0.0, NEG)
    c[:, 7] = 1.0
    c[:, 8] = (p >= f)
    c[:, 9] = (p <= f)
    dm = f % 64
    c[:, 10] = np.where((dm < 32) & (p == f + 32), -1.0, 0.0) + np.where((dm >= 32) & (p == f - 32), 1.0, 0.0)
    c[:, 11] = (p == (f % 64))
    c[:, 12] = (p == 64 + (f % 64))
    rows = 4096 // 64
    row = np.repeat(np.arange(rows, dtype=np.float32), 64)
    col = np.tile(np.arange(64, dtype=np.float32), rows)
    inv = np.power(np.float32(10000.0), -np.arange(16, dtype=np.float32) / np.float32(16)).astype(np.float32)
    ang = np.concatenate([row[:, None] * inv, col[:, None] * inv], axis=-1).astype(np.float32)
    cs = np.concatenate([np.cos(ang), np.sin(ang)], axis=-1).astype(np.float32)
    mi = (np.arange(128) % 64) % 32
    csT = np.stack([np.cos(ang)[:, mi].T, np.sin(ang)[:, mi].T], 0).astype(np.float32)
    return c, cs, np.ascontiguousarray(csT)


_NC_CACHE = {}


def kernel(x_prompt, x_sample, state_delta_fwd, state_delta_bwd, cache_k, cache_v, c, c_ctx,
           ada_w, ada_b, norm_ffn1, ffn1_w13, ffn1_w2, norm_mix, w_in, conv_w, a_log, dt_bias,
           onorm_a, w_oa, w_ob, w_out, sink, norm_ffn2, ffn2_w13, ffn2_w2, norm_final):
    A = lambda a: np.ascontiguousarray(np.asarray(a, dtype=np.float32))
    if "nc" not in _NC_CACHE:
        _NC_CACHE["nc"] = build_program()
    nc = _NC_CACHE["nc"]
    cst, cs, csT = _consts()
    shared = {
        "ada_w": A(ada_w)[0], "ada_b": A(ada_b).reshape(1, -1), "norm_ffn1": A(norm_ffn1).reshape(1, -1),
        "ffn1_w13": A(ffn1_w13)[0], "ffn1_w2": A(ffn1_w2)[0], "norm_mix": A(norm_mix).reshape(1, -1),
        "w_in": A(w_in)[0], "conv_w": A(conv_w)[0], "a_log": A(a_log).reshape(1, 8), "dt_bias": A(dt_bias).reshape(1, 8),
        "onorm_a": A(onorm_a).reshape(1, 128), "w_oa": A(w_oa)[0], "w_ob": A(w_ob)[0], "w_out": A(w_out)[0],
        "sink": A(sink).reshape(1, 8), "norm_ffn2": A(norm_ffn2).reshape(1, -1), "ffn2_w13": A(ffn2_w13)[0],
        "ffn2_w2": A(ffn2_w2)[0], "norm_final": A(norm_final).reshape(1, -1), "cst": cst, "cs": cs, "csT": csT,
    }
    xp = A(x_prompt); xs = A(x_sample)
    in_maps = []
    for k in range(8):
        d = dict(shared)
        d["xp"] = xp[4 * k:4 * k + 4].reshape(1024, D)
        d["xs"] = xs[k]
        d["sf0"] = A(state_delta_fwd)[k, 0]
        d["sb0"] = A(state_delta_bwd)[k, 0]
        d["ck"] = A(cache_k)[k, 0].reshape(256, 128)
        d["cv"] = A(cache_v)[k, 0].reshape(256, 128)
        d["cvec"] = np.ascontiguousarray(np.stack([A(c_ctx), A(c)[k]], axis=0))
        in_maps.append(d)
    res = run_bass_kernel_spmd(nc, in_maps, core_ids=list(range(8)))
    R = res.results
    if DEBUG:
        _NC_CACHE["dbg"] = R
    y_prompt = np.concatenate([R[k]["yp"].reshape(4, 256, D) for k in range(8)], axis=0)
    y_sample = np.stack([R[k]["ys"] for k in range(8)], axis=0)
    nsf = np.concatenate([R[k]["nsf"] for k in range(8)], axis=0)[:, None]
    nsb = np.concatenate([R[k]["nsb"] for k in range(8)], axis=0)[:, None]
    nck = np.concatenate([R[k]["nck"].reshape(4, 256, 2, 64) for k in range(8)], axis=0)[:, None]
    ncv = np.concatenate([R[k]["ncv"].reshape(4, 256, 2, 64) for k in range(8)], axis=0)[:, None]
    return (y_prompt.astype(np.float32), y_sample.astype(np.float32), nsf.astype(np.float32), nsb.astype(np.float32),
            nck.astype(np.float32), ncv.astype(np.float32))
```
